# Optimizing a Trainium2 kernel written in Bass

```python
import math
import jax, jax.numpy as jnp
from jax import lax
import numpy as np

D_MODEL = 1024
BATCH = 8
SEQ = 2048
DEPTH = 2

GRID_W = 64
HEAD_DIM = 64
NA_HEADS = D_MODEL // 128
NA_WIDTH = NA_HEADS * HEAD_DIM
NA_KH_MAX = 8
NA_KW = 16
GQA_HEADS = D_MODEL // 128
GQA_KV_HEADS = max(1, GQA_HEADS // 4)
GQA_WIDTH = GQA_HEADS * HEAD_DIM
GQA_KV_WIDTH = GQA_KV_HEADS * HEAD_DIM
Q_BLOCK = 128
ROPE_THETA = 10000.0
ROPE_AXIS_DIM = HEAD_DIM // 2
SG_WIDTH = D_MODEL // 2
SG_GROUPS = SG_WIDTH // HEAD_DIM
SG_CHUNK = 128
N_BRANCH = 3
LN_EPS = 1e-5
RMS_EPS = 1e-6
DEEPNORM_ALPHA = (2.0 * DEPTH) ** 0.25
DEEPNORM_BETA = (8.0 * DEPTH) ** -0.25
IN_SPLITS = (NA_WIDTH, NA_WIDTH, NA_WIDTH, NA_WIDTH,
             GQA_WIDTH, GQA_KV_WIDTH, GQA_KV_WIDTH, GQA_WIDTH,
             SG_WIDTH, SG_WIDTH, SG_WIDTH,
             N_BRANCH * D_MODEL)
IN_WIDTH = 4 * NA_WIDTH + 2 * GQA_WIDTH + 2 * GQA_KV_WIDTH + 3 * SG_WIDTH + N_BRANCH * D_MODEL

kernel_name = "hybrid_na_gqa_sgmlp_deepnorm_encoder"


def _split_points():
    pts, acc = [], 0
    for w in IN_SPLITS[:-1]:
        acc += w
        pts.append(acc)
    return pts


def layer_norm(x, g, b):
    xf = x.astype(jnp.float32)
    mu = jnp.mean(xf, axis=-1, keepdims=True)
    var = jnp.mean(jnp.square(xf - mu), axis=-1, keepdims=True)
    return ((xf - mu) * lax.rsqrt(var + LN_EPS)).astype(x.dtype) * g + b


def rms_norm(x, g):
    xf = x.astype(jnp.float32)
    ms = jnp.mean(jnp.square(xf), axis=-1, keepdims=True)
    return (xf * lax.rsqrt(ms + RMS_EPS)).astype(x.dtype) * g


def axial_rope_tables(s):
    t = jnp.arange(s)
    row = (t // GRID_W).astype(jnp.float32)
    col = (t % GRID_W).astype(jnp.float32)
    freqs = ROPE_THETA ** (-jnp.arange(0, ROPE_AXIS_DIM, 2, dtype=jnp.float32) / ROPE_AXIS_DIM)
    ang = jnp.concatenate([row[:, None] * freqs, col[:, None] * freqs], axis=-1)
    return jnp.cos(ang), jnp.sin(ang)


def apply_rope(x, cos, sin):
    xf = x.astype(jnp.float32).reshape(*x.shape[:-1], HEAD_DIM // 2, 2)
    x0, x1 = xf[..., 0], xf[..., 1]
    c = cos[None, :, None, :]
    s = sin[None, :, None, :]
    out = jnp.stack([x0 * c - x1 * s, x0 * s + x1 * c], axis=-1)
    return out.reshape(x.shape).astype(x.dtype)


def neighbourhood_attention(q, k, v, rpb):
    b, s, h, dh = q.shape
    rows = s // GRID_W
    kh = min(NA_KH_MAX, rows)
    r = jnp.arange(rows)
    row_start = jnp.clip(r - kh // 2, 0, rows - kh)
    key_rows = row_start[:, None] + jnp.arange(kh)[None, :]
    c = jnp.arange(GRID_W)
    col_start = jnp.clip(c - NA_KW // 2, 0, GRID_W - NA_KW)
    col_valid = (c[None, :] >= col_start[:, None]) & (c[None, :] < col_start[:, None] + NA_KW)
    qg = q.reshape(b, rows, GRID_W, h, dh)
    kg = k.reshape(b, rows, GRID_W, h, dh)[:, key_rows]
    vg = v.reshape(b, rows, GRID_W, h, dh)[:, key_rows]
    scores = jnp.einsum('brqhd,brjkhd->bhrqjk', qg, kg).astype(jnp.float32) * (dh ** -0.5)
    row_off = key_rows - r[:, None] + (NA_KH_MAX - 1)
    col_off = jnp.clip(c[None, :] - c[:, None] + (NA_KW - 1), 0, 2 * NA_KW - 2)
    bias = rpb[:, row_off[:, None, :, None], col_off[None, :, None, :]]
    scores = scores + bias.astype(jnp.float32)
    scores = jnp.where(col_valid[:, None, :], scores, jnp.float32(-1e30))
    shp = scores.shape
    p = jax.nn.softmax(scores.reshape(b, h, rows, GRID_W, kh * GRID_W), axis=-1)
    p = p.reshape(shp).astype(v.dtype)
    out = jnp.einsum('bhrqjk,brjkhd->brqhd', p, vg)
    return out.reshape(b, s, h * dh)


def gqa_attention(q, k, v):
    b, s, hq, dh = q.shape
    hkv = k.shape[2]
    grp = hq // hkv
    nblk = s // Q_BLOCK
    qb = q.reshape(b, nblk, Q_BLOCK, hkv, grp, dh).transpose(1, 0, 2, 3, 4, 5)
    scale = dh ** -0.5

    def block(qi):
        sc = jnp.einsum('bqkgd,bskd->bkgqs', qi, k).astype(jnp.float32) * scale
        p = jax.nn.softmax(sc, axis=-1).astype(v.dtype)
        return jnp.einsum('bkgqs,bskd->bqkgd', p, v)

    out = lax.map(block, qb)
    return out.transpose(1, 0, 2, 3, 4, 5).reshape(b, s, hq * dh)


def spatial_gating(u, v, ln_g, ln_b, w_s, b_s):
    b, s, wc = u.shape
    vn = layer_norm(v, ln_g, ln_b)
    vc = vn.reshape(b, s // SG_CHUNK, SG_CHUNK, SG_GROUPS, wc // SG_GROUPS)
    mixed = jnp.einsum('gmn,bcngd->bcmgd', w_s, vc) + b_s.T[None, None, :, :, None]
    return u * mixed.reshape(b, s, wc)


def hybrid_layer(x, w_in, b_in, na_rpb, q_norm_g, k_norm_g, sg_ln_g, sg_ln_b, sg_w, sg_b,
                 w_br_a, w_br_b, w_br_c, w_out, b_out, ln_g, ln_b, rope_cos, rope_sin):
    b, s, d = x.shape
    hcat = x @ w_in + b_in
    (na_q, na_k, na_v, na_z, gq_q, gq_k, gq_v, gq_z,
     sg_u, sg_v, sg_z, gates) = jnp.split(hcat, _split_points(), axis=-1)

    y_a = neighbourhood_attention(na_q.reshape(b, s, NA_HEADS, HEAD_DIM),
                                  na_k.reshape(b, s, NA_HEADS, HEAD_DIM),
                                  na_v.reshape(b, s, NA_HEADS, HEAD_DIM), na_rpb)
    p_a = (y_a * jax.nn.silu(na_z)) @ w_br_a

    q = apply_rope(rms_norm(gq_q.reshape(b, s, GQA_HEADS, HEAD_DIM), q_norm_g), rope_cos, rope_sin)
    k = apply_rope(rms_norm(gq_k.reshape(b, s, GQA_KV_HEADS, HEAD_DIM), k_norm_g), rope_cos, rope_sin)
    y_b = gqa_attention(q, k, gq_v.reshape(b, s, GQA_KV_HEADS, HEAD_DIM))
    p_b = (y_b * jax.nn.silu(gq_z)) @ w_br_b

    y_c = spatial_gating(sg_u, sg_v, sg_ln_g, sg_ln_b, sg_w, sg_b)
    p_c = (y_c * jax.nn.silu(sg_z)) @ w_br_c

    g = jax.nn.sigmoid(gates.reshape(b, s, N_BRANCH, d))
    merged = g[:, :, 0] * p_a + g[:, :, 1] * p_b + g[:, :, 2] * p_c
    sub = merged @ w_out + b_out
    return layer_norm(DEEPNORM_ALPHA * x + sub, ln_g, ln_b)


def setup_inputs(seed: int = 0) -> dict:
    key = jax.random.key(seed)
    ks = jax.random.split(key, 20)
    L, D = DEPTH, D_MODEL

    def nrm(k, shape, scale):
        return jax.random.normal(k, shape, jnp.float32) * scale

    return {
        "x": nrm(ks[0], (BATCH, SEQ, D), 1.0),
        "ln_in_g": 1.0 + nrm(ks[1], (D,), 0.02),
        "ln_in_b": nrm(ks[2], (D,), 0.02),
        "w_in": nrm(ks[3], (L, D, IN_WIDTH), D ** -0.5),
        "b_in": nrm(ks[4], (L, IN_WIDTH), 0.02),
        "na_rpb": nrm(ks[5], (L, NA_HEADS, 2 * NA_KH_MAX - 1, 2 * NA_KW - 1), 0.1),
        "q_norm_g": 1.0 + nrm(ks[6], (L, HEAD_DIM), 0.02),
        "k_norm_g": 1.0 + nrm(ks[7], (L, HEAD_DIM), 0.02),
        "sg_ln_g": 1.0 + nrm(ks[8], (L, SG_WIDTH), 0.02),
        "sg_ln_b": nrm(ks[9], (L, SG_WIDTH), 0.02),
        "sg_w": nrm(ks[10], (L, SG_GROUPS, SG_CHUNK, SG_CHUNK), SG_CHUNK ** -0.5),
        "sg_b": 1.0 + nrm(ks[11], (L, SG_GROUPS, SG_CHUNK), 0.02),
        "w_br_a": nrm(ks[12], (L, NA_WIDTH, D), NA_WIDTH ** -0.5),
        "w_br_b": nrm(ks[13], (L, GQA_WIDTH, D), GQA_WIDTH ** -0.5),
        "w_br_c": nrm(ks[14], (L, SG_WIDTH, D), SG_WIDTH ** -0.5),
        "w_out": nrm(ks[15], (L, D, D), (D ** -0.5) * DEEPNORM_BETA),
        "b_out": nrm(ks[16], (L, D), 0.02),
        "ln_post_g": 1.0 + nrm(ks[17], (L, D), 0.02),
        "ln_post_b": nrm(ks[18], (L, D), 0.02),
    }


def reference(x, ln_in_g, ln_in_b, w_in, b_in, na_rpb, q_norm_g, k_norm_g, sg_ln_g, sg_ln_b,
              sg_w, sg_b, w_br_a, w_br_b, w_br_c, w_out, b_out, ln_post_g, ln_post_b):
    s = x.shape[1]
    rope_cos, rope_sin = axial_rope_tables(s)
    h = layer_norm(x, ln_in_g, ln_in_b)
    for l in range(DEPTH):
        h = hybrid_layer(h, w_in[l], b_in[l], na_rpb[l], q_norm_g[l], k_norm_g[l],
                         sg_ln_g[l], sg_ln_b[l], sg_w[l], sg_b[l],
                         w_br_a[l], w_br_b[l], w_br_c[l], w_out[l], b_out[l],
                         ln_post_g[l], ln_post_b[l], rope_cos, rope_sin)
    return h
```

```python
import numpy as np
import concourse.bass as bass
import concourse.mybir as mybir
from concourse.bass_utils import run_bass_kernel_spmd

F32, BF16 = mybir.dt.float32, mybir.dt.bfloat16
AF = mybir.ActivationFunctionType
ALU = mybir.AluOpType

S = 2048
D = 1024
L = 2
NCH = 63
WIN = NCH * 128
ALPHA = (2.0 * L) ** 0.25
SB_BASE = 16512
SB_END = 229376
EMBED_WAIT = ("act",)


class Sched:
    ENGS = ("pe", "act", "dve", "pool", "sp")

    def __init__(self, n_dma_sems=24):
        self.prog = {e: [] for e in self.ENGS}
        self.serial = {e: 0 for e in self.ENGS}
        self.seen = {e: {} for e in self.ENGS}
        self.lastw = {}
        self.readers = {}
        self.lastx = {}
        self.waited = {e: set() for e in self.ENGS}
        self.n_dma = n_dma_sems
        self.dma_i = 0
        self.dma_ip = 0
        self.dma_val = [0] * n_dma_sems

    def _deps(self, eng, reads, writes, excl):
        need = {}

        def add(tok, raw):
            key, val, teng = tok
            if teng == eng and not raw:
                return
            if self.seen[eng].get(key, 0) >= val:
                return
            if need.get(key, 0) < val:
                need[key] = val

        for r in reads:
            t = self.lastw.get(r)
            if t:
                add(t, True)
        for w in writes:
            t = self.lastw.get(w)
            if t:
                add(t, False)
            for t in self.readers.get(w, {}).values():
                add(t, False)
        for x in excl:
            t = self.lastx.get(x)
            if t:
                add(t, False)
        for key, val in need.items():
            self.seen[eng][key] = val
            self.prog[eng].append(("wait", key, val))
            if key[0] != "q":
                self.waited[key].add(val)

    def _commit(self, tok, reads, writes, excl):
        for r in reads:
            self.readers.setdefault(r, {})[tok[0]] = tok
        for w in writes:
            self.lastw[w] = tok
            self.readers[w] = {}
        for x in excl:
            self.lastx[x] = tok

    def op(self, eng, fn, reads=(), writes=(), excl=()):
        self._deps(eng, reads, writes, excl)
        self.serial[eng] += 1
        tok = (eng, self.serial[eng], eng)
        self.prog[eng].append(("op", fn, self.serial[eng]))
        self._commit(tok, reads, writes, excl)
        return tok

    def dma(self, eng, fn, reads=(), writes=()):
        self._deps(eng, reads, writes, ())
        if eng == "pool":
            k = 16 + self.dma_ip % (self.n_dma - 16)
            self.dma_ip += 1
        else:
            k = self.dma_i % 16
            self.dma_i += 1
        key = "q%d" % k
        prev = self.dma_val[k]
        if prev > 0 and self.seen[eng].get(key, 0) < prev:
            self.seen[eng][key] = prev
            self.prog[eng].append(("wait", key, prev))
        self.dma_val[k] += 16
        tok = (key, self.dma_val[k], None)
        self.prog[eng].append(("dma", fn, k))
        self._commit(tok, reads, writes, ())
        return tok

    def wait_all(self, eng, dmas=True):
        for e in self.ENGS:
            if e != eng and self.serial[e] > 0 and self.seen[eng].get(e, 0) < self.serial[e]:
                self.seen[eng][e] = self.serial[e]
                self.prog[eng].append(("wait", e, self.serial[e]))
                self.waited[e].add(self.serial[e])
        for k in range(self.n_dma if dmas else 0):
            key = "q%d" % k
            if self.dma_val[k] > 0 and self.seen[eng].get(key, 0) < self.dma_val[k]:
                self.seen[eng][key] = self.dma_val[k]
                self.prog[eng].append(("wait", key, self.dma_val[k]))

    def emit(self, nc, block, sems, dsems):
        rank = {}
        for e in self.ENGS:
            ws = sorted(self.waited[e])
            rank[e] = {s: i + 1 for i, s in enumerate(ws)}
        handles = {"pe": "tensor", "act": "scalar", "dve": "vector", "pool": "gpsimd", "sp": "sync"}

        def run(ename):
            def body(eng):
                pend = []

                def semval(key, val):
                    if key[0] == "q":
                        return dsems[int(key[1:])], val
                    return sems[key], rank[key][val]

                for item in self.prog[ename]:
                    if item[0] == "wait":
                        pend.append(semval(item[1], item[2]))
                        continue
                    embed = None
                    if pend and item[0] == "op" and ename in EMBED_WAIT:
                        embed = pend.pop()
                    for (sm, v) in pend:
                        eng.wait_ge(sm, v)
                    pend = []
                    ins = item[1](eng)
                    if embed is not None:
                        ins._wait_ge(embed[0], embed[1])
                    if item[0] == "op":
                        if item[2] in rank[ename]:
                            ins.then_inc(sems[ename], 1)
                    else:
                        ins.then_inc(dsems[item[2]], 16)
                for (sm, v) in pend:
                    eng.wait_ge(sm, v)
            return body

        for ename in self.ENGS:
            getattr(block, handles[ename])(run(ename))


def _win_cols():
    naq, nak, nav, naz, gqq, gqk, gqv, gqz, sgu, sgv, sgz, gat = (0, 512, 1024, 1536, 2048, 2560, 2688, 2816,
                                                                 3328, 3840, 4352, 4864)
    r = lambda a, n: list(range(a, a + n))
    cols = []
    for c in range(4):
        cols += r(naq + c * 128, 128) + r(nak + c * 128, 128) + r(nav + c * 128, 128) + r(naz + c * 128, 128)
    cols += r(gqk, 64) + r(gqk, 64) + r(gqk + 64, 64) + r(gqk + 64, 64) + r(gqv, 128)
    for c in range(4):
        cols += r(gqq + c * 128, 128) + r(gqz + c * 128, 128)
    cols += r(sgv, 512)
    for c in range(4):
        cols += r(sgu + c * 128, 128) + r(sgz + c * 128, 128)
    for dc in range(8):
        for b in range(3):
            cols += r(gat + b * 1024 + dc * 128, 128)
    assert len(cols) == WIN
    return np.asarray(cols)


def CH_NA(c, which):
    return 4 * c + which
CH_GK = (16, 17)
CH_GV = 18
def CH_GQ(c, which):
    return 19 + 2 * c + which
CH_SV = 27
def CH_SG(c, which):
    return 31 + 2 * c + which
def CH_GATE(dc, b):
    return 39 + 3 * dc + b


def _na_index_tables():
    p = np.arange(128)
    ck = p % 64
    half = p // 64
    s = np.arange(2, 16)
    cq = np.arange(64)
    dr = (8 - s)[None, :, None] + half[:, None, None] + 0 * cq[None, None, :]
    dcol = np.clip(ck[:, None, None] - cq[None, None, :] + 15, 0, 30) + 0 * s[None, :, None]
    cs = np.clip(cq - 8, 0, 48)
    colv = (ck[:, None, None] >= cs[None, None, :]) & (ck[:, None, None] < cs[None, None, :] + 16)
    colv = colv & (s[None, :, None] > -100)
    full_ok = colv & (np.abs(dr) <= 7)
    int_ok = colv & (dr >= -4) & (dr <= 3)
    ridx = np.clip(dr + 7, 0, 14)
    return ridx, dcol, full_ok, int_ok


def _rope_tables():
    t = np.arange(S)
    row = (t // 64).astype(np.float64)
    col = (t % 64).astype(np.float64)
    freqs = 10000.0 ** (-np.arange(0, 32, 2, dtype=np.float64) / 32.0)
    ang = np.concatenate([row[:, None] * freqs, col[:, None] * freqs], axis=-1)
    cos = np.cos(ang)
    sin = np.sin(ang)
    p = np.arange(128)
    d = p % 64
    C = cos[:, d // 2].T
    Sg = (sin[:, d // 2] * np.where(d % 2 == 0, -1.0, 1.0)[None, :]).T
    return np.ascontiguousarray(np.stack([C, Sg], axis=1)).astype(np.float32)


def _consts32():
    perm = np.zeros((128, 128), np.float32)
    for i in range(64):
        perm[2 * i + 1, 2 * i] = 1.0
        perm[2 * i, 2 * i + 1] = 1.0
    k = np.arange(128)
    bones = (k[:, None] // 64 == k[None, :] // 64).astype(np.float32)
    return np.concatenate([perm, bones], axis=1)


def prep_inputs(x, ln_in_g, ln_in_b, w_in, b_in, na_rpb, q_norm_g, k_norm_g, sg_ln_g, sg_ln_b, sg_w, sg_b,
                w_br_a, w_br_b, w_br_c, w_out, b_out, ln_post_g, ln_post_b):
    f = lambda a: np.ascontiguousarray(np.asarray(a, dtype=np.float32))
    cols = _win_cols()
    shared = {}
    shared["w_in_p"] = f(np.take(w_in, cols, axis=2))
    bp = np.take(b_in, cols, axis=1)
    vecs = np.zeros((L, 128, 80), np.float32)
    vecs[:, :, 0:NCH] = bp.reshape(L, NCH, 128).transpose(0, 2, 1)
    vecs[:, :, 64] = np.tile(q_norm_g, (1, 2))
    vecs[:, :, 65] = np.tile(k_norm_g, (1, 2))
    shared["vecs"] = vecs
    bv = np.concatenate([b_in[:, 1024:1536], b_in[:, 2688:2816], b_in[:, 3840:4352]], axis=1)
    shared["bvb"] = f(np.broadcast_to(bv[:, None, :], (L, 128, 1152)))
    shared["lnin"] = f(np.broadcast_to(np.stack([ln_in_g, ln_in_b])[None], (128, 2, D)))
    ridx, dcol, full_ok, int_ok = _na_index_tables()
    rp = np.asarray(na_rpb, np.float32)
    g = rp[:, :, ridx, dcol]
    neg = np.float32(-1e30)
    tfull = np.where(full_ok[None, None], g, neg)
    tint = np.where(int_ok[None, None], g, neg)
    shared["natab"] = f(np.stack([tfull, tint], axis=3).reshape(L, 8, 128, 2 * 896))
    shared["rope"] = _rope_tables()
    shared["c32"] = _consts32()
    shared["ident"] = np.eye(128, dtype=np.float32)
    shared["sgln"] = f(np.broadcast_to(np.stack([sg_ln_g, sg_ln_b], axis=1)[:, None], (L, 128, 2, 512)))
    shared["sgwT"] = f(np.transpose(sg_w, (0, 3, 1, 2)))
    sb = np.asarray(sg_b, np.float32)
    sbb = sb.reshape(L, 4, 2, 1, 128)
    sbb = np.broadcast_to(sbb, (L, 4, 2, 64, 128)).reshape(L, 4, 128, 128)
    sbb = np.broadcast_to(sbb[:, :, :, None, :], (L, 4, 128, 4, 128)).reshape(L, 4, 128, 512)
    shared["sgb"] = f(sbb.transpose(0, 2, 1, 3))
    wf = np.stack([w_br_a, w_br_b, w_br_c], axis=2)
    wf = wf.reshape(L, 512, 3, 8, 128).transpose(0, 1, 3, 2, 4).reshape(L, 512, 3072)
    shared["wfin"] = f(wf)
    shared["wout"] = f(w_out)
    shared["post"] = f(np.broadcast_to(np.stack([b_out, ln_post_g, ln_post_b], axis=1)[:, None], (L, 128, 3, D)))
    return shared


DUMMY_MM = 0
POOL_EVERY = 0


class KB:
    def __init__(self, layers, do_ln_in, dbg=False):
        self.layers = list(layers)
        self.do_ln_in = do_ln_in
        self.dbg = dbg
        self.s = Sched()
        nc = self.nc = bass.Bass("TRN2", target_bir_lowering=False)
        di = lambda n, shp: nc.dram_tensor(n, shp, F32, kind="ExternalInput").ap()
        self.d = dict(
            x=di("x", [S, D]), lnin=di("lnin", [128, 2, D]), w_in_p=di("w_in_p", [L, D, WIN]),
            vecs=di("vecs", [L, 128, 80]), bvb=di("bvb", [L, 128, 1152]), natab=di("natab", [L, 8, 128, 1792]),
            rope=di("rope", [128, 2, S]), c32=di("c32", [128, 256]), ident=di("ident", [128, 128]),
            sgln=di("sgln", [L, 128, 2, 512]), sgwT=di("sgwT", [L, 128, 8, 128]), sgb=di("sgb", [L, 128, 4, 512]),
            wfin=di("wfin", [L, 512, 3072]), wout=di("wout", [L, D, D]), post=di("post", [L, 128, 3, D]),
        )
        self.y_d = nc.dram_tensor("y", [S, D], F32, kind="ExternalOutput").ap()
        self.dbg_d = {}
        self._nm = 0
        o = SB_BASE
        self.xtok = self.A("xtok", [128, 16, D], F32, o); o += 65536
        self.xT = self.A("xT", [128, 8, S], BF16, o); o += 32768
        self.R_y = o
        self.yA = self.A("yA", [128, 4, S], BF16, o); o += 16384
        self.yB = self.A("yB", [128, 4, S], BF16, o); o += 16384
        self.yC = self.A("yC", [128, 4, S], BF16, o); o += 16384
        self.ybr = [self.yA, self.yB, self.yC]
        self.ident = self.A("ident", [128, 128], BF16, o); o += 256
        self.c32 = self.A("c32", [128, 256], F32, o); o += 1024
        self.vecs = self.A("vecs", [128, 80], F32, o); o += 320
        self.stat = self.A("stat", [128, 64], F32, o); o += 256
        self.R_loc = o
        self.wslot = [self.A("wslot0", [128, 4608], BF16, o), self.A("wslot1", [128, 4608], BF16, o + 9216)]
        o += 18432
        self.R_ph = o
        self.psall = nc.alloc_psum_tensor("psall", [128, 4096], F32)
        self.ps = [self.psall[:, b * 512:(b + 1) * 512] for b in range(8)]
        self.pst = self.psall[:, 3584:4096].bitcast(BF16)
        self.pbanks = [0, 1, 2, 3, 4, 5, 6]
        self.pb_i = 0
        self.wjobs = []
        for l in self.layers:
            self.wjobs += self.layer_wjobs(l)
        self.w_issued = 0
        self.w_used = 0
        self._xT_pending = None

    def A(self, name, shape, dtype, off):
        nbytes = int(np.prod(shape[1:])) * (4 if dtype == F32 else 2)
        assert off + nbytes <= SB_END, (name, off, nbytes)
        self._nm += 1
        return self.nc.alloc_sbuf_tensor_at("%s_%d" % (name, self._nm), list(shape), dtype, offset=off)

    def bank(self):
        b = self.pbanks[self.pb_i % len(self.pbanks)]
        self.pb_i += 1
        return b

    def mm(self, banks, out, lhsT, rhs, start, stop, reads):
        if isinstance(banks, int):
            banks = [banks]
        self.s.op("pe", lambda e: e.matmul(out, lhsT, rhs, start=start, stop=stop), reads=reads,
                  excl=[("ps", b) for b in banks])

    def act(self, out, in_, func, reads=(), writes=(), excl=(), **kw):
        self.s.op("act", lambda e: e.activation(out, in_, func, **kw), reads=reads, writes=writes, excl=excl)

    def tt(self, eng, out, in0, in1, op, reads=(), writes=(), excl=()):
        self.s.op(eng, lambda e: e.tensor_tensor(out, in0, in1, op), reads=reads, writes=writes, excl=excl)

    def ts(self, eng, out, in0, s1, s2, op0, op1=None, reads=(), writes=(), excl=()):
        if op1 is None:
            self.s.op(eng, lambda e: e.tensor_scalar(out, in0, s1, None, op0), reads=reads, writes=writes, excl=excl)
        else:
            self.s.op(eng, lambda e: e.tensor_scalar(out, in0, s1, s2, op0, op1), reads=reads, writes=writes, excl=excl)

    def stt(self, eng, out, in0, scalar, in1, op0, op1, reads=(), writes=(), excl=()):
        self.s.op(eng, lambda e: e.scalar_tensor_tensor(out, in0, scalar, in1, op0, op1), reads=reads, writes=writes,
                  excl=excl)

    def cp(self, eng, out, in_, reads=(), writes=(), excl=()):
        self.s.op(eng, lambda e: e.tensor_copy(out, in_), reads=reads, writes=writes, excl=excl)

    def dma(self, eng, out, in_, reads=(), writes=()):
        self.s.dma(eng, lambda e: e.dma_start(out=out, in_=in_), reads=reads, writes=writes)

    def layer_wjobs(self, l):
        w = self.d["w_in_p"]
        jobs = []
        for c in range(4):
            jobs.append([(0, 8, 512, w[l, :, c * 512:(c + 1) * 512])])
        g0 = CH_GK[0] * 128
        jobs.append([(0, 8, 384, w[l, :, g0:g0 + 384])])
        for c in range(4):
            c0 = CH_GQ(c, 0) * 128
            jobs.append([(0, 8, 256, w[l, :, c0:c0 + 256])])
        c0 = CH_SV * 128
        jobs.append([(0, 8, 512, w[l, :, c0:c0 + 512])])
        for c in range(4):
            c0 = CH_SG(c, 0) * 128
            jobs.append([(0, 8, 256, w[l, :, c0:c0 + 256])])
        for dc in range(8):
            g0 = CH_GATE(dc, 0) * 128
            jobs.append([(0, 8, 384, w[l, :, g0:g0 + 384]),
                         (3072, 4, 384, self.d["wfin"][l, :, dc * 384:(dc + 1) * 384])])
        for hf in range(2):
            jobs.append([(0, 8, 512, self.d["wout"][l, :, hf * 512:(hf + 1) * 512])])
        return jobs

    def _issue_w(self):
        k = self.w_issued
        if k >= len(self.wjobs):
            return
        self.w_issued += 1
        i = k % 2
        for (off, kch, ncols, src) in self.wjobs[k]:
            dst = self.wslot[i][:, off:off + kch * ncols].rearrange("p (k n) -> p k n", n=ncols)
            self.dma("pool", dst, src.rearrange("(k p) n -> p k n", p=128), writes=[("w", i)])

    def take_w(self):
        k = self.w_used
        self.w_used += 1
        while self.w_issued <= min(k + 1, len(self.wjobs) - 1):
            self._issue_w()
        i = k % 2
        views = []
        for (off, kch, ncols, src) in self.wjobs[k]:
            views.append(self.wslot[i][:, off:off + kch * ncols].rearrange("p (k n) -> p k n", n=ncols))
        return views, ("w", i)

    def xT_res(self, tc):
        return [("xT", 4 * tc + i) for i in range(4)]

    def proj_fm(self, wt, wres, col0, evac):
        for tc in range(4):
            b = self.bank()
            for kc in range(8):
                self.mm(b, self.ps[b][:, :], wt[:, kc, col0:col0 + 128], self.xT[:, kc, tc * 512:(tc + 1) * 512],
                        kc == 0, kc == 7, reads=[wres] + self.xT_res(tc))
            evac(tc, b)

    def bias(self, ch):
        return self.vecs[:, ch:ch + 1]

    def ln_stats(self, src, src_res, tt_i):
        st = self.stat
        k = (tt_i % 4) * 16
        for h in range(2):
            self.s.op("dve", (lambda h: lambda e: e.bn_stats(st[:, k + h * 6:k + h * 6 + 6],
                                                               src[:, h * 512:(h + 1) * 512]))(h),
                      reads=[src_res], writes=[("stat", tt_i % 4)])
        self.s.op("dve", lambda e: e.bn_aggr(st[:, k + 12:k + 14], st[:, k:k + 12]),
                  reads=[("stat", tt_i % 4)], writes=[("stat", tt_i % 4)])
        self.act(st[:, k + 14:k + 15], st[:, k + 13:k + 14], AF.Sqrt, bias=1e-5,
                 reads=[("stat", tt_i % 4)], writes=[("stat", tt_i % 4)])

    def ln_apply(self, src, src_res, tt_i, g_ap, b_ap, gb_res):
        st = self.stat
        k = (tt_i % 4) * 16
        self.s.op("dve", lambda e: e.reciprocal(st[:, k + 14:k + 15], st[:, k + 14:k + 15]),
                  reads=[("stat", tt_i % 4)], writes=[("stat", tt_i % 4)])
        xt = self.xtok[:, tt_i, :]
        self.stt("dve", xt, src, st[:, k + 12:k + 13], g_ap, ALU.subtract, ALU.mult,
                 reads=[src_res, ("stat", tt_i % 4), gb_res], writes=[("xtok", tt_i)])
        self.stt("dve", xt, xt, st[:, k + 14:k + 15], b_ap, ALU.mult, ALU.add,
                 reads=[("xtok", tt_i), ("stat", tt_i % 4), gb_res], writes=[("xtok", tt_i)])

    def to_xT(self, tt_i, tmp_h16):
        if self._xT_pending is not None:
            self.to_xT_finish()
        k = tt_i % 2
        h16 = tmp_h16[k]
        self.act(h16[:, :], self.xtok[:, tt_i, :], AF.Copy, reads=[("xtok", tt_i)], writes=[("h16", k)])
        for kc in range(8):
            self.s.op("pe", (lambda kc: lambda e: e.transpose(self.pst[:, kc * 128:(kc + 1) * 128],
                                                              h16[:, kc * 128:(kc + 1) * 128], self.ident[:, :]))(kc),
                      reads=[("h16", k), ("ident",)], excl=[("ps", 7)])
        self._xT_pending = tt_i

    def to_xT_finish(self):
        tt_i = self._xT_pending
        if tt_i is None:
            return
        self._xT_pending = None
        self.cp("dve", self.xT[:, :, tt_i * 128:(tt_i + 1) * 128], self.pst[:, :].rearrange("p (k n) -> p k n", n=128),
                writes=[("xT", tt_i)], excl=[("ps", 7)])

    def phase0(self):
        o = self.R_ph
        gb = self.A("lnin", [128, 2, D], F32, o); o += 8192
        xin = [self.A("xin%d" % i, [128, D], F32, o + 4096 * i) for i in range(3)]; o += 12288
        h16 = [self.A("h16a", [128, D], BF16, o), self.A("h16b", [128, D], BF16, o + 2048)]; o += 4096
        self.dma("pool", self.ident[:, :], self.d["ident"], writes=[("ident",)])
        self.dma("sp", self.c32[:, :], self.d["c32"], writes=[("c32",)])
        self._issue_w()
        if self.do_ln_in:
            self.dma("sp", gb[:, :, :], self.d["lnin"], writes=[("lnin",)])

            def load(t):
                self.dma("sp", xin[t % 3][:, :], self.d["x"][t * 128:(t + 1) * 128, :], writes=[("xin", t % 3)])
            load(0)
            load(1)
            self.ln_stats(xin[0][:, :], ("xin", 0), 0)
            for t in range(16):
                if t + 2 < 16:
                    load(t + 2)
                if t + 1 < 16:
                    self.ln_stats(xin[(t + 1) % 3][:, :], ("xin", (t + 1) % 3), t + 1)
                self.ln_apply(xin[t % 3][:, :], ("xin", t % 3), t, gb[:, 0, :], gb[:, 1, :], ("lnin",))
                self.to_xT_finish()
                self.to_xT(t, h16)
            self.to_xT_finish()
        else:
            for tt_i in range(16):
                self.dma("sp", self.xtok[:, tt_i, :], self.d["x"][tt_i * 128:(tt_i + 1) * 128, :], writes=[("xtok", tt_i)])
                self.to_xT(tt_i, h16)
            self.to_xT_finish()

    def attn_pipeline(self, units, Pb, exb, regions, scale, look, s_order=None, filler=None, filler_every=2):
        nu = len(units)
        nr = len(regions)
        nb = len(Pb)
        mulcount = [0]

        def emitS(u, extra=()):
            banks = regions[u % nr]
            offs = []
            off = 0
            for t in units[u]:
                offs.append(off)
                off += t["n"]
            order = list(range(len(units[u]))) if s_order is None or len(units[u]) != len(s_order) else list(s_order)
            for c0 in range(0, len(order), 2):
                grp = order[c0:c0 + 2]
                for i in grp:
                    t = units[u][i]
                    n = t["n"]
                    base = banks[0] * 512 + offs[i]
                    self.mm([banks[offs[i] // 512]], self.psall[:, base:base + n], t["s_lhsT"], t["s_rhs"], True,
                            t.get("b_rhs") is None, reads=list(t["s_reads"]) + list(extra))
                for i in grp:
                    t = units[u][i]
                    if t.get("b_rhs") is None:
                        continue
                    n = t["n"]
                    base = banks[0] * 512 + offs[i]
                    self.mm([banks[offs[i] // 512]], self.psall[:, base:base + n], self.ident[:, :], t["b_rhs"], False, True,
                            reads=[("ident",), t["b_res"]])

        for u in range(min(look, nu)):
            emitS(u)
        for u in range(nu):
            banks = regions[u % nr]
            k = u % nb
            tot = sum(t["n"] for t in units[u])
            base = banks[0] * 512
            src = self.psall[:, base:base + tot]
            ex = [("ps", b) for b in banks]
            if units[u][0]["etab"] is None:
                self.act(Pb[k][:, 0:tot], src, AF.Exp, scale=scale,
                         writes=[("P", k, i) for i in range(len(units[u]))], excl=ex)
            else:
                self.act(exb[k][:, 0:tot], src, AF.Exp, scale=scale, writes=[("ex", k)], excl=ex)
                off = 0
                for i, t in enumerate(units[u]):
                    n = t["n"]
                    eng = "pool" if (POOL_EVERY and mulcount[0] % POOL_EVERY == POOL_EVERY - 1) else "dve"
                    mulcount[0] += 1
                    self.tt(eng, Pb[k][:, off:off + n], exb[k][:, off:off + n], t["etab"], ALU.mult,
                            reads=[("ex", k), t["etab_res"]], writes=[("P", k, i)])
                    off += n
            if u + look < nu:
                emitS(u + look)
            off = 0
            for i, t in enumerate(units[u]):
                n = t["n"]
                ob = t["ob"]
                oc = t.get("ocol", 0)
                if t.get("zero_first"):
                    self.s.op("dve", (lambda ob: lambda e: e.memset(self.ps[ob][:, :], 0.0))(ob), excl=[("ps", ob)])
                if t.get("acc_only"):
                    self.mm(ob, self.ps[ob][:, oc:oc + n], t["v_lhsT"], Pb[k][:, off:off + n], False, False,
                            reads=[("P", k, i)] + t["v_reads"])
                else:
                    self.mm(ob, self.ps[ob][:, oc:oc + n], t["v_lhsT"], Pb[k][:, off:off + n], t["first"], t["last"],
                            reads=[("P", k, i)] + t["v_reads"])
                off += n
                if t["last"] and t["epi"] is not None:
                    t["epi"]()
            if filler is not None and u % filler_every == 0:
                next(filler, None)
        if filler is not None:
            for _ in filler:
                pass

    def attn_epilogue(self, ob, n, hb, rs, sz_ap, sz_res, y_ap, y_res, slot):
        so = 64 - hb
        ps = self.ps[ob]
        self.s.op("dve", lambda e: e.reciprocal(rs[hb:hb + 64, 0:n], ps[so:so + 64, 0:n]),
                  writes=[("rs", slot)], excl=[("ps", ob)])
        self.tt("pool", rs[hb:hb + 64, 0:n], rs[hb:hb + 64, 0:n], sz_ap, ALU.mult,
                reads=[("rs", slot), sz_res], writes=[("rs", slot)])
        self.tt("dve", y_ap, ps[hb:hb + 64, 0:n], rs[hb:hb + 64, 0:n], ALU.mult,
                reads=[("rs", slot)], writes=[y_res], excl=[("ps", ob)])

    def phaseA(self, l):
        o = self.R_ph
        qT = self.A("naq", [128, S], BF16, o); o += 4096
        kT = self.A("nak", [128, S], BF16, o); o += 4096
        sz = self.A("nasz", [128, S], BF16, o); o += 4096
        Va = self.A("nava", [128, 16, 2, 128], BF16, o); o += 8192
        exb = None
        Pb = [self.A("P%d" % i, [128, 1024], BF16, o + 2048 * i) for i in range(3)]; o += 6144
        rs = [[self.A("rs%d%d" % (i, h), [128, 256], F32, o + 2048 * i + 1024 * h) for h in range(2)] for i in range(2)]
        o += 4096
        bv = self.A("bvna", [128, 512], F32, o); o += 2048
        oa = self.R_y + 32768
        tabraw = self.A("tabraw", [128, 1792], F32, oa); oa += 7168
        E = [self.A("E%d" % i, [128, 1792], BF16, oa + 3584 * i) for i in range(2)]; oa += 7168
        regions = [[0, 1], [2, 3]]
        obanks = [(4, 5), (6, 7)]
        self.pbanks = [0, 1, 2, 3, 4, 5, 6, 7]
        d = self.d
        self.dma("sp", bv[:, :], d["bvb"][l, :, 0:512], writes=[("bvna",)])
        self.s.op("pool", lambda e: e.memset(Va[:, :, 0, 64:128], 1.0), writes=[("nava", j) for j in range(16)])
        self.s.op("pool", lambda e: e.memset(Va[:, :, 1, 0:64], 1.0), writes=[("nava", j) for j in range(16)])
        hcount = 0
        ecount = 0
        for c in range(4):
            (wt,), wres = self.take_w()

            def ev_q(tc, b, dst=qT, ch=CH_NA(c, 0), nm="naq"):
                self.ts("dve", dst[:, tc * 512:(tc + 1) * 512], self.ps[b][:, :], self.bias(ch), None, ALU.add,
                        reads=[("vecs",)], writes=[(nm, tc)], excl=[("ps", b)])
            self.proj_fm(wt, wres, 0, ev_q)
            self.proj_fm(wt, wres, 128, lambda tc, b: ev_q(tc, b, kT, CH_NA(c, 1), "nak"))

            def ev_z(tc, b, ch=CH_NA(c, 3)):
                self.act(sz[:, tc * 512:(tc + 1) * 512], self.ps[b][:, :], AF.Silu, bias=self.bias(ch),
                         reads=[("vecs",)], writes=[("nasz", tc)], excl=[("ps", b)])
            self.proj_fm(wt, wres, 384, ev_z)
            for j0 in range(0, 16, 4):
                b = self.bank()
                for t in range(4):
                    for kc in range(8):
                        self.mm(b, self.ps[b][:, t * 128:(t + 1) * 128], self.xT[:, kc, (j0 + t) * 128:(j0 + t + 1) * 128],
                                wt[:, kc, 256:384], kc == 0, kc == 7, reads=[wres, ("xT", j0 + t)])
                pv = self.ps[b][:, :].rearrange("p (a n) -> p a n", n=128)
                for hi in range(2):
                    bsl = bv[:, c * 128 + hi * 64:c * 128 + hi * 64 + 64].unsqueeze(1).broadcast_to([128, 4, 64])
                    self.tt("dve", Va[:, j0:j0 + 4, hi, hi * 64:hi * 64 + 64], pv[:, :, hi * 64:hi * 64 + 64], bsl, ALU.add,
                            reads=[("bvna",)], writes=[("nava", j0 + t) for t in range(4)], excl=[("ps", b)])
            units = []
            eks = []
            for hi in range(2):
                ek = ecount % 2
                ecount += 1
                eks.append(ek)
                self.dma("sp", tabraw[:, :], d["natab"][l, 2 * c + hi], writes=[("tabraw",)])
                self.act(E[ek][:, :], tabraw[:, :], AF.Identity, scale=8.0, reads=[("tabraw",)], writes=[("E", ek)])
            for q8 in range(8):
                r0 = 4 * q8
                if q8 == 0:
                    jl, kind = [0, 1, 2, 3], 0
                elif q8 == 7:
                    jl, kind = [12, 13, 14, 15], 0
                else:
                    jl, kind = list(range(2 * q8 - 2, 2 * q8 + 4)), 1
                obs = obanks[hcount % 2]
                slot = hcount % 2
                hcount += 1
                q0 = q8 * 256
                tcq = q0 // 512
                tl = {0: [], 1: []}
                for hi in range(2):
                    hb = 64 * hi
                    ek = eks[hi]
                    ob = obs[hi]

                    def epi(ob=ob, hb=hb, slot=slot, hi=hi, q0=q0, tcq=tcq, c=c):
                        self.attn_epilogue(ob, 256, hb, rs[slot][hi], sz[hb:hb + 64, q0:q0 + 256], ("nasz", tcq),
                                           self.yA[hb:hb + 64, c, q0:q0 + 256], ("yA", c, tcq), (slot, hi))
                    for idx, j in enumerate(jl):
                        s0 = r0 - 2 * j + 8
                        assert 2 <= s0 and s0 + 4 <= 16
                        tl[hi].append(dict(
                            n=256, s_lhsT=kT[hb:hb + 64, j * 128:(j + 1) * 128], s_rhs=qT[hb:hb + 64, q0:q0 + 256],
                            s_reads=[("nak", j // 4), ("naq", tcq)],
                            etab=None, etab_res=None,
                            b_rhs=E[ek][:, kind * 896 + (s0 - 2) * 64:kind * 896 + (s0 + 2) * 64], b_res=("E", ek),
                            v_lhsT=Va[:, j, hi, :], v_reads=[("nava", j)], ob=ob,
                            first=(idx == 0), last=(idx == len(jl) - 1), epi=epi if idx == len(jl) - 1 else None))
                for i in range(0, len(jl), 2):
                    units.append([tl[0][i], tl[0][i + 1], tl[1][i], tl[1][i + 1]])
            self.attn_pipeline(units, Pb, exb, regions, 0.125, look=2, s_order=[0, 2, 1, 3])
        if self.dbg:
            self.dbg_dump("yA%d" % l, self.yA, [("yA", c, t) for c in range(4) for t in range(4)])

    def dbg_dump(self, name, t, reads):
        shp = list(t.shape)
        dt = t.dtype
        dd = self.nc.dram_tensor("dbg_" + name, shp, dt, kind="ExternalOutput").ap()
        self.dbg_d[name] = dd
        self.dma("sp", dd, t[tuple(slice(None) for _ in shp)], reads=reads)

    def barrier(self):
        for e in Sched.ENGS:
            self.s.wait_all(e, dmas=False)

    def phaseB(self, l):
        o = self.R_ph
        kT2 = self.A("gk", [128, 2, S], BF16, o); o += 8192
        sz2 = [self.A("gsz%d" % i, [128, S], BF16, o + 4096 * i) for i in range(2)]; o += 8192
        ropeb = self.A("rope", [128, 2, 512], F32, o); o += 4096
        sq2 = [self.A("sq%d" % i, [128, 512], F32, o + 2048 * i) for i in range(2)]; o += 4096
        u2 = [self.A("u%d" % i, [128, 512], F32, o + 2048 * i) for i in range(2)]; o += 4096
        qb = self.A("qb", [128, 512], F32, o); o += 2048
        rstd = self.A("rstd", [128, 512], F32, o); o += 2048
        t1 = self.A("t1", [128, 512], F32, o); o += 2048
        bv = self.A("bvg", [128, 128], F32, o); o += 512
        Pb = [self.A("gP%d" % i, [128, 1024], BF16, o + 2048 * i) for i in range(2)]; o += 4096
        assert o <= SB_END, o
        oa = self.R_y + 32768
        Vg = self.A("gv", [128, 16, 2, 192], BF16, oa); oa += 12288
        rs = [[self.A("grs%d%d" % (i, h), [128, 256], F32, oa + 2048 * i + 1024 * h) for h in range(2)] for i in range(2)]
        oa += 4096
        regions = [[0, 1], [2, 3]]
        self.pbanks = [0, 1, 2, 3, 4, 5, 6, 7]
        d = self.d
        c32 = self.c32
        self.dma("sp", bv[:, :], d["bvb"][l, :, 512:640], writes=[("bvg",)])
        self.s.op("pool", lambda e: e.memset(Vg[:, :, :, 0:64], 1.0), writes=[("gv", j) for j in range(16)])
        self.s.op("pool", lambda e: e.memset(Vg[:, :, :, 128:192], 1.0), writes=[("gv", j) for j in range(16)])
        self.rope_i = 0

        def rope_steps(tc, b, bias_ch, gain_col, dst_ap, dst_keys):
            rk = self.rope_i % 2
            self.rope_i += 1
            sq, u = sq2[rk], u2[rk]
            self.dma("sp", ropeb[:, :, :], d["rope"][:, :, tc * 512:(tc + 1) * 512], writes=[("rope",)])
            ps = self.ps[b]
            self.ts("dve", qb[:, :], ps[:, :], self.bias(bias_ch), None, ALU.add, reads=[("vecs",)], writes=[("qb",)],
                    excl=[("ps", b)])
            self.tt("dve", sq[:, :], qb[:, :], qb[:, :], ALU.mult, reads=[("qb",)], writes=[("sq", rk)])
            self.ts("dve", u[:, :], qb[:, :], self.vecs[:, gain_col:gain_col + 1], None, ALU.mult,
                    reads=[("vecs",), ("qb",)], writes=[("u", rk)])
            yield
            b2 = self.bank()
            self.mm(b2, self.ps[b2][:, :], c32[:, 128:256], sq[:, :], True, True, reads=[("sq", rk), ("c32",)])
            b3 = self.bank()
            self.mm(b3, self.ps[b3][:, :], c32[:, 0:128], u[:, :], True, True, reads=[("u", rk), ("c32",)])
            self.act(rstd[:, :], self.ps[b2][:, :], AF.Ln, bias=64e-6, writes=[("rstd",)], excl=[("ps", b2)])
            self.act(rstd[:, :], rstd[:, :], AF.Exp, scale=-0.5, reads=[("rstd",)], writes=[("rstd",)])
            self.tt("pool", t1[:, :], u[:, :], ropeb[:, 0, :], ALU.mult, reads=[("u", rk), ("rope",)], writes=[("t1",)])
            self.tt("dve", sq[:, :], self.ps[b3][:, :], ropeb[:, 1, :], ALU.mult, reads=[("rope",)],
                    writes=[("sq", rk)], excl=[("ps", b3)])
            yield
            self.tt("dve", t1[:, :], t1[:, :], sq[:, :], ALU.add, reads=[("t1",), ("sq", rk)], writes=[("t1",)])
            self.tt("dve", dst_ap, t1[:, :], rstd[:, :], ALU.mult, reads=[("t1",), ("rstd",)], writes=dst_keys)
            yield

        def proj_chunk(wt, wres, col0, tc):
            b = self.bank()
            for kc in range(8):
                self.mm(b, self.ps[b][:, :], wt[:, kc, col0:col0 + 128], self.xT[:, kc, tc * 512:(tc + 1) * 512],
                        kc == 0, kc == 7, reads=[wres] + self.xT_res(tc))
            return b

        def ykeys(c, tc):
            return [("yB", c, 2 * tc + a, hi) for a in range(2) for hi in range(2)]

        def prologue(c):
            (wt,), wres = self.take_w()
            szc = sz2[c % 2]
            for tc in range(4):
                b = proj_chunk(wt, wres, 0, tc)
                yield
                for _ in rope_steps(tc, b, CH_GQ(c, 0), 64, self.yB[:, c, tc * 512:(tc + 1) * 512], ykeys(c, tc)):
                    yield
            for tc in range(4):
                b = proj_chunk(wt, wres, 128, tc)
                self.act(szc[:, tc * 512:(tc + 1) * 512], self.ps[b][:, :], AF.Silu, bias=self.bias(CH_GQ(c, 1)),
                         reads=[("vecs",)], writes=[("gsz", c % 2, tc)], excl=[("ps", b)])
            yield

        (wt,), wres = self.take_w()
        for g in range(2):
            for tc in range(4):
                b = proj_chunk(wt, wres, g * 128, tc)
                for _ in rope_steps(tc, b, CH_GK[g], 65, kT2[:, g, tc * 512:(tc + 1) * 512], [("gk", g, tc)]):
                    pass
        for j0 in range(0, 16, 4):
            b = self.bank()
            for t in range(4):
                for kc in range(8):
                    self.mm(b, self.ps[b][:, t * 128:(t + 1) * 128], self.xT[:, kc, (j0 + t) * 128:(j0 + t + 1) * 128],
                            wt[:, kc, 256:384], kc == 0, kc == 7, reads=[wres, ("xT", j0 + t)])
            pv = self.ps[b][:, :].rearrange("p (a n) -> p a n", n=128)
            for g in range(2):
                bsl = bv[:, g * 64:g * 64 + 64].unsqueeze(1).broadcast_to([128, 4, 64])
                self.tt("dve", Vg[:, j0:j0 + 4, g, 64:128], pv[:, :, g * 64:g * 64 + 64], bsl, ALU.add,
                        reads=[("bvg",)], writes=[("gv", j0 + t) for t in range(4)], excl=[("ps", b)])
        for _ in prologue(0):
            pass
        self.pbanks = [6, 7]
        hcount = 0
        for c in range(4):
            g = c // 2
            szc = sz2[c % 2]
            units = []
            for q8 in range(8):
                q0 = q8 * 256
                tcq = q0 // 512
                slot = hcount % 2
                hcount += 1
                tl = {0: [], 1: []}
                ob = 4 + slot
                for hi in range(2):
                    hb = 64 * hi
                    oc = 256 * hi

                    def epi(ob=ob, hb=hb, oc=oc, slot=slot, hi=hi, q0=q0, tcq=tcq, c=c, q8=q8, szc=szc):
                        so = 64 - hb
                        ps = self.ps[ob]
                        r = rs[slot][hi]
                        self.s.op("dve", lambda e: e.reciprocal(r[hb:hb + 64, 0:256], ps[so:so + 64, oc:oc + 256]),
                                  writes=[("rs", slot, hi)], excl=[("ps", ob)])
                        self.tt("pool", r[hb:hb + 64, 0:256], r[hb:hb + 64, 0:256], szc[hb:hb + 64, q0:q0 + 256], ALU.mult,
                                reads=[("rs", slot, hi), ("gsz", c % 2, tcq)], writes=[("rs", slot, hi)])
                        self.tt("dve", self.yB[hb:hb + 64, c, q0:q0 + 256], ps[hb:hb + 64, oc:oc + 256], r[hb:hb + 64, 0:256],
                                ALU.mult, reads=[("rs", slot, hi)], writes=[("yB", c, q8, hi)], excl=[("ps", ob)])
                    for j in range(16):
                        vl = Vg[:, j, g, 64:192] if hi == 0 else Vg[:, j, g, 0:128]
                        tl[hi].append(dict(
                            n=256, s_lhsT=kT2[hb:hb + 64, g, j * 128:(j + 1) * 128],
                            s_rhs=self.yB[hb:hb + 64, c, q0:q0 + 256],
                            s_reads=[("gk", g, j // 4), ("yB", c, q8, hi)], etab=None, etab_res=None,
                            v_lhsT=vl, v_reads=[("gv", j)], ob=ob, ocol=oc, acc_only=True,
                            zero_first=(j == 0 and hi == 0), first=(j == 0), last=(j == 15),
                            epi=epi if j == 15 else None))
                for i in range(0, 16, 2):
                    units.append([tl[0][i], tl[0][i + 1], tl[1][i], tl[1][i + 1]])
            filler = prologue(c + 1) if c + 1 < 4 else None
            self.attn_pipeline(units, Pb, None, regions, 8.0, look=2, s_order=[0, 2, 1, 3], filler=filler, filler_every=2)
        self.pbanks = [0, 1, 2, 3, 4, 5, 6, 7]
        if self.dbg:
            self.dbg_dump("yB%d" % l, self.yB, [("yB", c, q, h) for c in range(4) for q in range(8) for h in range(2)])

    def attn_epilogue2(self, ob, n, hb, osb, rs, sz_ap, sz_res, y_ap, y_res, slot):
        so = 64 - hb
        self.cp("dve", osb[:, 0:n], self.ps[ob][:, 0:n], writes=[("osb", slot)], excl=[("ps", ob)])
        self.s.op("dve", lambda e: e.reciprocal(rs[hb:hb + 64, 0:n], osb[so:so + 64, 0:n]),
                  reads=[("osb", slot)], writes=[("rs", slot)])
        self.tt("pool", rs[hb:hb + 64, 0:n], rs[hb:hb + 64, 0:n], sz_ap, ALU.mult,
                reads=[("rs", slot), sz_res], writes=[("rs", slot)])
        self.tt("dve", y_ap, osb[hb:hb + 64, 0:n], rs[hb:hb + 64, 0:n], ALU.mult,
                reads=[("rs", slot), ("osb", slot)], writes=[y_res])

    def phaseC(self, l):
        o = self.R_ph
        vn = self.A("vn", [128, 16, 512], BF16, o); o += 16384
        sgw = self.A("sgw", [128, 8, 128], BF16, o); o += 2048
        vraw = [self.A("vraw%d" % i, [128, 512], F32, o + 2048 * i) for i in range(2)]; o += 4096
        sgln = self.A("sgln", [128, 2, 512], F32, o); o += 4096
        bvs = self.A("bvs", [128, 512], F32, o); o += 2048
        uzb = [self.A("uz%d" % i, [128, 512], F32, o + 2048 * i) for i in range(2)]; o += 4096
        szt = [self.A("szt%d" % i, [128, 512], F32, o + 2048 * i) for i in range(2)]; o += 4096
        sgb = [self.A("sgb%d" % i, [128, 512], F32, o + 2048 * i) for i in range(2)]; o += 4096
        mix = [self.A("mix%d" % i, [128, 512], F32, o + 2048 * i) for i in range(2)]; o += 4096
        assert o <= SB_END, o
        self.pbanks = [0, 1, 2, 3, 4, 5, 6]
        d = self.d
        st = self.stat
        self.dma("sp", bvs[:, :], d["bvb"][l, :, 640:1152], writes=[("bvs",)])
        self.dma("sp", sgln[:, :, :], d["sgln"][l], writes=[("sgln",)])
        self.dma("pool", sgw[:, :, :], d["sgwT"][l], writes=[("sgw",)])
        (wt,), wres = self.take_w()
        for tt_i in range(16):
            b = self.bank()
            k = tt_i % 2
            for kc in range(8):
                self.mm(b, self.ps[b][:, :], self.xT[:, kc, tt_i * 128:(tt_i + 1) * 128], wt[:, kc, 0:512],
                        kc == 0, kc == 7, reads=[wres, ("xT", tt_i)])
            vr = vraw[k]
            self.tt("dve", vr[:, :], self.ps[b][:, :], bvs[:, :], ALU.add, reads=[("bvs",)], writes=[("vraw", k)],
                    excl=[("ps", b)])
            sk = 16 * k
            self.s.op("dve", (lambda vr, sk: lambda e: e.bn_stats(st[:, sk:sk + 6], vr[:, :]))(vr, sk),
                      reads=[("vraw", k)], writes=[("stat", k)])
            self.s.op("dve", (lambda sk: lambda e: e.bn_aggr(st[:, sk + 12:sk + 14], st[:, sk:sk + 6]))(sk),
                      reads=[("stat", k)], writes=[("stat", k)])
            self.act(st[:, sk + 14:sk + 15], st[:, sk + 13:sk + 14], AF.Sqrt, bias=1e-5,
                     reads=[("stat", k)], writes=[("stat", k)])
            self.s.op("dve", (lambda sk: lambda e: e.reciprocal(st[:, sk + 14:sk + 15], st[:, sk + 14:sk + 15]))(sk),
                      reads=[("stat", k)], writes=[("stat", k)])
            self.stt("dve", vr[:, :], vr[:, :], st[:, sk + 12:sk + 13], sgln[:, 0, :], ALU.subtract, ALU.mult,
                     reads=[("vraw", k), ("stat", k), ("sgln",)], writes=[("vraw", k)])
            self.stt("dve", vn[:, tt_i, :], vr[:, :], st[:, sk + 14:sk + 15], sgln[:, 1, :], ALU.mult, ALU.add,
                     reads=[("vraw", k), ("stat", k), ("sgln",)], writes=[("vn", tt_i)])
        cnt = 0
        for c in range(4):
            (wt,), wres = self.take_w()
            self.dma("sp", sgb[c % 2][:, :], d["sgb"][l, :, c, :], writes=[("sgb", c % 2)])
            for tc in range(4):
                bu = self.bank()
                for kc in range(8):
                    self.mm(bu, self.ps[bu][:, :], wt[:, kc, 0:128], self.xT[:, kc, tc * 512:(tc + 1) * 512],
                            kc == 0, kc == 7, reads=[wres] + self.xT_res(tc))
                bz = self.bank()
                for kc in range(8):
                    self.mm(bz, self.ps[bz][:, :], wt[:, kc, 128:256], self.xT[:, kc, tc * 512:(tc + 1) * 512],
                            kc == 0, kc == 7, reads=[wres] + self.xT_res(tc))
                k = cnt % 2
                cnt += 1
                self.act(szt[k][:, :], self.ps[bz][:, :], AF.Silu, bias=self.bias(CH_SG(c, 1)), reads=[("vecs",)],
                         writes=[("szt", k)], excl=[("ps", bz)])
                uzs = uzb[k][:, :]
                self.stt("dve", uzs, self.ps[bu][:, :], self.bias(CH_SG(c, 0)), szt[k][:, :], ALU.add, ALU.mult,
                         reads=[("vecs",), ("szt", k)], writes=[("uz", k)], excl=[("ps", bu)])
                bm = [self.bank(), self.bank()]
                for gi in range(2):
                    for t in range(4):
                        tt_i = 4 * tc + t
                        self.mm(bm[gi], self.ps[bm[gi]][:, t * 128:(t + 1) * 128], vn[:, tt_i, c * 128:(c + 1) * 128],
                                sgw[:, 2 * c + gi, :], True, True, reads=[("vn", tt_i), ("sgw",)])
                mk = mix[k]
                for gi in range(2):
                    hb = 64 * gi
                    self.tt("dve", mk[hb:hb + 64, :], self.ps[bm[gi]][hb:hb + 64, :], sgb[c % 2][hb:hb + 64, :], ALU.add,
                            reads=[("sgb", c % 2)], writes=[("mix", k)], excl=[("ps", bm[gi])])
                self.tt("pool", self.yC[:, c, tc * 512:(tc + 1) * 512], mk[:, :], uzs, ALU.mult,
                        reads=[("mix", k), ("uz", k)], writes=[("yC", c, tc)])
        if self.dbg:
            self.dbg_dump("yC%d" % l, self.yC, [("yC", c, t) for c in range(4) for t in range(4)])

    def phaseD(self, l, last):
        o = self.R_ph
        mT = self.A("mT", [128, 8, S], BF16, o); o += 32768
        gsig = [self.A("gsig%d" % i, [128, 512], F32, o + 2048 * i) for i in range(2)]; o += 4096
        mt = [self.A("mt%d" % i, [128, 512], F32, o + 2048 * i) for i in range(3)]; o += 6144
        assert o <= SB_END, o
        self.pbanks = [0, 1, 2, 3, 4, 5, 6]
        d = self.d
        gcnt = 0
        for dc in range(8):
            (wg, wb), wres = self.take_w()
            for tc in range(4):
                for br in range(3):
                    bg = self.bank()
                    for kc in range(8):
                        self.mm(bg, self.ps[bg][:, :], wg[:, kc, br * 128:(br + 1) * 128],
                                self.xT[:, kc, tc * 512:(tc + 1) * 512], kc == 0, kc == 7, reads=[wres] + self.xT_res(tc))
                    k = gcnt % 2
                    gcnt += 1
                    self.act(gsig[k][:, :], self.ps[bg][:, :], AF.Sigmoid, bias=self.bias(CH_GATE(dc, br)),
                             reads=[("vecs",)], writes=[("gsig", k)], excl=[("ps", bg)])
                    bp = self.bank()
                    yb = self.ybr[br]
                    nm = ("yA", "yB", "yC")[br]
                    for kc in range(4):
                        yk = [(nm, kc, tc)] if nm != "yB" else [("yB", kc, 2 * tc + a, h2) for a in range(2) for h2 in range(2)]
                        self.mm(bp, self.ps[bp][:, :], wb[:, kc, br * 128:(br + 1) * 128], yb[:, kc, tc * 512:(tc + 1) * 512],
                                kc == 0, kc == 3, reads=[wres] + yk)
                    self.tt("dve", mt[br][:, :], self.ps[bp][:, :], gsig[k][:, :], ALU.mult, reads=[("gsig", k)],
                            writes=[("mt", br)], excl=[("ps", bp)])
                self.tt("pool", mt[0][:, :], mt[0][:, :], mt[1][:, :], ALU.add, reads=[("mt", 0), ("mt", 1)], writes=[("mt", 0)])
                self.tt("pool", mT[:, dc, tc * 512:(tc + 1) * 512], mt[0][:, :], mt[2][:, :], ALU.add,
                        reads=[("mt", 0), ("mt", 2)], writes=[("mT", tc)])
        self.barrier()
        oa = self.R_y
        post = self.A("post", [128, 3, D], F32, oa); oa += 12288
        h16 = [self.A("dh16%d" % i, [128, D], BF16, oa + 2048 * i) for i in range(2)]; oa += 4096
        self.dma("sp", post[:, :, :], d["post"][l], writes=[("post",)])
        for hf in range(2):
            (wo,), wres = self.take_w()
            for tt_i in range(16):
                tcq = tt_i // 4
                b = self.bank()
                for dc in range(8):
                    self.mm(b, self.ps[b][:, :], mT[:, dc, tt_i * 128:(tt_i + 1) * 128], wo[:, dc, 0:512],
                            dc == 0, dc == 7, reads=[("mT", tcq), wres])
                x_ap = self.xtok[:, tt_i, hf * 512:(hf + 1) * 512]
                self.stt("dve", x_ap, x_ap, float(ALPHA), self.ps[b][:, :], ALU.mult, ALU.add,
                         reads=[("xtok", tt_i)], writes=[("xtok", tt_i)], excl=[("ps", b)])
                if hf == 1:
                    self.tt("pool", self.xtok[:, tt_i, :], self.xtok[:, tt_i, :], post[:, 0, :], ALU.add,
                            reads=[("xtok", tt_i), ("post",)], writes=[("xtok", tt_i)])
        self.ln_stats(self.xtok[:, 0, :], ("xtok", 0), 0)
        for t in range(16):
            if t + 1 < 16:
                self.ln_stats(self.xtok[:, t + 1, :], ("xtok", t + 1), t + 1)
            self.ln_apply(self.xtok[:, t, :], ("xtok", t), t, post[:, 1, :], post[:, 2, :], ("post",))
            if last:
                self.dma("sp", self.y_d[t * 128:(t + 1) * 128, :], self.xtok[:, t, :], reads=[("xtok", t)])
            else:
                self.to_xT_finish()
                self.to_xT(t, h16)
        self.to_xT_finish()
        self.barrier()

    def layer(self, l, last):
        self.dma("sp", self.vecs[:, :], self.d["vecs"][l], writes=[("vecs",)])
        self.phaseA(l)
        self.barrier()
        self.phaseB(l)
        self.barrier()
        self.phaseC(l)
        self.barrier()
        self.phaseD(l, last)

    def build(self):
        from contextlib import ExitStack
        nc = self.nc
        self.phase0()
        self.barrier()
        for i, l in enumerate(self.layers):
            self.layer(l, last=(i == len(self.layers) - 1))
        self.s.wait_all("sp")
        with ExitStack() as es:
            sems = {e: es.enter_context(nc.semaphore("s_" + e)) for e in Sched.ENGS}
            dsems = [es.enter_context(nc.semaphore("q%d" % i)) for i in range(self.s.n_dma)]
            block = es.enter_context(nc.Block())
            self.s.emit(nc, block, sems, dsems)
        return nc


def _run(layers, do_ln_in, x_list, shared, dbg=False):
    kb = KB(layers, do_ln_in, dbg)
    nc = kb.build()
    in_maps = []
    for xb in x_list:
        m = dict(shared)
        m["x"] = np.ascontiguousarray(xb, dtype=np.float32)
        in_maps.append(m)
    return run_bass_kernel_spmd(nc, in_maps, core_ids=list(range(len(x_list))))


def kernel(**inputs):
    x = np.asarray(inputs["x"], dtype=np.float32)
    shared = prep_inputs(**inputs)
    res = _run([0, 1], True, [x[b] for b in range(x.shape[0])], shared)
    return np.stack([r["y"] for r in res.results], axis=0).astype(np.float32)
```

```python
import numpy as np
import concourse.bass as bass
import concourse.mybir as mybir
from concourse.bass_utils import run_bass_kernel_spmd

F32, BF16 = mybir.dt.float32, mybir.dt.bfloat16
AF = mybir.ActivationFunctionType
ALU = mybir.AluOpType

S = 2048
D = 1024
L = 2
NCH = 63
WIN = NCH * 128
ALPHA = (2.0 * L) ** 0.25
SB_BASE = 16512
SB_END = 229376
EMBED_WAIT = ("act", "dve", "pool")


class Sched:
    ENGS = ("pe", "act", "dve", "pool", "sp")

    def __init__(self, n_dma_sems=24):
        self.prog = {e: [] for e in self.ENGS}
        self.serial = {e: 0 for e in self.ENGS}
        self.seen = {e: {} for e in self.ENGS}
        self.lastw = {}
        self.readers = {}
        self.lastx = {}
        self.waited = {e: set() for e in self.ENGS}
        self.n_dma = n_dma_sems
        self.dma_i = 0
        self.dma_ip = 0
        self.dma_val = [0] * n_dma_sems

    def _deps(self, eng, reads, writes, excl):
        need = {}

        def add(tok, raw):
            key, val, teng = tok
            if teng == eng and not raw:
                return
            if self.seen[eng].get(key, 0) >= val:
                return
            if need.get(key, 0) < val:
                need[key] = val

        for r in reads:
            t = self.lastw.get(r)
            if t:
                add(t, True)
        for w in writes:
            t = self.lastw.get(w)
            if t:
                add(t, False)
            for t in self.readers.get(w, {}).values():
                add(t, False)
        for x in excl:
            t = self.lastx.get(x)
            if t:
                add(t, False)
        for key, val in need.items():
            self.seen[eng][key] = val
            self.prog[eng].append(("wait", key, val))
            if key[0] != "q":
                self.waited[key].add(val)

    def _commit(self, tok, reads, writes, excl):
        for r in reads:
            self.readers.setdefault(r, {})[tok[0]] = tok
        for w in writes:
            self.lastw[w] = tok
            self.readers[w] = {}
        for x in excl:
            self.lastx[x] = tok

    def op(self, eng, fn, reads=(), writes=(), excl=()):
        self._deps(eng, reads, writes, excl)
        self.serial[eng] += 1
        tok = (eng, self.serial[eng], eng)
        self.prog[eng].append(("op", fn, self.serial[eng]))
        self._commit(tok, reads, writes, excl)
        return tok

    def dma(self, eng, fn, reads=(), writes=()):
        self._deps(eng, reads, writes, ())
        if eng == "pool":
            k = 16 + self.dma_ip % (self.n_dma - 16)
            self.dma_ip += 1
        else:
            k = self.dma_i % 16
            self.dma_i += 1
        key = "q%d" % k
        prev = self.dma_val[k]
        if prev > 0 and self.seen[eng].get(key, 0) < prev:
            self.seen[eng][key] = prev
            self.prog[eng].append(("wait", key, prev))
        self.dma_val[k] += 16
        tok = (key, self.dma_val[k], None)
        self.prog[eng].append(("dma", fn, k))
        self._commit(tok, reads, writes, ())
        return tok

    def wait_all(self, eng, dmas=True):
        for e in self.ENGS:
            if e != eng and self.serial[e] > 0 and self.seen[eng].get(e, 0) < self.serial[e]:
                self.seen[eng][e] = self.serial[e]
                self.prog[eng].append(("wait", e, self.serial[e]))
                self.waited[e].add(self.serial[e])
        for k in range(self.n_dma if dmas else 0):
            key = "q%d" % k
            if self.dma_val[k] > 0 and self.seen[eng].get(key, 0) < self.dma_val[k]:
                self.seen[eng][key] = self.dma_val[k]
                self.prog[eng].append(("wait", key, self.dma_val[k]))

    def emit(self, nc, block, sems, dsems):
        rank = {}
        for e in self.ENGS:
            ws = sorted(self.waited[e])
            rank[e] = {s: i + 1 for i, s in enumerate(ws)}
        handles = {"pe": "tensor", "act": "scalar", "dve": "vector", "pool": "gpsimd", "sp": "sync"}

        def run(ename):
            def body(eng):
                pend = []

                def semval(key, val):
                    if key[0] == "q":
                        return dsems[int(key[1:])], val
                    return sems[key], rank[key][val]

                for item in self.prog[ename]:
                    if item[0] == "wait":
                        pend.append(semval(item[1], item[2]))
                        continue
                    embed = None
                    if pend and item[0] == "op" and ename in EMBED_WAIT:
                        embed = pend.pop()
                    for (sm, v) in pend:
                        eng.wait_ge(sm, v)
                    pend = []
                    ins = item[1](eng)
                    if embed is not None:
                        ins._wait_ge(embed[0], embed[1])
                    if item[0] == "op":
                        if item[2] in rank[ename]:
                            ins.then_inc(sems[ename], 1)
                    else:
                        ins.then_inc(dsems[item[2]], 16)
                for (sm, v) in pend:
                    eng.wait_ge(sm, v)
            return body

        for ename in self.ENGS:
            getattr(block, handles[ename])(run(ename))


def _win_cols():
    naq, nak, nav, naz, gqq, gqk, gqv, gqz, sgu, sgv, sgz, gat = (0, 512, 1024, 1536, 2048, 2560, 2688, 2816,
                                                                 3328, 3840, 4352, 4864)
    r = lambda a, n: list(range(a, a + n))
    cols = []
    for c in range(4):
        cols += r(naq + c * 128, 128) + r(nak + c * 128, 128) + r(nav + c * 128, 128) + r(naz + c * 128, 128)
    cols += r(gqk, 64) + r(gqk, 64) + r(gqk + 64, 64) + r(gqk + 64, 64) + r(gqv, 128)
    for c in range(4):
        cols += r(gqq + c * 128, 128) + r(gqz + c * 128, 128)
    cols += r(sgv, 512)
    for c in range(4):
        cols += r(sgu + c * 128, 128) + r(sgz + c * 128, 128)
    for dc in range(8):
        for b in range(3):
            cols += r(gat + b * 1024 + dc * 128, 128)
    assert len(cols) == WIN
    return np.asarray(cols)


def CH_NA(c, which):
    return 4 * c + which
CH_GK = (16, 17)
CH_GV = 18
def CH_GQ(c, which):
    return 19 + 2 * c + which
CH_SV = 27
def CH_SG(c, which):
    return 31 + 2 * c + which
def CH_GATE(dc, b):
    return 39 + 3 * dc + b


def _na_index_tables():
    p = np.arange(128)
    ck = p % 64
    half = p // 64
    s = np.arange(2, 16)
    cq = np.arange(64)
    dr = (8 - s)[None, :, None] + half[:, None, None] + 0 * cq[None, None, :]
    dcol = np.clip(ck[:, None, None] - cq[None, None, :] + 15, 0, 30) + 0 * s[None, :, None]
    cs = np.clip(cq - 8, 0, 48)
    colv = (ck[:, None, None] >= cs[None, None, :]) & (ck[:, None, None] < cs[None, None, :] + 16)
    colv = colv & (s[None, :, None] > -100)
    full_ok = colv & (np.abs(dr) <= 7)
    int_ok = colv & (dr >= -4) & (dr <= 3)
    ridx = np.clip(dr + 7, 0, 14)
    return ridx, dcol, full_ok, int_ok


def _rope_tables():
    t = np.arange(S)
    row = (t // 64).astype(np.float64)
    col = (t % 64).astype(np.float64)
    freqs = 10000.0 ** (-np.arange(0, 32, 2, dtype=np.float64) / 32.0)
    ang = np.concatenate([row[:, None] * freqs, col[:, None] * freqs], axis=-1)
    cos = np.cos(ang)
    sin = np.sin(ang)
    p = np.arange(128)
    d = p % 64
    C = cos[:, d // 2].T
    Sg = (sin[:, d // 2] * np.where(d % 2 == 0, -1.0, 1.0)[None, :]).T
    return np.ascontiguousarray(np.stack([C, Sg], axis=1)).astype(np.float32)


def _consts32():
    perm = np.zeros((128, 128), np.float32)
    for i in range(64):
        perm[2 * i + 1, 2 * i] = 1.0
        perm[2 * i, 2 * i + 1] = 1.0
    k = np.arange(128)
    bones = (k[:, None] // 64 == k[None, :] // 64).astype(np.float32)
    return np.concatenate([perm, bones], axis=1)


def prep_inputs(x, ln_in_g, ln_in_b, w_in, b_in, na_rpb, q_norm_g, k_norm_g, sg_ln_g, sg_ln_b, sg_w, sg_b,
                w_br_a, w_br_b, w_br_c, w_out, b_out, ln_post_g, ln_post_b):
    f = lambda a: np.ascontiguousarray(np.asarray(a, dtype=np.float32))
    cols = _win_cols()
    shared = {}
    shared["w_in_p"] = f(np.take(w_in, cols, axis=2))
    bp = np.take(b_in, cols, axis=1)
    vecs = np.zeros((L, 128, 80), np.float32)
    vecs[:, :, 0:NCH] = bp.reshape(L, NCH, 128).transpose(0, 2, 1)
    vecs[:, :, 64] = np.tile(q_norm_g, (1, 2))
    vecs[:, :, 65] = np.tile(k_norm_g, (1, 2))
    shared["vecs"] = vecs
    bv = np.concatenate([b_in[:, 1024:1536], b_in[:, 2688:2816], b_in[:, 3840:4352]], axis=1)
    shared["bvb"] = f(np.broadcast_to(bv[:, None, :], (L, 128, 1152)))
    shared["lnin"] = f(np.broadcast_to(np.stack([ln_in_g, ln_in_b])[None], (128, 2, D)))
    ridx, dcol, full_ok, int_ok = _na_index_tables()
    rp = np.asarray(na_rpb, np.float32)
    g = rp[:, :, ridx, dcol]
    neg = np.float32(-1e30)
    tfull = np.where(full_ok[None, None], g, neg)
    tint = np.where(int_ok[None, None], g, neg)
    shared["natab"] = f(np.stack([tfull, tint], axis=3).reshape(L, 8, 128, 2 * 896))
    shared["rope"] = _rope_tables()
    shared["c32"] = _consts32()
    shared["ident"] = np.eye(128, dtype=np.float32)
    shared["sgln"] = f(np.broadcast_to(np.stack([sg_ln_g, sg_ln_b], axis=1)[:, None], (L, 128, 2, 512)))
    shared["sgwT"] = f(np.transpose(sg_w, (0, 3, 1, 2)))
    sb = np.asarray(sg_b, np.float32)
    sbb = sb.reshape(L, 4, 2, 1, 128)
    sbb = np.broadcast_to(sbb, (L, 4, 2, 64, 128)).reshape(L, 4, 128, 128)
    sbb = np.broadcast_to(sbb[:, :, :, None, :], (L, 4, 128, 4, 128)).reshape(L, 4, 128, 512)
    shared["sgb"] = f(sbb.transpose(0, 2, 1, 3))
    wf = np.stack([w_br_a, w_br_b, w_br_c], axis=2)
    wf = wf.reshape(L, 512, 3, 8, 128).transpose(0, 1, 3, 2, 4).reshape(L, 512, 3072)
    shared["wfin"] = f(wf)
    shared["wout"] = f(w_out)
    shared["post"] = f(np.broadcast_to(np.stack([b_out, ln_post_g, ln_post_b], axis=1)[:, None], (L, 128, 3, D)))
    return shared


DUMMY_MM = 0
POOL_EVERY = 0


class KB:
    def __init__(self, layers, do_ln_in, dbg=False):
        self.layers = list(layers)
        self.do_ln_in = do_ln_in
        self.dbg = dbg
        self.s = Sched()
        nc = self.nc = bass.Bass("TRN2", target_bir_lowering=False)
        di = lambda n, shp: nc.dram_tensor(n, shp, F32, kind="ExternalInput").ap()
        self.d = dict(
            x=di("x", [S, D]), lnin=di("lnin", [128, 2, D]), w_in_p=di("w_in_p", [L, D, WIN]),
            vecs=di("vecs", [L, 128, 80]), bvb=di("bvb", [L, 128, 1152]), natab=di("natab", [L, 8, 128, 1792]),
            rope=di("rope", [128, 2, S]), c32=di("c32", [128, 256]), ident=di("ident", [128, 128]),
            sgln=di("sgln", [L, 128, 2, 512]), sgwT=di("sgwT", [L, 128, 8, 128]), sgb=di("sgb", [L, 128, 4, 512]),
            wfin=di("wfin", [L, 512, 3072]), wout=di("wout", [L, D, D]), post=di("post", [L, 128, 3, D]),
        )
        self.y_d = nc.dram_tensor("y", [S, D], F32, kind="ExternalOutput").ap()
        self.dbg_d = {}
        self._nm = 0
        o = SB_BASE
        self.xtok = self.A("xtok", [128, 16, D], F32, o); o += 65536
        self.xT = self.A("xT", [128, 8, S], BF16, o); o += 32768
        self.R_y = o
        self.yA = self.A("yA", [128, 4, S], BF16, o); o += 16384
        self.yB = self.A("yB", [128, 4, S], BF16, o); o += 16384
        self.yC = self.A("yC", [128, 4, S], BF16, o); o += 16384
        self.ybr = [self.yA, self.yB, self.yC]
        self.ident = self.A("ident", [128, 128], BF16, o); o += 256
        self.c32 = self.A("c32", [128, 256], F32, o); o += 1024
        self.vecs = self.A("vecs", [128, 80], F32, o); o += 320
        self.stat = self.A("stat", [128, 64], F32, o); o += 256
        self.R_loc = o
        self.wslot = [self.A("wslot0", [128, 4608], BF16, o), self.A("wslot1", [128, 4608], BF16, o + 9216)]
        o += 18432
        self.R_ph = o
        self.psall = nc.alloc_psum_tensor("psall", [128, 4096], F32)
        self.ps = [self.psall[:, b * 512:(b + 1) * 512] for b in range(8)]
        self.pst = self.psall[:, 3584:4096].bitcast(BF16)
        self.pbanks = [0, 1, 2, 3, 4, 5, 6]
        self.pb_i = 0
        self.wjobs = []
        for l in self.layers:
            self.wjobs += self.layer_wjobs(l)
        self.w_issued = 0
        self.w_used = 0
        self._xT_pending = None

    def A(self, name, shape, dtype, off):
        nbytes = int(np.prod(shape[1:])) * (4 if dtype == F32 else 2)
        assert off + nbytes <= SB_END, (name, off, nbytes)
        self._nm += 1
        return self.nc.alloc_sbuf_tensor_at("%s_%d" % (name, self._nm), list(shape), dtype, offset=off)

    def bank(self):
        b = self.pbanks[self.pb_i % len(self.pbanks)]
        self.pb_i += 1
        return b

    def mm(self, banks, out, lhsT, rhs, start, stop, reads):
        if isinstance(banks, int):
            banks = [banks]
        self.s.op("pe", lambda e: e.matmul(out, lhsT, rhs, start=start, stop=stop), reads=reads,
                  excl=[("ps", b) for b in banks])

    def act(self, out, in_, func, reads=(), writes=(), excl=(), **kw):
        self.s.op("act", lambda e: e.activation(out, in_, func, **kw), reads=reads, writes=writes, excl=excl)

    def tt(self, eng, out, in0, in1, op, reads=(), writes=(), excl=()):
        self.s.op(eng, lambda e: e.tensor_tensor(out, in0, in1, op), reads=reads, writes=writes, excl=excl)

    def ts(self, eng, out, in0, s1, s2, op0, op1=None, reads=(), writes=(), excl=()):
        if op1 is None:
            self.s.op(eng, lambda e: e.tensor_scalar(out, in0, s1, None, op0), reads=reads, writes=writes, excl=excl)
        else:
            self.s.op(eng, lambda e: e.tensor_scalar(out, in0, s1, s2, op0, op1), reads=reads, writes=writes, excl=excl)

    def stt(self, eng, out, in0, scalar, in1, op0, op1, reads=(), writes=(), excl=()):
        self.s.op(eng, lambda e: e.scalar_tensor_tensor(out, in0, scalar, in1, op0, op1), reads=reads, writes=writes,
                  excl=excl)

    def cp(self, eng, out, in_, reads=(), writes=(), excl=()):
        self.s.op(eng, lambda e: e.tensor_copy(out, in_), reads=reads, writes=writes, excl=excl)

    def dma(self, eng, out, in_, reads=(), writes=()):
        self.s.dma(eng, lambda e: e.dma_start(out=out, in_=in_), reads=reads, writes=writes)

    def layer_wjobs(self, l):
        w = self.d["w_in_p"]
        jobs = []
        for c in range(4):
            jobs.append([(0, 8, 512, w[l, :, c * 512:(c + 1) * 512])])
        g0 = CH_GK[0] * 128
        jobs.append([(0, 8, 384, w[l, :, g0:g0 + 384])])
        for c in range(4):
            c0 = CH_GQ(c, 0) * 128
            jobs.append([(0, 8, 256, w[l, :, c0:c0 + 256])])
        c0 = CH_SV * 128
        jobs.append([(0, 8, 512, w[l, :, c0:c0 + 512])])
        for c in range(4):
            c0 = CH_SG(c, 0) * 128
            jobs.append([(0, 8, 256, w[l, :, c0:c0 + 256])])
        for dc in range(8):
            g0 = CH_GATE(dc, 0) * 128
            jobs.append([(0, 8, 384, w[l, :, g0:g0 + 384]),
                         (3072, 4, 384, self.d["wfin"][l, :, dc * 384:(dc + 1) * 384])])
        for hf in range(2):
            jobs.append([(0, 8, 512, self.d["wout"][l, :, hf * 512:(hf + 1) * 512])])
        return jobs

    def _issue_w(self):
        k = self.w_issued
        if k >= len(self.wjobs):
            return
        self.w_issued += 1
        i = k % 2
        for (off, kch, ncols, src) in self.wjobs[k]:
            dst = self.wslot[i][:, off:off + kch * ncols].rearrange("p (k n) -> p k n", n=ncols)
            self.dma("pool", dst, src.rearrange("(k p) n -> p k n", p=128), writes=[("w", i)])

    def take_w(self):
        k = self.w_used
        self.w_used += 1
        while self.w_issued <= min(k + 1, len(self.wjobs) - 1):
            self._issue_w()
        i = k % 2
        views = []
        for (off, kch, ncols, src) in self.wjobs[k]:
            views.append(self.wslot[i][:, off:off + kch * ncols].rearrange("p (k n) -> p k n", n=ncols))
        return views, ("w", i)

    def xT_res(self, tc):
        return [("xT", 4 * tc + i) for i in range(4)]

    def proj_fm(self, wt, wres, col0, evac):
        for tc in range(4):
            b = self.bank()
            for kc in range(8):
                self.mm(b, self.ps[b][:, :], wt[:, kc, col0:col0 + 128], self.xT[:, kc, tc * 512:(tc + 1) * 512],
                        kc == 0, kc == 7, reads=[wres] + self.xT_res(tc))
            evac(tc, b)

    def bias(self, ch):
        return self.vecs[:, ch:ch + 1]

    def ln_stats(self, src, src_res, tt_i):
        st = self.stat
        k = (tt_i % 4) * 16
        for h in range(2):
            self.s.op("dve", (lambda h: lambda e: e.bn_stats(st[:, k + h * 6:k + h * 6 + 6],
                                                               src[:, h * 512:(h + 1) * 512]))(h),
                      reads=[src_res], writes=[("stat", tt_i % 4)])
        self.s.op("dve", lambda e: e.bn_aggr(st[:, k + 12:k + 14], st[:, k:k + 12]),
                  reads=[("stat", tt_i % 4)], writes=[("stat", tt_i % 4)])
        self.act(st[:, k + 14:k + 15], st[:, k + 13:k + 14], AF.Sqrt, bias=1e-5,
                 reads=[("stat", tt_i % 4)], writes=[("stat", tt_i % 4)])

    def ln_apply(self, src, src_res, tt_i, g_ap, b_ap, gb_res):
        st = self.stat
        k = (tt_i % 4) * 16
        self.s.op("dve", lambda e: e.reciprocal(st[:, k + 14:k + 15], st[:, k + 14:k + 15]),
                  reads=[("stat", tt_i % 4)], writes=[("stat", tt_i % 4)])
        xt = self.xtok[:, tt_i, :]
        self.stt("dve", xt, src, st[:, k + 12:k + 13], g_ap, ALU.subtract, ALU.mult,
                 reads=[src_res, ("stat", tt_i % 4), gb_res], writes=[("xtok", tt_i)])
        self.stt("dve", xt, xt, st[:, k + 14:k + 15], b_ap, ALU.mult, ALU.add,
                 reads=[("xtok", tt_i), ("stat", tt_i % 4), gb_res], writes=[("xtok", tt_i)])

    def to_xT(self, tt_i, tmp_h16):
        if self._xT_pending is not None:
            self.to_xT_finish()
        k = tt_i % 2
        h16 = tmp_h16[k]
        self.act(h16[:, :], self.xtok[:, tt_i, :], AF.Copy, reads=[("xtok", tt_i)], writes=[("h16", k)])
        for kc in range(8):
            self.s.op("pe", (lambda kc: lambda e: e.transpose(self.pst[:, kc * 128:(kc + 1) * 128],
                                                              h16[:, kc * 128:(kc + 1) * 128], self.ident[:, :]))(kc),
                      reads=[("h16", k), ("ident",)], excl=[("ps", 7)])
        self._xT_pending = tt_i

    def to_xT_finish(self):
        tt_i = self._xT_pending
        if tt_i is None:
            return
        self._xT_pending = None
        self.cp("dve", self.xT[:, :, tt_i * 128:(tt_i + 1) * 128], self.pst[:, :].rearrange("p (k n) -> p k n", n=128),
                writes=[("xT", tt_i)], excl=[("ps", 7)])

    def phase0(self):
        o = self.R_ph
        gb = self.A("lnin", [128, 2, D], F32, o); o += 8192
        xin = [self.A("xin%d" % i, [128, D], F32, o + 4096 * i) for i in range(3)]; o += 12288
        h16 = [self.A("h16a", [128, D], BF16, o), self.A("h16b", [128, D], BF16, o + 2048)]; o += 4096
        self.dma("pool", self.ident[:, :], self.d["ident"], writes=[("ident",)])
        self.dma("sp", self.c32[:, :], self.d["c32"], writes=[("c32",)])
        self._issue_w()
        if self.do_ln_in:
            self.dma("sp", gb[:, :, :], self.d["lnin"], writes=[("lnin",)])

            def load(t):
                self.dma("sp", xin[t % 3][:, :], self.d["x"][t * 128:(t + 1) * 128, :], writes=[("xin", t % 3)])
            load(0)
            load(1)
            self.ln_stats(xin[0][:, :], ("xin", 0), 0)
            for t in range(16):
                if t + 2 < 16:
                    load(t + 2)
                if t + 1 < 16:
                    self.ln_stats(xin[(t + 1) % 3][:, :], ("xin", (t + 1) % 3), t + 1)
                self.ln_apply(xin[t % 3][:, :], ("xin", t % 3), t, gb[:, 0, :], gb[:, 1, :], ("lnin",))
                self.to_xT_finish()
                self.to_xT(t, h16)
            self.to_xT_finish()
        else:
            for tt_i in range(16):
                self.dma("sp", self.xtok[:, tt_i, :], self.d["x"][tt_i * 128:(tt_i + 1) * 128, :], writes=[("xtok", tt_i)])
                self.to_xT(tt_i, h16)
            self.to_xT_finish()

    def attn_pipeline(self, units, Pb, exb, regions, scale, look, s_order=None):
        nu = len(units)
        nr = len(regions)
        nb = len(Pb)
        mulcount = [0]

        def emitS(u, extra=()):
            banks = regions[u % nr]
            offs = []
            off = 0
            for t in units[u]:
                offs.append(off)
                off += t["n"]
            order = list(range(len(units[u]))) if s_order is None or len(units[u]) != len(s_order) else list(s_order)
            for c0 in range(0, len(order), 2):
                grp = order[c0:c0 + 2]
                for i in grp:
                    t = units[u][i]
                    n = t["n"]
                    base = banks[0] * 512 + offs[i]
                    self.mm([banks[offs[i] // 512]], self.psall[:, base:base + n], t["s_lhsT"], t["s_rhs"], True,
                            t.get("b_rhs") is None, reads=list(t["s_reads"]) + list(extra))
                for i in grp:
                    t = units[u][i]
                    if t.get("b_rhs") is None:
                        continue
                    n = t["n"]
                    base = banks[0] * 512 + offs[i]
                    self.mm([banks[offs[i] // 512]], self.psall[:, base:base + n], self.ident[:, :], t["b_rhs"], False, True,
                            reads=[("ident",), t["b_res"]])

        for u in range(min(look, nu)):
            emitS(u)
        for u in range(nu):
            banks = regions[u % nr]
            k = u % nb
            tot = sum(t["n"] for t in units[u])
            base = banks[0] * 512
            src = self.psall[:, base:base + tot]
            ex = [("ps", b) for b in banks]
            if units[u][0]["etab"] is None:
                self.act(Pb[k][:, 0:tot], src, AF.Exp, scale=scale,
                         writes=[("P", k, i) for i in range(len(units[u]))], excl=ex)
            else:
                self.act(exb[k][:, 0:tot], src, AF.Exp, scale=scale, writes=[("ex", k)], excl=ex)
                off = 0
                for i, t in enumerate(units[u]):
                    n = t["n"]
                    eng = "pool" if (POOL_EVERY and mulcount[0] % POOL_EVERY == POOL_EVERY - 1) else "dve"
                    mulcount[0] += 1
                    self.tt(eng, Pb[k][:, off:off + n], exb[k][:, off:off + n], t["etab"], ALU.mult,
                            reads=[("ex", k), t["etab_res"]], writes=[("P", k, i)])
                    off += n
            if u + look < nu:
                emitS(u + look)
            off = 0
            for i, t in enumerate(units[u]):
                n = t["n"]
                ob = t["ob"]
                self.mm(ob, self.ps[ob][:, 0:n], t["v_lhsT"], Pb[k][:, off:off + n], t["first"], t["last"],
                        reads=[("P", k, i)] + t["v_reads"])
                off += n
                for _ in range(DUMMY_MM):
                    self.mm(6, self.ps[6][:, 0:n], self.ident[:, :], Pb[k][:, 0:n], True, True, reads=[("P", k, i), ("ident",)])
                if t["last"] and t["epi"] is not None:
                    t["epi"]()

    def attn_epilogue(self, ob, n, hb, rs, sz_ap, sz_res, y_ap, y_res, slot):
        so = 64 - hb
        ps = self.ps[ob]
        self.s.op("dve", lambda e: e.reciprocal(rs[hb:hb + 64, 0:n], ps[so:so + 64, 0:n]),
                  writes=[("rs", slot)], excl=[("ps", ob)])
        self.tt("pool", rs[hb:hb + 64, 0:n], rs[hb:hb + 64, 0:n], sz_ap, ALU.mult,
                reads=[("rs", slot), sz_res], writes=[("rs", slot)])
        self.tt("dve", y_ap, ps[hb:hb + 64, 0:n], rs[hb:hb + 64, 0:n], ALU.mult,
                reads=[("rs", slot)], writes=[y_res], excl=[("ps", ob)])

    def phaseA(self, l):
        o = self.R_ph
        qT = self.A("naq", [128, S], BF16, o); o += 4096
        kT = self.A("nak", [128, S], BF16, o); o += 4096
        sz = self.A("nasz", [128, S], BF16, o); o += 4096
        Va = self.A("nava", [128, 16, 2, 128], BF16, o); o += 8192
        exb = None
        Pb = [self.A("P%d" % i, [128, 1024], BF16, o + 2048 * i) for i in range(3)]; o += 6144
        rs = [[self.A("rs%d%d" % (i, h), [128, 256], F32, o + 2048 * i + 1024 * h) for h in range(2)] for i in range(2)]
        o += 4096
        bv = self.A("bvna", [128, 512], F32, o); o += 2048
        oa = self.R_y + 32768
        tabraw = self.A("tabraw", [128, 1792], F32, oa); oa += 7168
        E = [self.A("E%d" % i, [128, 1792], BF16, oa + 3584 * i) for i in range(2)]; oa += 7168
        regions = [[0, 1], [2, 3]]
        obanks = [(4, 5), (6, 7)]
        self.pbanks = [0, 1, 2, 3, 4, 5, 6, 7]
        d = self.d
        self.dma("sp", bv[:, :], d["bvb"][l, :, 0:512], writes=[("bvna",)])
        self.s.op("pool", lambda e: e.memset(Va[:, :, 0, 64:128], 1.0), writes=[("nava", j) for j in range(16)])
        self.s.op("pool", lambda e: e.memset(Va[:, :, 1, 0:64], 1.0), writes=[("nava", j) for j in range(16)])
        hcount = 0
        ecount = 0
        for c in range(4):
            (wt,), wres = self.take_w()

            def ev_q(tc, b, dst=qT, ch=CH_NA(c, 0), nm="naq"):
                self.ts("dve", dst[:, tc * 512:(tc + 1) * 512], self.ps[b][:, :], self.bias(ch), None, ALU.add,
                        reads=[("vecs",)], writes=[(nm, tc)], excl=[("ps", b)])
            self.proj_fm(wt, wres, 0, ev_q)
            self.proj_fm(wt, wres, 128, lambda tc, b: ev_q(tc, b, kT, CH_NA(c, 1), "nak"))

            def ev_z(tc, b, ch=CH_NA(c, 3)):
                self.act(sz[:, tc * 512:(tc + 1) * 512], self.ps[b][:, :], AF.Silu, bias=self.bias(ch),
                         reads=[("vecs",)], writes=[("nasz", tc)], excl=[("ps", b)])
            self.proj_fm(wt, wres, 384, ev_z)
            for j0 in range(0, 16, 4):
                b = self.bank()
                for t in range(4):
                    for kc in range(8):
                        self.mm(b, self.ps[b][:, t * 128:(t + 1) * 128], self.xT[:, kc, (j0 + t) * 128:(j0 + t + 1) * 128],
                                wt[:, kc, 256:384], kc == 0, kc == 7, reads=[wres, ("xT", j0 + t)])
                pv = self.ps[b][:, :].rearrange("p (a n) -> p a n", n=128)
                for hi in range(2):
                    bsl = bv[:, c * 128 + hi * 64:c * 128 + hi * 64 + 64].unsqueeze(1).broadcast_to([128, 4, 64])
                    self.tt("dve", Va[:, j0:j0 + 4, hi, hi * 64:hi * 64 + 64], pv[:, :, hi * 64:hi * 64 + 64], bsl, ALU.add,
                            reads=[("bvna",)], writes=[("nava", j0 + t) for t in range(4)], excl=[("ps", b)])
            units = []
            eks = []
            for hi in range(2):
                ek = ecount % 2
                ecount += 1
                eks.append(ek)
                self.dma("sp", tabraw[:, :], d["natab"][l, 2 * c + hi], writes=[("tabraw",)])
                self.act(E[ek][:, :], tabraw[:, :], AF.Identity, scale=8.0, reads=[("tabraw",)], writes=[("E", ek)])
            for q8 in range(8):
                r0 = 4 * q8
                if q8 == 0:
                    jl, kind = [0, 1, 2, 3], 0
                elif q8 == 7:
                    jl, kind = [12, 13, 14, 15], 0
                else:
                    jl, kind = list(range(2 * q8 - 2, 2 * q8 + 4)), 1
                obs = obanks[hcount % 2]
                slot = hcount % 2
                hcount += 1
                q0 = q8 * 256
                tcq = q0 // 512
                tl = {0: [], 1: []}
                for hi in range(2):
                    hb = 64 * hi
                    ek = eks[hi]
                    ob = obs[hi]

                    def epi(ob=ob, hb=hb, slot=slot, hi=hi, q0=q0, tcq=tcq, c=c):
                        self.attn_epilogue(ob, 256, hb, rs[slot][hi], sz[hb:hb + 64, q0:q0 + 256], ("nasz", tcq),
                                           self.yA[hb:hb + 64, c, q0:q0 + 256], ("yA", c, tcq), (slot, hi))
                    for idx, j in enumerate(jl):
                        s0 = r0 - 2 * j + 8
                        assert 2 <= s0 and s0 + 4 <= 16
                        tl[hi].append(dict(
                            n=256, s_lhsT=kT[hb:hb + 64, j * 128:(j + 1) * 128], s_rhs=qT[hb:hb + 64, q0:q0 + 256],
                            s_reads=[("nak", j // 4), ("naq", tcq)],
                            etab=None, etab_res=None,
                            b_rhs=E[ek][:, kind * 896 + (s0 - 2) * 64:kind * 896 + (s0 + 2) * 64], b_res=("E", ek),
                            v_lhsT=Va[:, j, hi, :], v_reads=[("nava", j)], ob=ob,
                            first=(idx == 0), last=(idx == len(jl) - 1), epi=epi if idx == len(jl) - 1 else None))
                for i in range(0, len(jl), 2):
                    units.append([tl[0][i], tl[0][i + 1], tl[1][i], tl[1][i + 1]])
            self.attn_pipeline(units, Pb, exb, regions, 0.125, look=2, s_order=[0, 2, 1, 3])
        if self.dbg:
            self.dbg_dump("yA%d" % l, self.yA, [("yA", c, t) for c in range(4) for t in range(4)])

    def dbg_dump(self, name, t, reads):
        shp = list(t.shape)
        dt = t.dtype
        dd = self.nc.dram_tensor("dbg_" + name, shp, dt, kind="ExternalOutput").ap()
        self.dbg_d[name] = dd
        self.dma("sp", dd, t[tuple(slice(None) for _ in shp)], reads=reads)

    def barrier(self):
        for e in Sched.ENGS:
            self.s.wait_all(e, dmas=False)

    def phaseB(self, l):
        o = self.R_ph
        kT2 = self.A("gk", [128, 2, S], BF16, o); o += 8192
        qT = self.A("gq", [128, S], BF16, o); o += 4096
        sz = self.A("gsz", [128, S], BF16, o); o += 4096
        ropeb = [self.A("rope", [128, 2, 512], F32, o)] * 2; o += 4096
        sq2 = [self.A("sq%d" % i, [128, 512], F32, o + 2048 * i) for i in range(2)]; o += 4096
        qb2 = [self.A("qb%d" % i, [128, 512], F32, o + 2048 * i) for i in range(2)]; o += 4096
        u2 = [self.A("u%d" % i, [128, 512], F32, o + 2048 * i) for i in range(2)]; o += 4096
        rstd2 = [self.A("rstd", [128, 512], F32, o)] * 2; o += 2048
        t12 = [self.A("t1", [128, 512], F32, o)] * 2; o += 2048
        bv = self.A("bvg", [128, 128], F32, o); o += 512
        oa = self.R_y + 32768
        Vg = self.A("gv", [128, 16, 2, 192], BF16, oa); oa += 12288
        rs = [[self.A("grs%d%d" % (i, h), [128, 256], F32, oa + 2048 * i + 1024 * h) for h in range(2)] for i in range(2)]
        oa += 4096
        Pb = [self.A("gP%d" % i, [128, 1024], BF16, o + 2048 * i) for i in range(3)]; o += 6144
        assert o <= SB_END, o
        regions = [[0, 1], [2, 3]]
        obanks = [(4, 5), (6, 7)]
        self.pbanks = [0, 1, 2, 3, 4, 5, 6, 7]
        d = self.d
        c32 = self.c32
        self.dma("sp", bv[:, :], d["bvb"][l, :, 512:640], writes=[("bvg",)])
        self.s.op("pool", lambda e: e.memset(Vg[:, :, :, 0:64], 1.0), writes=[("gv", j) for j in range(16)])
        self.s.op("pool", lambda e: e.memset(Vg[:, :, :, 128:192], 1.0), writes=[("gv", j) for j in range(16)])
        self.rope_i = 0

        def rope_unit(tc, b, bias_ch, gain_col, dst_ap, dst_res):
            rk = self.rope_i % 2
            self.rope_i += 1
            sq, qb, u, rstd, t1, t2 = sq2[rk], qb2[rk], u2[rk], rstd2[rk], t12[rk], sq2[rk]
            R = lambda nm: ("sq", rk) if nm == "t2" else (nm, rk if nm in ("qb", "sq", "u") else 0)
            self.dma("sp", ropeb[rk][:, :, :], d["rope"][:, :, tc * 512:(tc + 1) * 512], writes=[("rope", 0)])
            ps = self.ps[b]
            self.act(sq[:, :], ps[:, :], AF.Square, bias=self.bias(bias_ch), reads=[("vecs",)], writes=[R("sq")],
                     excl=[("ps", b)])
            self.ts("dve", u[:, :], ps[:, :], self.bias(bias_ch), self.vecs[:, gain_col:gain_col + 1], ALU.add, ALU.mult,
                    reads=[("vecs",)], writes=[R("u")], excl=[("ps", b)])
            b2 = self.bank()
            self.mm(b2, self.ps[b2][:, :], c32[:, 128:256], sq[:, :], True, True, reads=[R("sq"), ("c32",)])
            b3 = self.bank()
            self.mm(b3, self.ps[b3][:, :], c32[:, 0:128], u[:, :], True, True, reads=[R("u"), ("c32",)])
            self.act(rstd[:, :], self.ps[b2][:, :], AF.Ln, bias=64e-6, writes=[R("rstd")], excl=[("ps", b2)])
            self.act(rstd[:, :], rstd[:, :], AF.Exp, scale=-0.5, reads=[R("rstd")], writes=[R("rstd")])
            self.tt("pool", t1[:, :], u[:, :], ropeb[rk][:, 0, :], ALU.mult, reads=[R("u"), ("rope", 0)], writes=[R("t1")])
            self.tt("dve", t2[:, :], self.ps[b3][:, :], ropeb[rk][:, 1, :], ALU.mult, reads=[("rope", 0)],
                    writes=[R("t2")], excl=[("ps", b3)])
            self.tt("dve", t1[:, :], t1[:, :], t2[:, :], ALU.add, reads=[R("t1"), R("t2")], writes=[R("t1")])
            self.tt("dve", dst_ap, t1[:, :], rstd[:, :], ALU.mult, reads=[R("t1"), R("rstd")], writes=[dst_res])

        (wt,), wres = self.take_w()
        for g in range(2):
            self.proj_fm(wt, wres, g * 128,
                         lambda tc, b, g=g: rope_unit(tc, b, CH_GK[g], 65, kT2[:, g, tc * 512:(tc + 1) * 512], ("gk", g, tc)))
        for j0 in range(0, 16, 4):
            b = self.bank()
            for t in range(4):
                for kc in range(8):
                    self.mm(b, self.ps[b][:, t * 128:(t + 1) * 128], self.xT[:, kc, (j0 + t) * 128:(j0 + t + 1) * 128],
                            wt[:, kc, 256:384], kc == 0, kc == 7, reads=[wres, ("xT", j0 + t)])
            pv = self.ps[b][:, :].rearrange("p (a n) -> p a n", n=128)
            for g in range(2):
                bsl = bv[:, g * 64:g * 64 + 64].unsqueeze(1).broadcast_to([128, 4, 64])
                self.tt("dve", Vg[:, j0:j0 + 4, g, 64:128], pv[:, :, g * 64:g * 64 + 64], bsl, ALU.add,
                        reads=[("bvg",)], writes=[("gv", j0 + t) for t in range(4)], excl=[("ps", b)])
        hcount = 0
        for c in range(4):
            g = c // 2
            (wt,), wres = self.take_w()
            self.proj_fm(wt, wres, 0,
                         lambda tc, b: rope_unit(tc, b, CH_GQ(c, 0), 64, qT[:, tc * 512:(tc + 1) * 512], ("gq", tc)))

            def ev_z(tc, b, ch=CH_GQ(c, 1)):
                self.act(sz[:, tc * 512:(tc + 1) * 512], self.ps[b][:, :], AF.Silu, bias=self.bias(ch),
                         reads=[("vecs",)], writes=[("gsz", tc)], excl=[("ps", b)])
            self.proj_fm(wt, wres, 128, ev_z)
            units = []
            for q8 in range(8):
                q0 = q8 * 256
                tcq = q0 // 512
                obs = obanks[hcount % 2]
                slot = hcount % 2
                hcount += 1
                tl = {0: [], 1: []}
                for hi in range(2):
                    hb = 64 * hi
                    ob = obs[hi]

                    def epi(ob=ob, hb=hb, slot=slot, hi=hi, q0=q0, tcq=tcq, c=c):
                        self.attn_epilogue(ob, 256, hb, rs[slot][hi], sz[hb:hb + 64, q0:q0 + 256], ("gsz", tcq),
                                           self.yB[hb:hb + 64, c, q0:q0 + 256], ("yB", c, tcq), (slot, hi))
                    for j in range(16):
                        vl = Vg[:, j, g, 64:192] if hi == 0 else Vg[:, j, g, 0:128]
                        tl[hi].append(dict(
                            n=256, s_lhsT=kT2[hb:hb + 64, g, j * 128:(j + 1) * 128], s_rhs=qT[hb:hb + 64, q0:q0 + 256],
                            s_reads=[("gk", g, j // 4), ("gq", tcq)], etab=None, etab_res=None,
                            v_lhsT=vl, v_reads=[("gv", j)], ob=ob, first=(j == 0), last=(j == 15),
                            epi=epi if j == 15 else None))
                for i in range(0, 16, 2):
                    units.append([tl[0][i], tl[0][i + 1], tl[1][i], tl[1][i + 1]])
            self.attn_pipeline(units, Pb, None, regions, 8.0, look=2, s_order=[0, 2, 1, 3])
        if self.dbg:
            self.dbg_dump("yB%d" % l, self.yB, [("yB", c, t) for c in range(4) for t in range(4)])

    def phaseC(self, l):
        o = self.R_ph
        vn = self.A("vn", [128, 16, 512], BF16, o); o += 16384
        sgw = self.A("sgw", [128, 8, 128], BF16, o); o += 2048
        vraw = [self.A("vraw%d" % i, [128, 512], F32, o + 2048 * i) for i in range(2)]; o += 4096
        sgln = self.A("sgln", [128, 2, 512], F32, o); o += 4096
        bvs = self.A("bvs", [128, 512], F32, o); o += 2048
        uzb = [self.A("uz%d" % i, [128, 512], F32, o + 2048 * i) for i in range(2)]; o += 4096
        szt = [self.A("szt%d" % i, [128, 512], F32, o + 2048 * i) for i in range(2)]; o += 4096
        sgb = [self.A("sgb%d" % i, [128, 512], F32, o + 2048 * i) for i in range(2)]; o += 4096
        mix = [self.A("mix%d" % i, [128, 512], F32, o + 2048 * i) for i in range(2)]; o += 4096
        assert o <= SB_END, o
        self.pbanks = [0, 1, 2, 3, 4, 5, 6]
        d = self.d
        st = self.stat
        self.dma("sp", bvs[:, :], d["bvb"][l, :, 640:1152], writes=[("bvs",)])
        self.dma("sp", sgln[:, :, :], d["sgln"][l], writes=[("sgln",)])
        self.dma("pool", sgw[:, :, :], d["sgwT"][l], writes=[("sgw",)])
        (wt,), wres = self.take_w()
        for tt_i in range(16):
            b = self.bank()
            k = tt_i % 2
            for kc in range(8):
                self.mm(b, self.ps[b][:, :], self.xT[:, kc, tt_i * 128:(tt_i + 1) * 128], wt[:, kc, 0:512],
                        kc == 0, kc == 7, reads=[wres, ("xT", tt_i)])
            vr = vraw[k]
            self.tt("dve", vr[:, :], self.ps[b][:, :], bvs[:, :], ALU.add, reads=[("bvs",)], writes=[("vraw", k)],
                    excl=[("ps", b)])
            sk = 16 * k
            self.s.op("dve", (lambda vr, sk: lambda e: e.bn_stats(st[:, sk:sk + 6], vr[:, :]))(vr, sk),
                      reads=[("vraw", k)], writes=[("stat", k)])
            self.s.op("dve", (lambda sk: lambda e: e.bn_aggr(st[:, sk + 12:sk + 14], st[:, sk:sk + 6]))(sk),
                      reads=[("stat", k)], writes=[("stat", k)])
            self.act(st[:, sk + 14:sk + 15], st[:, sk + 13:sk + 14], AF.Sqrt, bias=1e-5,
                     reads=[("stat", k)], writes=[("stat", k)])
            self.s.op("dve", (lambda sk: lambda e: e.reciprocal(st[:, sk + 14:sk + 15], st[:, sk + 14:sk + 15]))(sk),
                      reads=[("stat", k)], writes=[("stat", k)])
            self.stt("dve", vr[:, :], vr[:, :], st[:, sk + 12:sk + 13], sgln[:, 0, :], ALU.subtract, ALU.mult,
                     reads=[("vraw", k), ("stat", k), ("sgln",)], writes=[("vraw", k)])
            self.stt("dve", vn[:, tt_i, :], vr[:, :], st[:, sk + 14:sk + 15], sgln[:, 1, :], ALU.mult, ALU.add,
                     reads=[("vraw", k), ("stat", k), ("sgln",)], writes=[("vn", tt_i)])
        cnt = 0
        for c in range(4):
            (wt,), wres = self.take_w()
            self.dma("sp", sgb[c % 2][:, :], d["sgb"][l, :, c, :], writes=[("sgb", c % 2)])
            for tc in range(4):
                bu = self.bank()
                for kc in range(8):
                    self.mm(bu, self.ps[bu][:, :], wt[:, kc, 0:128], self.xT[:, kc, tc * 512:(tc + 1) * 512],
                            kc == 0, kc == 7, reads=[wres] + self.xT_res(tc))
                bz = self.bank()
                for kc in range(8):
                    self.mm(bz, self.ps[bz][:, :], wt[:, kc, 128:256], self.xT[:, kc, tc * 512:(tc + 1) * 512],
                            kc == 0, kc == 7, reads=[wres] + self.xT_res(tc))
                k = cnt % 2
                cnt += 1
                self.act(szt[k][:, :], self.ps[bz][:, :], AF.Silu, bias=self.bias(CH_SG(c, 1)), reads=[("vecs",)],
                         writes=[("szt", k)], excl=[("ps", bz)])
                uzs = uzb[k][:, :]
                self.stt("dve", uzs, self.ps[bu][:, :], self.bias(CH_SG(c, 0)), szt[k][:, :], ALU.add, ALU.mult,
                         reads=[("vecs",), ("szt", k)], writes=[("uz", k)], excl=[("ps", bu)])
                bm = [self.bank(), self.bank()]
                for gi in range(2):
                    for t in range(4):
                        tt_i = 4 * tc + t
                        self.mm(bm[gi], self.ps[bm[gi]][:, t * 128:(t + 1) * 128], vn[:, tt_i, c * 128:(c + 1) * 128],
                                sgw[:, 2 * c + gi, :], True, True, reads=[("vn", tt_i), ("sgw",)])
                mk = mix[k]
                for gi in range(2):
                    hb = 64 * gi
                    self.tt("dve", mk[hb:hb + 64, :], self.ps[bm[gi]][hb:hb + 64, :], sgb[c % 2][hb:hb + 64, :], ALU.add,
                            reads=[("sgb", c % 2)], writes=[("mix", k)], excl=[("ps", bm[gi])])
                self.tt("pool", self.yC[:, c, tc * 512:(tc + 1) * 512], mk[:, :], uzs, ALU.mult,
                        reads=[("mix", k), ("uz", k)], writes=[("yC", c, tc)])
        if self.dbg:
            self.dbg_dump("yC%d" % l, self.yC, [("yC", c, t) for c in range(4) for t in range(4)])

    def phaseD(self, l, last):
        o = self.R_ph
        mT = self.A("mT", [128, 8, S], BF16, o); o += 32768
        gsig = [self.A("gsig%d" % i, [128, 512], F32, o + 2048 * i) for i in range(2)]; o += 4096
        mt = [self.A("mt%d" % i, [128, 512], F32, o + 2048 * i) for i in range(3)]; o += 6144
        assert o <= SB_END, o
        self.pbanks = [0, 1, 2, 3, 4, 5, 6]
        d = self.d
        gcnt = 0
        for dc in range(8):
            (wg, wb), wres = self.take_w()
            for tc in range(4):
                for br in range(3):
                    bg = self.bank()
                    for kc in range(8):
                        self.mm(bg, self.ps[bg][:, :], wg[:, kc, br * 128:(br + 1) * 128],
                                self.xT[:, kc, tc * 512:(tc + 1) * 512], kc == 0, kc == 7, reads=[wres] + self.xT_res(tc))
                    k = gcnt % 2
                    gcnt += 1
                    self.act(gsig[k][:, :], self.ps[bg][:, :], AF.Sigmoid, bias=self.bias(CH_GATE(dc, br)),
                             reads=[("vecs",)], writes=[("gsig", k)], excl=[("ps", bg)])
                    bp = self.bank()
                    yb = self.ybr[br]
                    nm = ("yA", "yB", "yC")[br]
                    for kc in range(4):
                        self.mm(bp, self.ps[bp][:, :], wb[:, kc, br * 128:(br + 1) * 128], yb[:, kc, tc * 512:(tc + 1) * 512],
                                kc == 0, kc == 3, reads=[wres, (nm, kc, tc)])
                    mb = mt[0] if br == 0 else mt[1]
                    mk = ("mt", 0 if br == 0 else 1)
                    self.tt("dve", mb[:, :], self.ps[bp][:, :], gsig[k][:, :], ALU.mult, reads=[("gsig", k)],
                            writes=[mk], excl=[("ps", bp)])
                    if br == 1:
                        self.tt("dve", mt[0][:, :], mt[0][:, :], mt[1][:, :], ALU.add, reads=[("mt", 0), ("mt", 1)],
                                writes=[("mt", 0)])
                    elif br == 2:
                        self.tt("dve", mT[:, dc, tc * 512:(tc + 1) * 512], mt[0][:, :], mt[1][:, :], ALU.add,
                                reads=[("mt", 0), ("mt", 1)], writes=[("mT", tc)])
        self.barrier()
        oa = self.R_y
        post = self.A("post", [128, 3, D], F32, oa); oa += 12288
        h16 = [self.A("dh16%d" % i, [128, D], BF16, oa + 2048 * i) for i in range(2)]; oa += 4096
        self.dma("sp", post[:, :, :], d["post"][l], writes=[("post",)])
        for hf in range(2):
            (wo,), wres = self.take_w()
            for tt_i in range(16):
                tcq = tt_i // 4
                b = self.bank()
                for dc in range(8):
                    self.mm(b, self.ps[b][:, :], mT[:, dc, tt_i * 128:(tt_i + 1) * 128], wo[:, dc, 0:512],
                            dc == 0, dc == 7, reads=[("mT", tcq), wres])
                x_ap = self.xtok[:, tt_i, hf * 512:(hf + 1) * 512]
                self.stt("dve", x_ap, x_ap, float(ALPHA), self.ps[b][:, :], ALU.mult, ALU.add,
                         reads=[("xtok", tt_i)], writes=[("xtok", tt_i)], excl=[("ps", b)])
                if hf == 1:
                    self.tt("pool", self.xtok[:, tt_i, :], self.xtok[:, tt_i, :], post[:, 0, :], ALU.add,
                            reads=[("xtok", tt_i), ("post",)], writes=[("xtok", tt_i)])
        self.ln_stats(self.xtok[:, 0, :], ("xtok", 0), 0)
        for t in range(16):
            if t + 1 < 16:
                self.ln_stats(self.xtok[:, t + 1, :], ("xtok", t + 1), t + 1)
            self.ln_apply(self.xtok[:, t, :], ("xtok", t), t, post[:, 1, :], post[:, 2, :], ("post",))
            if last:
                self.dma("sp", self.y_d[t * 128:(t + 1) * 128, :], self.xtok[:, t, :], reads=[("xtok", t)])
            else:
                self.to_xT_finish()
                self.to_xT(t, h16)
        self.to_xT_finish()
        self.barrier()

    def layer(self, l, last):
        self.dma("sp", self.vecs[:, :], self.d["vecs"][l], writes=[("vecs",)])
        self.phaseA(l)
        self.barrier()
        self.phaseB(l)
        self.barrier()
        self.phaseC(l)
        self.barrier()
        self.phaseD(l, last)

    def build(self):
        from contextlib import ExitStack
        nc = self.nc
        self.phase0()
        self.barrier()
        for i, l in enumerate(self.layers):
            self.layer(l, last=(i == len(self.layers) - 1))
        self.s.wait_all("sp")
        with ExitStack() as es:
            sems = {e: es.enter_context(nc.semaphore("s_" + e)) for e in Sched.ENGS}
            dsems = [es.enter_context(nc.semaphore("q%d" % i)) for i in range(self.s.n_dma)]
            block = es.enter_context(nc.Block())
            self.s.emit(nc, block, sems, dsems)
        return nc


def _run(layers, do_ln_in, x_list, shared, dbg=False):
    kb = KB(layers, do_ln_in, dbg)
    nc = kb.build()
    in_maps = []
    for xb in x_list:
        m = dict(shared)
        m["x"] = np.ascontiguousarray(xb, dtype=np.float32)
        in_maps.append(m)
    return run_bass_kernel_spmd(nc, in_maps, core_ids=list(range(len(x_list))))


def kernel(**inputs):
    x = np.asarray(inputs["x"], dtype=np.float32)
    shared = prep_inputs(**inputs)
    res = _run([0, 1], True, [x[b] for b in range(x.shape[0])], shared)
    return np.stack([r["y"] for r in res.results], axis=0).astype(np.float32)
```

```python
import numpy as np
import concourse.bass as bass
import concourse.mybir as mybir
from concourse.bass_utils import run_bass_kernel_spmd

F32, BF16 = mybir.dt.float32, mybir.dt.bfloat16
AF = mybir.ActivationFunctionType
ALU = mybir.AluOpType

S = 2048
D = 1024
L = 2
NCH = 63
WIN = NCH * 128
ALPHA = (2.0 * L) ** 0.25
SB_BASE = 16512
SB_END = 229376
EMBED_WAIT = ("act", "dve", "pool")


class Sched:
    ENGS = ("pe", "act", "dve", "pool", "sp")

    def __init__(self, n_dma_sems=24):
        self.prog = {e: [] for e in self.ENGS}
        self.serial = {e: 0 for e in self.ENGS}
        self.seen = {e: {} for e in self.ENGS}
        self.lastw = {}
        self.readers = {}
        self.lastx = {}
        self.waited = {e: set() for e in self.ENGS}
        self.n_dma = n_dma_sems
        self.dma_i = 0
        self.dma_ip = 0
        self.dma_val = [0] * n_dma_sems

    def _deps(self, eng, reads, writes, excl):
        need = {}

        def add(tok, raw):
            key, val, teng = tok
            if teng == eng and not raw:
                return
            if self.seen[eng].get(key, 0) >= val:
                return
            if need.get(key, 0) < val:
                need[key] = val

        for r in reads:
            t = self.lastw.get(r)
            if t:
                add(t, True)
        for w in writes:
            t = self.lastw.get(w)
            if t:
                add(t, False)
            for t in self.readers.get(w, {}).values():
                add(t, False)
        for x in excl:
            t = self.lastx.get(x)
            if t:
                add(t, False)
        for key, val in need.items():
            self.seen[eng][key] = val
            self.prog[eng].append(("wait", key, val))
            if key[0] != "q":
                self.waited[key].add(val)

    def _commit(self, tok, reads, writes, excl):
        for r in reads:
            self.readers.setdefault(r, {})[tok[0]] = tok
        for w in writes:
            self.lastw[w] = tok
            self.readers[w] = {}
        for x in excl:
            self.lastx[x] = tok

    def op(self, eng, fn, reads=(), writes=(), excl=()):
        self._deps(eng, reads, writes, excl)
        self.serial[eng] += 1
        tok = (eng, self.serial[eng], eng)
        self.prog[eng].append(("op", fn, self.serial[eng]))
        self._commit(tok, reads, writes, excl)
        return tok

    def dma(self, eng, fn, reads=(), writes=()):
        self._deps(eng, reads, writes, ())
        if eng == "pool":
            k = 16 + self.dma_ip % (self.n_dma - 16)
            self.dma_ip += 1
        else:
            k = self.dma_i % 16
            self.dma_i += 1
        key = "q%d" % k
        prev = self.dma_val[k]
        if prev > 0 and self.seen[eng].get(key, 0) < prev:
            self.seen[eng][key] = prev
            self.prog[eng].append(("wait", key, prev))
        self.dma_val[k] += 16
        tok = (key, self.dma_val[k], None)
        self.prog[eng].append(("dma", fn, k))
        self._commit(tok, reads, writes, ())
        return tok

    def wait_all(self, eng, dmas=True):
        for e in self.ENGS:
            if e != eng and self.serial[e] > 0 and self.seen[eng].get(e, 0) < self.serial[e]:
                self.seen[eng][e] = self.serial[e]
                self.prog[eng].append(("wait", e, self.serial[e]))
                self.waited[e].add(self.serial[e])
        for k in range(self.n_dma if dmas else 0):
            key = "q%d" % k
            if self.dma_val[k] > 0 and self.seen[eng].get(key, 0) < self.dma_val[k]:
                self.seen[eng][key] = self.dma_val[k]
                self.prog[eng].append(("wait", key, self.dma_val[k]))

    def emit(self, nc, block, sems, dsems):
        rank = {}
        for e in self.ENGS:
            ws = sorted(self.waited[e])
            rank[e] = {s: i + 1 for i, s in enumerate(ws)}
        handles = {"pe": "tensor", "act": "scalar", "dve": "vector", "pool": "gpsimd", "sp": "sync"}

        def run(ename):
            def body(eng):
                pend = []

                def semval(key, val):
                    if key[0] == "q":
                        return dsems[int(key[1:])], val
                    return sems[key], rank[key][val]

                for item in self.prog[ename]:
                    if item[0] == "wait":
                        pend.append(semval(item[1], item[2]))
                        continue
                    embed = None
                    if pend and item[0] == "op" and ename in EMBED_WAIT:
                        embed = pend.pop()
                    for (sm, v) in pend:
                        eng.wait_ge(sm, v)
                    pend = []
                    ins = item[1](eng)
                    if embed is not None:
                        ins._wait_ge(embed[0], embed[1])
                    if item[0] == "op":
                        if item[2] in rank[ename]:
                            ins.then_inc(sems[ename], 1)
                    else:
                        ins.then_inc(dsems[item[2]], 16)
                for (sm, v) in pend:
                    eng.wait_ge(sm, v)
            return body

        for ename in self.ENGS:
            getattr(block, handles[ename])(run(ename))


def _win_cols():
    naq, nak, nav, naz, gqq, gqk, gqv, gqz, sgu, sgv, sgz, gat = (0, 512, 1024, 1536, 2048, 2560, 2688, 2816,
                                                                 3328, 3840, 4352, 4864)
    r = lambda a, n: list(range(a, a + n))
    cols = []
    for c in range(4):
        cols += r(naq + c * 128, 128) + r(nak + c * 128, 128) + r(nav + c * 128, 128) + r(naz + c * 128, 128)
    cols += r(gqk, 64) + r(gqk, 64) + r(gqk + 64, 64) + r(gqk + 64, 64) + r(gqv, 128)
    for c in range(4):
        cols += r(gqq + c * 128, 128) + r(gqz + c * 128, 128)
    cols += r(sgv, 512)
    for c in range(4):
        cols += r(sgu + c * 128, 128) + r(sgz + c * 128, 128)
    for dc in range(8):
        for b in range(3):
            cols += r(gat + b * 1024 + dc * 128, 128)
    assert len(cols) == WIN
    return np.asarray(cols)


def CH_NA(c, which):
    return 4 * c + which
CH_GK = (16, 17)
CH_GV = 18
def CH_GQ(c, which):
    return 19 + 2 * c + which
CH_SV = 27
def CH_SG(c, which):
    return 31 + 2 * c + which
def CH_GATE(dc, b):
    return 39 + 3 * dc + b


def _na_index_tables():
    p = np.arange(128)
    ck = p % 64
    half = p // 64
    s = np.arange(2, 16)
    cq = np.arange(64)
    dr = (8 - s)[None, :, None] + half[:, None, None] + 0 * cq[None, None, :]
    dcol = np.clip(ck[:, None, None] - cq[None, None, :] + 15, 0, 30) + 0 * s[None, :, None]
    cs = np.clip(cq - 8, 0, 48)
    colv = (ck[:, None, None] >= cs[None, None, :]) & (ck[:, None, None] < cs[None, None, :] + 16)
    colv = colv & (s[None, :, None] > -100)
    full_ok = colv & (np.abs(dr) <= 7)
    int_ok = colv & (dr >= -4) & (dr <= 3)
    ridx = np.clip(dr + 7, 0, 14)
    return ridx, dcol, full_ok, int_ok


def _rope_tables():
    t = np.arange(S)
    row = (t // 64).astype(np.float64)
    col = (t % 64).astype(np.float64)
    freqs = 10000.0 ** (-np.arange(0, 32, 2, dtype=np.float64) / 32.0)
    ang = np.concatenate([row[:, None] * freqs, col[:, None] * freqs], axis=-1)
    cos = np.cos(ang)
    sin = np.sin(ang)
    p = np.arange(128)
    d = p % 64
    C = cos[:, d // 2].T
    Sg = (sin[:, d // 2] * np.where(d % 2 == 0, -1.0, 1.0)[None, :]).T
    return np.ascontiguousarray(np.stack([C, Sg], axis=1)).astype(np.float32)


def _consts32():
    perm = np.zeros((128, 128), np.float32)
    for i in range(64):
        perm[2 * i + 1, 2 * i] = 1.0
        perm[2 * i, 2 * i + 1] = 1.0
    k = np.arange(128)
    bones = (k[:, None] // 64 == k[None, :] // 64).astype(np.float32)
    return np.concatenate([perm, bones], axis=1)


def prep_inputs(x, ln_in_g, ln_in_b, w_in, b_in, na_rpb, q_norm_g, k_norm_g, sg_ln_g, sg_ln_b, sg_w, sg_b,
                w_br_a, w_br_b, w_br_c, w_out, b_out, ln_post_g, ln_post_b):
    f = lambda a: np.ascontiguousarray(np.asarray(a, dtype=np.float32))
    cols = _win_cols()
    shared = {}
    shared["w_in_p"] = f(np.take(w_in, cols, axis=2))
    bp = np.take(b_in, cols, axis=1)
    vecs = np.zeros((L, 128, 80), np.float32)
    vecs[:, :, 0:NCH] = bp.reshape(L, NCH, 128).transpose(0, 2, 1)
    vecs[:, :, 64] = np.tile(q_norm_g, (1, 2))
    vecs[:, :, 65] = np.tile(k_norm_g, (1, 2))
    shared["vecs"] = vecs
    bv = np.concatenate([b_in[:, 1024:1536], b_in[:, 2688:2816], b_in[:, 3840:4352]], axis=1)
    shared["bvb"] = f(np.broadcast_to(bv[:, None, :], (L, 128, 1152)))
    shared["lnin"] = f(np.broadcast_to(np.stack([ln_in_g, ln_in_b])[None], (128, 2, D)))
    ridx, dcol, full_ok, int_ok = _na_index_tables()
    rp = np.asarray(na_rpb, np.float32)
    g = rp[:, :, ridx, dcol]
    neg = np.float32(-1e30)
    tfull = np.where(full_ok[None, None], g, neg)
    tint = np.where(int_ok[None, None], g, neg)
    shared["natab"] = f(np.stack([tfull, tint], axis=3).reshape(L, 8, 128, 2 * 896))
    shared["rope"] = _rope_tables()
    shared["c32"] = _consts32()
    shared["ident"] = np.eye(128, dtype=np.float32)
    shared["sgln"] = f(np.broadcast_to(np.stack([sg_ln_g, sg_ln_b], axis=1)[:, None], (L, 128, 2, 512)))
    shared["sgwT"] = f(np.transpose(sg_w, (0, 3, 1, 2)))
    sb = np.asarray(sg_b, np.float32)
    sbb = sb.reshape(L, 4, 2, 1, 128)
    sbb = np.broadcast_to(sbb, (L, 4, 2, 64, 128)).reshape(L, 4, 128, 128)
    sbb = np.broadcast_to(sbb[:, :, :, None, :], (L, 4, 128, 4, 128)).reshape(L, 4, 128, 512)
    shared["sgb"] = f(sbb.transpose(0, 2, 1, 3))
    wf = np.stack([w_br_a, w_br_b, w_br_c], axis=2)
    wf = wf.reshape(L, 512, 3, 8, 128).transpose(0, 1, 3, 2, 4).reshape(L, 512, 3072)
    shared["wfin"] = f(wf)
    shared["wout"] = f(w_out)
    shared["post"] = f(np.broadcast_to(np.stack([b_out, ln_post_g, ln_post_b], axis=1)[:, None], (L, 128, 3, D)))
    return shared


DUMMY_MM = 0
POOL_EVERY = 0


class KB:
    def __init__(self, layers, do_ln_in, dbg=False):
        self.layers = list(layers)
        self.do_ln_in = do_ln_in
        self.dbg = dbg
        self.s = Sched()
        nc = self.nc = bass.Bass("TRN2", target_bir_lowering=False)
        di = lambda n, shp: nc.dram_tensor(n, shp, F32, kind="ExternalInput").ap()
        self.d = dict(
            x=di("x", [S, D]), lnin=di("lnin", [128, 2, D]), w_in_p=di("w_in_p", [L, D, WIN]),
            vecs=di("vecs", [L, 128, 80]), bvb=di("bvb", [L, 128, 1152]), natab=di("natab", [L, 8, 128, 1792]),
            rope=di("rope", [128, 2, S]), c32=di("c32", [128, 256]), ident=di("ident", [128, 128]),
            sgln=di("sgln", [L, 128, 2, 512]), sgwT=di("sgwT", [L, 128, 8, 128]), sgb=di("sgb", [L, 128, 4, 512]),
            wfin=di("wfin", [L, 512, 3072]), wout=di("wout", [L, D, D]), post=di("post", [L, 128, 3, D]),
        )
        self.y_d = nc.dram_tensor("y", [S, D], F32, kind="ExternalOutput").ap()
        self.dbg_d = {}
        self._nm = 0
        o = SB_BASE
        self.xtok = self.A("xtok", [128, 16, D], F32, o); o += 65536
        self.xT = self.A("xT", [128, 8, S], BF16, o); o += 32768
        self.R_y = o
        self.yA = self.A("yA", [128, 4, S], BF16, o); o += 16384
        self.yB = self.A("yB", [128, 4, S], BF16, o); o += 16384
        self.yC = self.A("yC", [128, 4, S], BF16, o); o += 16384
        self.ybr = [self.yA, self.yB, self.yC]
        self.ident = self.A("ident", [128, 128], BF16, o); o += 256
        self.c32 = self.A("c32", [128, 256], F32, o); o += 1024
        self.vecs = self.A("vecs", [128, 80], F32, o); o += 320
        self.stat = self.A("stat", [128, 64], F32, o); o += 256
        self.R_loc = o
        self.wslot = [self.A("wslot0", [128, 4608], BF16, o), self.A("wslot1", [128, 4608], BF16, o + 9216)]
        o += 18432
        self.R_ph = o
        self.psall = nc.alloc_psum_tensor("psall", [128, 4096], F32)
        self.ps = [self.psall[:, b * 512:(b + 1) * 512] for b in range(8)]
        self.pst = self.psall[:, 3584:4096].bitcast(BF16)
        self.pbanks = [0, 1, 2, 3, 4, 5, 6]
        self.pb_i = 0
        self.wjobs = []
        for l in self.layers:
            self.wjobs += self.layer_wjobs(l)
        self.w_issued = 0
        self.w_used = 0
        self._xT_pending = None

    def A(self, name, shape, dtype, off):
        nbytes = int(np.prod(shape[1:])) * (4 if dtype == F32 else 2)
        assert off + nbytes <= SB_END, (name, off, nbytes)
        self._nm += 1
        return self.nc.alloc_sbuf_tensor_at("%s_%d" % (name, self._nm), list(shape), dtype, offset=off)

    def bank(self):
        b = self.pbanks[self.pb_i % len(self.pbanks)]
        self.pb_i += 1
        return b

    def mm(self, banks, out, lhsT, rhs, start, stop, reads):
        if isinstance(banks, int):
            banks = [banks]
        self.s.op("pe", lambda e: e.matmul(out, lhsT, rhs, start=start, stop=stop), reads=reads,
                  excl=[("ps", b) for b in banks])

    def act(self, out, in_, func, reads=(), writes=(), excl=(), **kw):
        self.s.op("act", lambda e: e.activation(out, in_, func, **kw), reads=reads, writes=writes, excl=excl)

    def tt(self, eng, out, in0, in1, op, reads=(), writes=(), excl=()):
        self.s.op(eng, lambda e: e.tensor_tensor(out, in0, in1, op), reads=reads, writes=writes, excl=excl)

    def ts(self, eng, out, in0, s1, s2, op0, op1=None, reads=(), writes=(), excl=()):
        if op1 is None:
            self.s.op(eng, lambda e: e.tensor_scalar(out, in0, s1, None, op0), reads=reads, writes=writes, excl=excl)
        else:
            self.s.op(eng, lambda e: e.tensor_scalar(out, in0, s1, s2, op0, op1), reads=reads, writes=writes, excl=excl)

    def stt(self, eng, out, in0, scalar, in1, op0, op1, reads=(), writes=(), excl=()):
        self.s.op(eng, lambda e: e.scalar_tensor_tensor(out, in0, scalar, in1, op0, op1), reads=reads, writes=writes,
                  excl=excl)

    def cp(self, eng, out, in_, reads=(), writes=(), excl=()):
        self.s.op(eng, lambda e: e.tensor_copy(out, in_), reads=reads, writes=writes, excl=excl)

    def dma(self, eng, out, in_, reads=(), writes=()):
        self.s.dma(eng, lambda e: e.dma_start(out=out, in_=in_), reads=reads, writes=writes)

    def layer_wjobs(self, l):
        w = self.d["w_in_p"]
        jobs = []
        for c in range(4):
            jobs.append([(0, 8, 512, w[l, :, c * 512:(c + 1) * 512])])
        g0 = CH_GK[0] * 128
        jobs.append([(0, 8, 384, w[l, :, g0:g0 + 384])])
        for c in range(4):
            c0 = CH_GQ(c, 0) * 128
            jobs.append([(0, 8, 256, w[l, :, c0:c0 + 256])])
        c0 = CH_SV * 128
        jobs.append([(0, 8, 512, w[l, :, c0:c0 + 512])])
        for c in range(4):
            c0 = CH_SG(c, 0) * 128
            jobs.append([(0, 8, 256, w[l, :, c0:c0 + 256])])
        for dc in range(8):
            g0 = CH_GATE(dc, 0) * 128
            jobs.append([(0, 8, 384, w[l, :, g0:g0 + 384]),
                         (3072, 4, 384, self.d["wfin"][l, :, dc * 384:(dc + 1) * 384])])
        for hf in range(2):
            jobs.append([(0, 8, 512, self.d["wout"][l, :, hf * 512:(hf + 1) * 512])])
        return jobs

    def _issue_w(self):
        k = self.w_issued
        if k >= len(self.wjobs):
            return
        self.w_issued += 1
        i = k % 2
        for (off, kch, ncols, src) in self.wjobs[k]:
            dst = self.wslot[i][:, off:off + kch * ncols].rearrange("p (k n) -> p k n", n=ncols)
            self.dma("pool", dst, src.rearrange("(k p) n -> p k n", p=128), writes=[("w", i)])

    def take_w(self):
        k = self.w_used
        self.w_used += 1
        while self.w_issued <= min(k + 1, len(self.wjobs) - 1):
            self._issue_w()
        i = k % 2
        views = []
        for (off, kch, ncols, src) in self.wjobs[k]:
            views.append(self.wslot[i][:, off:off + kch * ncols].rearrange("p (k n) -> p k n", n=ncols))
        return views, ("w", i)

    def xT_res(self, tc):
        return [("xT", 4 * tc + i) for i in range(4)]

    def proj_fm(self, wt, wres, col0, evac):
        for tc in range(4):
            b = self.bank()
            for kc in range(8):
                self.mm(b, self.ps[b][:, :], wt[:, kc, col0:col0 + 128], self.xT[:, kc, tc * 512:(tc + 1) * 512],
                        kc == 0, kc == 7, reads=[wres] + self.xT_res(tc))
            evac(tc, b)

    def bias(self, ch):
        return self.vecs[:, ch:ch + 1]

    def ln_stats(self, src, src_res, tt_i):
        st = self.stat
        k = (tt_i % 4) * 16
        for h in range(2):
            self.s.op("dve", (lambda h: lambda e: e.bn_stats(st[:, k + h * 6:k + h * 6 + 6],
                                                               src[:, h * 512:(h + 1) * 512]))(h),
                      reads=[src_res], writes=[("stat", tt_i % 4)])
        self.s.op("dve", lambda e: e.bn_aggr(st[:, k + 12:k + 14], st[:, k:k + 12]),
                  reads=[("stat", tt_i % 4)], writes=[("stat", tt_i % 4)])
        self.act(st[:, k + 14:k + 15], st[:, k + 13:k + 14], AF.Sqrt, bias=1e-5,
                 reads=[("stat", tt_i % 4)], writes=[("stat", tt_i % 4)])

    def ln_apply(self, src, src_res, tt_i, g_ap, b_ap, gb_res):
        st = self.stat
        k = (tt_i % 4) * 16
        self.s.op("dve", lambda e: e.reciprocal(st[:, k + 14:k + 15], st[:, k + 14:k + 15]),
                  reads=[("stat", tt_i % 4)], writes=[("stat", tt_i % 4)])
        xt = self.xtok[:, tt_i, :]
        self.stt("dve", xt, src, st[:, k + 12:k + 13], g_ap, ALU.subtract, ALU.mult,
                 reads=[src_res, ("stat", tt_i % 4), gb_res], writes=[("xtok", tt_i)])
        self.stt("dve", xt, xt, st[:, k + 14:k + 15], b_ap, ALU.mult, ALU.add,
                 reads=[("xtok", tt_i), ("stat", tt_i % 4), gb_res], writes=[("xtok", tt_i)])

    def to_xT(self, tt_i, tmp_h16):
        if self._xT_pending is not None:
            self.to_xT_finish()
        k = tt_i % 2
        h16 = tmp_h16[k]
        self.act(h16[:, :], self.xtok[:, tt_i, :], AF.Copy, reads=[("xtok", tt_i)], writes=[("h16", k)])
        for kc in range(8):
            self.s.op("pe", (lambda kc: lambda e: e.transpose(self.pst[:, kc * 128:(kc + 1) * 128],
                                                              h16[:, kc * 128:(kc + 1) * 128], self.ident[:, :]))(kc),
                      reads=[("h16", k), ("ident",)], excl=[("ps", 7)])
        self._xT_pending = tt_i

    def to_xT_finish(self):
        tt_i = self._xT_pending
        if tt_i is None:
            return
        self._xT_pending = None
        self.act(self.xT[:, :, tt_i * 128:(tt_i + 1) * 128], self.pst[:, :].rearrange("p (k n) -> p k n", n=128), AF.Copy,
                 writes=[("xT", tt_i)], excl=[("ps", 7)])

    def phase0(self):
        o = self.R_ph
        gb = self.A("lnin", [128, 2, D], F32, o); o += 8192
        xin = [self.A("xin%d" % i, [128, D], F32, o + 4096 * i) for i in range(3)]; o += 12288
        h16 = [self.A("h16a", [128, D], BF16, o), self.A("h16b", [128, D], BF16, o + 2048)]; o += 4096
        self.dma("pool", self.ident[:, :], self.d["ident"], writes=[("ident",)])
        self.dma("sp", self.c32[:, :], self.d["c32"], writes=[("c32",)])
        self._issue_w()
        if self.do_ln_in:
            self.dma("sp", gb[:, :, :], self.d["lnin"], writes=[("lnin",)])

            def load(t):
                self.dma("sp", xin[t % 3][:, :], self.d["x"][t * 128:(t + 1) * 128, :], writes=[("xin", t % 3)])
            load(0)
            load(1)
            self.ln_stats(xin[0][:, :], ("xin", 0), 0)
            for t in range(16):
                if t + 2 < 16:
                    load(t + 2)
                if t + 1 < 16:
                    self.ln_stats(xin[(t + 1) % 3][:, :], ("xin", (t + 1) % 3), t + 1)
                self.ln_apply(xin[t % 3][:, :], ("xin", t % 3), t, gb[:, 0, :], gb[:, 1, :], ("lnin",))
                self.to_xT_finish()
                self.to_xT(t, h16)
            self.to_xT_finish()
        else:
            for tt_i in range(16):
                self.dma("sp", self.xtok[:, tt_i, :], self.d["x"][tt_i * 128:(tt_i + 1) * 128, :], writes=[("xtok", tt_i)])
                self.to_xT(tt_i, h16)
            self.to_xT_finish()

    def attn_pipeline(self, units, Pb, exb, regions, scale, look, s_order=None):
        nu = len(units)
        nr = len(regions)
        nb = len(Pb)
        mulcount = [0]

        def emitS(u, extra=()):
            banks = regions[u % nr]
            offs = []
            off = 0
            for t in units[u]:
                offs.append(off)
                off += t["n"]
            order = list(range(len(units[u]))) if s_order is None or len(units[u]) != len(s_order) else list(s_order)
            for c0 in range(0, len(order), 2):
                grp = order[c0:c0 + 2]
                for i in grp:
                    t = units[u][i]
                    n = t["n"]
                    base = banks[0] * 512 + offs[i]
                    self.mm([banks[offs[i] // 512]], self.psall[:, base:base + n], t["s_lhsT"], t["s_rhs"], True,
                            t.get("b_rhs") is None, reads=list(t["s_reads"]) + list(extra))
                for i in grp:
                    t = units[u][i]
                    if t.get("b_rhs") is None:
                        continue
                    n = t["n"]
                    base = banks[0] * 512 + offs[i]
                    self.mm([banks[offs[i] // 512]], self.psall[:, base:base + n], self.ident[:, :], t["b_rhs"], False, True,
                            reads=[("ident",), t["b_res"]])

        for u in range(min(look, nu)):
            emitS(u)
        for u in range(nu):
            banks = regions[u % nr]
            k = u % nb
            tot = sum(t["n"] for t in units[u])
            base = banks[0] * 512
            src = self.psall[:, base:base + tot]
            ex = [("ps", b) for b in banks]
            if units[u][0]["etab"] is None:
                self.act(Pb[k][:, 0:tot], src, AF.Exp, scale=scale,
                         writes=[("P", k, i) for i in range(len(units[u]))], excl=ex)
            else:
                self.act(exb[k][:, 0:tot], src, AF.Exp, scale=scale, writes=[("ex", k)], excl=ex)
                off = 0
                for i, t in enumerate(units[u]):
                    n = t["n"]
                    eng = "pool" if (POOL_EVERY and mulcount[0] % POOL_EVERY == POOL_EVERY - 1) else "dve"
                    mulcount[0] += 1
                    self.tt(eng, Pb[k][:, off:off + n], exb[k][:, off:off + n], t["etab"], ALU.mult,
                            reads=[("ex", k), t["etab_res"]], writes=[("P", k, i)])
                    off += n
            if u + look < nu:
                emitS(u + look)
            off = 0
            for i, t in enumerate(units[u]):
                n = t["n"]
                ob = t["ob"]
                self.mm(ob, self.ps[ob][:, 0:n], t["v_lhsT"], Pb[k][:, off:off + n], t["first"], t["last"],
                        reads=[("P", k, i)] + t["v_reads"])
                off += n
                for _ in range(DUMMY_MM):
                    self.mm(6, self.ps[6][:, 0:n], self.ident[:, :], Pb[k][:, 0:n], True, True, reads=[("P", k, i), ("ident",)])
                if t["last"] and t["epi"] is not None:
                    t["epi"]()

    def attn_epilogue(self, ob, n, hb, rs, sz_ap, sz_res, y_ap, y_res, slot):
        so = 64 - hb
        ps = self.ps[ob]
        self.s.op("dve", lambda e: e.reciprocal(rs[hb:hb + 64, 0:n], ps[so:so + 64, 0:n]),
                  writes=[("rs", slot)], excl=[("ps", ob)])
        self.tt("pool", rs[hb:hb + 64, 0:n], rs[hb:hb + 64, 0:n], sz_ap, ALU.mult,
                reads=[("rs", slot), sz_res], writes=[("rs", slot)])
        self.tt("dve", y_ap, ps[hb:hb + 64, 0:n], rs[hb:hb + 64, 0:n], ALU.mult,
                reads=[("rs", slot)], writes=[y_res], excl=[("ps", ob)])

    def phaseA(self, l):
        o = self.R_ph
        qT = self.A("naq", [128, S], BF16, o); o += 4096
        kT = self.A("nak", [128, S], BF16, o); o += 4096
        sz = self.A("nasz", [128, S], BF16, o); o += 4096
        Va = self.A("nava", [128, 16, 2, 128], BF16, o); o += 8192
        exb = None
        Pb = [self.A("P%d" % i, [128, 1024], BF16, o + 2048 * i) for i in range(3)]; o += 6144
        rs = [[self.A("rs%d%d" % (i, h), [128, 256], F32, o + 2048 * i + 1024 * h) for h in range(2)] for i in range(2)]
        o += 4096
        bv = self.A("bvna", [128, 512], F32, o); o += 2048
        oa = self.R_y + 32768
        tabraw = self.A("tabraw", [128, 1792], F32, oa); oa += 7168
        E = [self.A("E%d" % i, [128, 1792], BF16, oa + 3584 * i) for i in range(2)]; oa += 7168
        regions = [[0, 1], [2, 3]]
        obanks = [(4, 5), (6, 7)]
        self.pbanks = [0, 1, 2, 3, 4, 5, 6, 7]
        d = self.d
        self.dma("sp", bv[:, :], d["bvb"][l, :, 0:512], writes=[("bvna",)])
        self.s.op("pool", lambda e: e.memset(Va[:, :, 0, 64:128], 1.0), writes=[("nava", j) for j in range(16)])
        self.s.op("pool", lambda e: e.memset(Va[:, :, 1, 0:64], 1.0), writes=[("nava", j) for j in range(16)])
        hcount = 0
        ecount = 0
        for c in range(4):
            (wt,), wres = self.take_w()

            def ev_q(tc, b, dst=qT, ch=CH_NA(c, 0), nm="naq"):
                self.ts("dve", dst[:, tc * 512:(tc + 1) * 512], self.ps[b][:, :], self.bias(ch), None, ALU.add,
                        reads=[("vecs",)], writes=[(nm, tc)], excl=[("ps", b)])
            self.proj_fm(wt, wres, 0, ev_q)
            self.proj_fm(wt, wres, 128, lambda tc, b: ev_q(tc, b, kT, CH_NA(c, 1), "nak"))

            def ev_z(tc, b, ch=CH_NA(c, 3)):
                self.act(sz[:, tc * 512:(tc + 1) * 512], self.ps[b][:, :], AF.Silu, bias=self.bias(ch),
                         reads=[("vecs",)], writes=[("nasz", tc)], excl=[("ps", b)])
            self.proj_fm(wt, wres, 384, ev_z)
            for j0 in range(0, 16, 4):
                b = self.bank()
                for t in range(4):
                    for kc in range(8):
                        self.mm(b, self.ps[b][:, t * 128:(t + 1) * 128], self.xT[:, kc, (j0 + t) * 128:(j0 + t + 1) * 128],
                                wt[:, kc, 256:384], kc == 0, kc == 7, reads=[wres, ("xT", j0 + t)])
                pv = self.ps[b][:, :].rearrange("p (a n) -> p a n", n=128)
                for hi in range(2):
                    bsl = bv[:, c * 128 + hi * 64:c * 128 + hi * 64 + 64].unsqueeze(1).broadcast_to([128, 4, 64])
                    self.tt("dve", Va[:, j0:j0 + 4, hi, hi * 64:hi * 64 + 64], pv[:, :, hi * 64:hi * 64 + 64], bsl, ALU.add,
                            reads=[("bvna",)], writes=[("nava", j0 + t) for t in range(4)], excl=[("ps", b)])
            units = []
            eks = []
            for hi in range(2):
                ek = ecount % 2
                ecount += 1
                eks.append(ek)
                self.dma("sp", tabraw[:, :], d["natab"][l, 2 * c + hi], writes=[("tabraw",)])
                self.act(E[ek][:, :], tabraw[:, :], AF.Identity, scale=8.0, reads=[("tabraw",)], writes=[("E", ek)])
            for q8 in range(8):
                r0 = 4 * q8
                if q8 == 0:
                    jl, kind = [0, 1, 2, 3], 0
                elif q8 == 7:
                    jl, kind = [12, 13, 14, 15], 0
                else:
                    jl, kind = list(range(2 * q8 - 2, 2 * q8 + 4)), 1
                obs = obanks[hcount % 2]
                slot = hcount % 2
                hcount += 1
                q0 = q8 * 256
                tcq = q0 // 512
                tl = {0: [], 1: []}
                for hi in range(2):
                    hb = 64 * hi
                    ek = eks[hi]
                    ob = obs[hi]

                    def epi(ob=ob, hb=hb, slot=slot, hi=hi, q0=q0, tcq=tcq, c=c):
                        self.attn_epilogue(ob, 256, hb, rs[slot][hi], sz[hb:hb + 64, q0:q0 + 256], ("nasz", tcq),
                                           self.yA[hb:hb + 64, c, q0:q0 + 256], ("yA", c, tcq), (slot, hi))
                    for idx, j in enumerate(jl):
                        s0 = r0 - 2 * j + 8
                        assert 2 <= s0 and s0 + 4 <= 16
                        tl[hi].append(dict(
                            n=256, s_lhsT=kT[hb:hb + 64, j * 128:(j + 1) * 128], s_rhs=qT[hb:hb + 64, q0:q0 + 256],
                            s_reads=[("nak", j // 4), ("naq", tcq)],
                            etab=None, etab_res=None,
                            b_rhs=E[ek][:, kind * 896 + (s0 - 2) * 64:kind * 896 + (s0 + 2) * 64], b_res=("E", ek),
                            v_lhsT=Va[:, j, hi, :], v_reads=[("nava", j)], ob=ob,
                            first=(idx == 0), last=(idx == len(jl) - 1), epi=epi if idx == len(jl) - 1 else None))
                for i in range(0, len(jl), 2):
                    units.append([tl[0][i], tl[0][i + 1], tl[1][i], tl[1][i + 1]])
            self.attn_pipeline(units, Pb, exb, regions, 0.125, look=2, s_order=[0, 2, 1, 3])
        if self.dbg:
            self.dbg_dump("yA%d" % l, self.yA, [("yA", c, t) for c in range(4) for t in range(4)])

    def dbg_dump(self, name, t, reads):
        shp = list(t.shape)
        dt = t.dtype
        dd = self.nc.dram_tensor("dbg_" + name, shp, dt, kind="ExternalOutput").ap()
        self.dbg_d[name] = dd
        self.dma("sp", dd, t[tuple(slice(None) for _ in shp)], reads=reads)

    def barrier(self):
        for e in Sched.ENGS:
            self.s.wait_all(e, dmas=False)

    def phaseB(self, l):
        o = self.R_ph
        kT2 = self.A("gk", [128, 2, S], BF16, o); o += 8192
        qT = self.A("gq", [128, S], BF16, o); o += 4096
        sz = self.A("gsz", [128, S], BF16, o); o += 4096
        ropeb = [self.A("rope", [128, 2, 512], F32, o)] * 2; o += 4096
        sq2 = [self.A("sq%d" % i, [128, 512], F32, o + 2048 * i) for i in range(2)]; o += 4096
        qb2 = [self.A("qb%d" % i, [128, 512], F32, o + 2048 * i) for i in range(2)]; o += 4096
        u2 = [self.A("u%d" % i, [128, 512], F32, o + 2048 * i) for i in range(2)]; o += 4096
        rstd2 = [self.A("rstd", [128, 512], F32, o)] * 2; o += 2048
        t12 = [self.A("t1", [128, 512], F32, o)] * 2; o += 2048
        bv = self.A("bvg", [128, 128], F32, o); o += 512
        oa = self.R_y + 32768
        Vg = self.A("gv", [128, 16, 2, 192], BF16, oa); oa += 12288
        rs = [[self.A("grs%d%d" % (i, h), [128, 256], F32, oa + 2048 * i + 1024 * h) for h in range(2)] for i in range(2)]
        oa += 4096
        Pb = [self.A("gP%d" % i, [128, 1024], BF16, o + 2048 * i) for i in range(3)]; o += 6144
        assert o <= SB_END, o
        regions = [[0, 1], [2, 3]]
        obanks = [(4, 5), (6, 7)]
        self.pbanks = [0, 1, 2, 3, 4, 5, 6, 7]
        d = self.d
        c32 = self.c32
        self.dma("sp", bv[:, :], d["bvb"][l, :, 512:640], writes=[("bvg",)])
        self.s.op("pool", lambda e: e.memset(Vg[:, :, :, 0:64], 1.0), writes=[("gv", j) for j in range(16)])
        self.s.op("pool", lambda e: e.memset(Vg[:, :, :, 128:192], 1.0), writes=[("gv", j) for j in range(16)])
        self.rope_i = 0

        def rope_unit(tc, b, bias_ch, gain_col, dst_ap, dst_res):
            rk = self.rope_i % 2
            self.rope_i += 1
            sq, qb, u, rstd, t1, t2 = sq2[rk], qb2[rk], u2[rk], rstd2[rk], t12[rk], sq2[rk]
            R = lambda nm: ("sq", rk) if nm == "t2" else (nm, rk if nm in ("qb", "sq", "u") else 0)
            self.dma("sp", ropeb[rk][:, :, :], d["rope"][:, :, tc * 512:(tc + 1) * 512], writes=[("rope", 0)])
            ps = self.ps[b]
            self.act(sq[:, :], ps[:, :], AF.Square, bias=self.bias(bias_ch), reads=[("vecs",)], writes=[R("sq")],
                     excl=[("ps", b)])
            self.ts("dve", u[:, :], ps[:, :], self.bias(bias_ch), self.vecs[:, gain_col:gain_col + 1], ALU.add, ALU.mult,
                    reads=[("vecs",)], writes=[R("u")], excl=[("ps", b)])
            b2 = self.bank()
            self.mm(b2, self.ps[b2][:, :], c32[:, 128:256], sq[:, :], True, True, reads=[R("sq"), ("c32",)])
            b3 = self.bank()
            self.mm(b3, self.ps[b3][:, :], c32[:, 0:128], u[:, :], True, True, reads=[R("u"), ("c32",)])
            self.act(rstd[:, :], self.ps[b2][:, :], AF.Ln, bias=64e-6, writes=[R("rstd")], excl=[("ps", b2)])
            self.act(rstd[:, :], rstd[:, :], AF.Exp, scale=-0.5, reads=[R("rstd")], writes=[R("rstd")])
            self.tt("pool", t1[:, :], u[:, :], ropeb[rk][:, 0, :], ALU.mult, reads=[R("u"), ("rope", 0)], writes=[R("t1")])
            self.tt("dve", t2[:, :], self.ps[b3][:, :], ropeb[rk][:, 1, :], ALU.mult, reads=[("rope", 0)],
                    writes=[R("t2")], excl=[("ps", b3)])
            self.tt("dve", t1[:, :], t1[:, :], t2[:, :], ALU.add, reads=[R("t1"), R("t2")], writes=[R("t1")])
            self.tt("dve", dst_ap, t1[:, :], rstd[:, :], ALU.mult, reads=[R("t1"), R("rstd")], writes=[dst_res])

        (wt,), wres = self.take_w()
        for g in range(2):
            self.proj_fm(wt, wres, g * 128,
                         lambda tc, b, g=g: rope_unit(tc, b, CH_GK[g], 65, kT2[:, g, tc * 512:(tc + 1) * 512], ("gk", g, tc)))
        for j0 in range(0, 16, 4):
            b = self.bank()
            for t in range(4):
                for kc in range(8):
                    self.mm(b, self.ps[b][:, t * 128:(t + 1) * 128], self.xT[:, kc, (j0 + t) * 128:(j0 + t + 1) * 128],
                            wt[:, kc, 256:384], kc == 0, kc == 7, reads=[wres, ("xT", j0 + t)])
            pv = self.ps[b][:, :].rearrange("p (a n) -> p a n", n=128)
            for g in range(2):
                bsl = bv[:, g * 64:g * 64 + 64].unsqueeze(1).broadcast_to([128, 4, 64])
                self.tt("dve", Vg[:, j0:j0 + 4, g, 64:128], pv[:, :, g * 64:g * 64 + 64], bsl, ALU.add,
                        reads=[("bvg",)], writes=[("gv", j0 + t) for t in range(4)], excl=[("ps", b)])
        hcount = 0
        for c in range(4):
            g = c // 2
            (wt,), wres = self.take_w()
            self.proj_fm(wt, wres, 0,
                         lambda tc, b: rope_unit(tc, b, CH_GQ(c, 0), 64, qT[:, tc * 512:(tc + 1) * 512], ("gq", tc)))

            def ev_z(tc, b, ch=CH_GQ(c, 1)):
                self.act(sz[:, tc * 512:(tc + 1) * 512], self.ps[b][:, :], AF.Silu, bias=self.bias(ch),
                         reads=[("vecs",)], writes=[("gsz", tc)], excl=[("ps", b)])
            self.proj_fm(wt, wres, 128, ev_z)
            units = []
            for q8 in range(8):
                q0 = q8 * 256
                tcq = q0 // 512
                obs = obanks[hcount % 2]
                slot = hcount % 2
                hcount += 1
                tl = {0: [], 1: []}
                for hi in range(2):
                    hb = 64 * hi
                    ob = obs[hi]

                    def epi(ob=ob, hb=hb, slot=slot, hi=hi, q0=q0, tcq=tcq, c=c):
                        self.attn_epilogue(ob, 256, hb, rs[slot][hi], sz[hb:hb + 64, q0:q0 + 256], ("gsz", tcq),
                                           self.yB[hb:hb + 64, c, q0:q0 + 256], ("yB", c, tcq), (slot, hi))
                    for j in range(16):
                        vl = Vg[:, j, g, 64:192] if hi == 0 else Vg[:, j, g, 0:128]
                        tl[hi].append(dict(
                            n=256, s_lhsT=kT2[hb:hb + 64, g, j * 128:(j + 1) * 128], s_rhs=qT[hb:hb + 64, q0:q0 + 256],
                            s_reads=[("gk", g, j // 4), ("gq", tcq)], etab=None, etab_res=None,
                            v_lhsT=vl, v_reads=[("gv", j)], ob=ob, first=(j == 0), last=(j == 15),
                            epi=epi if j == 15 else None))
                for i in range(0, 16, 2):
                    units.append([tl[0][i], tl[0][i + 1], tl[1][i], tl[1][i + 1]])
            self.attn_pipeline(units, Pb, None, regions, 8.0, look=2, s_order=[0, 2, 1, 3])
        if self.dbg:
            self.dbg_dump("yB%d" % l, self.yB, [("yB", c, t) for c in range(4) for t in range(4)])

    def phaseC(self, l):
        o = self.R_ph
        vn = self.A("vn", [128, 16, 512], BF16, o); o += 16384
        sgw = self.A("sgw", [128, 8, 128], BF16, o); o += 2048
        vraw = [self.A("vraw%d" % i, [128, 512], F32, o + 2048 * i) for i in range(2)]; o += 4096
        sgln = self.A("sgln", [128, 2, 512], F32, o); o += 4096
        bvs = self.A("bvs", [128, 512], F32, o); o += 2048
        uzb = [self.A("uz%d" % i, [128, 512], F32, o + 2048 * i) for i in range(2)]; o += 4096
        szt = [self.A("szt%d" % i, [128, 512], F32, o + 2048 * i) for i in range(2)]; o += 4096
        sgb = [self.A("sgb%d" % i, [128, 512], F32, o + 2048 * i) for i in range(2)]; o += 4096
        mix = [self.A("mix%d" % i, [128, 512], F32, o + 2048 * i) for i in range(2)]; o += 4096
        assert o <= SB_END, o
        self.pbanks = [0, 1, 2, 3, 4, 5, 6]
        d = self.d
        st = self.stat
        self.dma("sp", bvs[:, :], d["bvb"][l, :, 640:1152], writes=[("bvs",)])
        self.dma("sp", sgln[:, :, :], d["sgln"][l], writes=[("sgln",)])
        self.dma("pool", sgw[:, :, :], d["sgwT"][l], writes=[("sgw",)])
        (wt,), wres = self.take_w()
        for tt_i in range(16):
            b = self.bank()
            k = tt_i % 2
            for kc in range(8):
                self.mm(b, self.ps[b][:, :], self.xT[:, kc, tt_i * 128:(tt_i + 1) * 128], wt[:, kc, 0:512],
                        kc == 0, kc == 7, reads=[wres, ("xT", tt_i)])
            vr = vraw[k]
            self.tt("dve", vr[:, :], self.ps[b][:, :], bvs[:, :], ALU.add, reads=[("bvs",)], writes=[("vraw", k)],
                    excl=[("ps", b)])
            sk = 16 * k
            self.s.op("dve", (lambda vr, sk: lambda e: e.bn_stats(st[:, sk:sk + 6], vr[:, :]))(vr, sk),
                      reads=[("vraw", k)], writes=[("stat", k)])
            self.s.op("dve", (lambda sk: lambda e: e.bn_aggr(st[:, sk + 12:sk + 14], st[:, sk:sk + 6]))(sk),
                      reads=[("stat", k)], writes=[("stat", k)])
            self.act(st[:, sk + 14:sk + 15], st[:, sk + 13:sk + 14], AF.Sqrt, bias=1e-5,
                     reads=[("stat", k)], writes=[("stat", k)])
            self.s.op("dve", (lambda sk: lambda e: e.reciprocal(st[:, sk + 14:sk + 15], st[:, sk + 14:sk + 15]))(sk),
                      reads=[("stat", k)], writes=[("stat", k)])
            self.stt("dve", vr[:, :], vr[:, :], st[:, sk + 12:sk + 13], sgln[:, 0, :], ALU.subtract, ALU.mult,
                     reads=[("vraw", k), ("stat", k), ("sgln",)], writes=[("vraw", k)])
            self.stt("dve", vn[:, tt_i, :], vr[:, :], st[:, sk + 14:sk + 15], sgln[:, 1, :], ALU.mult, ALU.add,
                     reads=[("vraw", k), ("stat", k), ("sgln",)], writes=[("vn", tt_i)])
        cnt = 0
        for c in range(4):
            (wt,), wres = self.take_w()
            self.dma("sp", sgb[c % 2][:, :], d["sgb"][l, :, c, :], writes=[("sgb", c % 2)])
            for tc in range(4):
                bu = self.bank()
                for kc in range(8):
                    self.mm(bu, self.ps[bu][:, :], wt[:, kc, 0:128], self.xT[:, kc, tc * 512:(tc + 1) * 512],
                            kc == 0, kc == 7, reads=[wres] + self.xT_res(tc))
                bz = self.bank()
                for kc in range(8):
                    self.mm(bz, self.ps[bz][:, :], wt[:, kc, 128:256], self.xT[:, kc, tc * 512:(tc + 1) * 512],
                            kc == 0, kc == 7, reads=[wres] + self.xT_res(tc))
                k = cnt % 2
                cnt += 1
                self.act(szt[k][:, :], self.ps[bz][:, :], AF.Silu, bias=self.bias(CH_SG(c, 1)), reads=[("vecs",)],
                         writes=[("szt", k)], excl=[("ps", bz)])
                uzs = uzb[k][:, :]
                self.stt("dve", uzs, self.ps[bu][:, :], self.bias(CH_SG(c, 0)), szt[k][:, :], ALU.add, ALU.mult,
                         reads=[("vecs",), ("szt", k)], writes=[("uz", k)], excl=[("ps", bu)])
                bm = [self.bank(), self.bank()]
                for gi in range(2):
                    for t in range(4):
                        tt_i = 4 * tc + t
                        self.mm(bm[gi], self.ps[bm[gi]][:, t * 128:(t + 1) * 128], vn[:, tt_i, c * 128:(c + 1) * 128],
                                sgw[:, 2 * c + gi, :], True, True, reads=[("vn", tt_i), ("sgw",)])
                mk = mix[k]
                for gi in range(2):
                    hb = 64 * gi
                    self.tt("dve", mk[hb:hb + 64, :], self.ps[bm[gi]][hb:hb + 64, :], sgb[c % 2][hb:hb + 64, :], ALU.add,
                            reads=[("sgb", c % 2)], writes=[("mix", k)], excl=[("ps", bm[gi])])
                self.tt("pool", self.yC[:, c, tc * 512:(tc + 1) * 512], mk[:, :], uzs, ALU.mult,
                        reads=[("mix", k), ("uz", k)], writes=[("yC", c, tc)])
        if self.dbg:
            self.dbg_dump("yC%d" % l, self.yC, [("yC", c, t) for c in range(4) for t in range(4)])

    def phaseD(self, l, last):
        o = self.R_ph
        mT = self.A("mT", [128, 8, S], BF16, o); o += 32768
        gsig = [self.A("gsig%d" % i, [128, 512], F32, o + 2048 * i) for i in range(2)]; o += 4096
        mt = [self.A("mt%d" % i, [128, 512], F32, o + 2048 * i) for i in range(3)]; o += 6144
        assert o <= SB_END, o
        self.pbanks = [0, 1, 2, 3, 4, 5, 6]
        d = self.d
        gcnt = 0
        for dc in range(8):
            (wg, wb), wres = self.take_w()
            for tc in range(4):
                for br in range(3):
                    bg = self.bank()
                    for kc in range(8):
                        self.mm(bg, self.ps[bg][:, :], wg[:, kc, br * 128:(br + 1) * 128],
                                self.xT[:, kc, tc * 512:(tc + 1) * 512], kc == 0, kc == 7, reads=[wres] + self.xT_res(tc))
                    k = gcnt % 2
                    gcnt += 1
                    self.act(gsig[k][:, :], self.ps[bg][:, :], AF.Sigmoid, bias=self.bias(CH_GATE(dc, br)),
                             reads=[("vecs",)], writes=[("gsig", k)], excl=[("ps", bg)])
                    bp = self.bank()
                    yb = self.ybr[br]
                    nm = ("yA", "yB", "yC")[br]
                    for kc in range(4):
                        self.mm(bp, self.ps[bp][:, :], wb[:, kc, br * 128:(br + 1) * 128], yb[:, kc, tc * 512:(tc + 1) * 512],
                                kc == 0, kc == 3, reads=[wres, (nm, kc, tc)])
                    mb = mt[0] if br == 0 else mt[1]
                    mk = ("mt", 0 if br == 0 else 1)
                    self.tt("dve", mb[:, :], self.ps[bp][:, :], gsig[k][:, :], ALU.mult, reads=[("gsig", k)],
                            writes=[mk], excl=[("ps", bp)])
                    if br == 1:
                        self.tt("dve", mt[0][:, :], mt[0][:, :], mt[1][:, :], ALU.add, reads=[("mt", 0), ("mt", 1)],
                                writes=[("mt", 0)])
                    elif br == 2:
                        self.tt("dve", mT[:, dc, tc * 512:(tc + 1) * 512], mt[0][:, :], mt[1][:, :], ALU.add,
                                reads=[("mt", 0), ("mt", 1)], writes=[("mT", tc)])
        self.barrier()
        oa = self.R_y
        post = self.A("post", [128, 3, D], F32, oa); oa += 12288
        h16 = [self.A("dh16%d" % i, [128, D], BF16, oa + 2048 * i) for i in range(2)]; oa += 4096
        self.dma("sp", post[:, :, :], d["post"][l], writes=[("post",)])
        for hf in range(2):
            (wo,), wres = self.take_w()
            for tt_i in range(16):
                tcq = tt_i // 4
                b = self.bank()
                for dc in range(8):
                    self.mm(b, self.ps[b][:, :], mT[:, dc, tt_i * 128:(tt_i + 1) * 128], wo[:, dc, 0:512],
                            dc == 0, dc == 7, reads=[("mT", tcq), wres])
                x_ap = self.xtok[:, tt_i, hf * 512:(hf + 1) * 512]
                self.stt("dve", x_ap, x_ap, float(ALPHA), self.ps[b][:, :], ALU.mult, ALU.add,
                         reads=[("xtok", tt_i)], writes=[("xtok", tt_i)], excl=[("ps", b)])
                if hf == 1:
                    self.tt("pool", self.xtok[:, tt_i, :], self.xtok[:, tt_i, :], post[:, 0, :], ALU.add,
                            reads=[("xtok", tt_i), ("post",)], writes=[("xtok", tt_i)])
        self.ln_stats(self.xtok[:, 0, :], ("xtok", 0), 0)
        for t in range(16):
            if t + 1 < 16:
                self.ln_stats(self.xtok[:, t + 1, :], ("xtok", t + 1), t + 1)
            self.ln_apply(self.xtok[:, t, :], ("xtok", t), t, post[:, 1, :], post[:, 2, :], ("post",))
            if last:
                self.dma("sp", self.y_d[t * 128:(t + 1) * 128, :], self.xtok[:, t, :], reads=[("xtok", t)])
            else:
                self.to_xT_finish()
                self.to_xT(t, h16)
        self.to_xT_finish()
        self.barrier()

    def layer(self, l, last):
        self.dma("sp", self.vecs[:, :], self.d["vecs"][l], writes=[("vecs",)])
        self.phaseA(l)
        self.barrier()
        self.phaseB(l)
        self.barrier()
        self.phaseC(l)
        self.barrier()
        self.phaseD(l, last)

    def build(self):
        from contextlib import ExitStack
        nc = self.nc
        self.phase0()
        self.barrier()
        for i, l in enumerate(self.layers):
            self.layer(l, last=(i == len(self.layers) - 1))
        self.s.wait_all("sp")
        with ExitStack() as es:
            sems = {e: es.enter_context(nc.semaphore("s_" + e)) for e in Sched.ENGS}
            dsems = [es.enter_context(nc.semaphore("q%d" % i)) for i in range(self.s.n_dma)]
            block = es.enter_context(nc.Block())
            self.s.emit(nc, block, sems, dsems)
        return nc


def _run(layers, do_ln_in, x_list, shared, dbg=False):
    kb = KB(layers, do_ln_in, dbg)
    nc = kb.build()
    in_maps = []
    for xb in x_list:
        m = dict(shared)
        m["x"] = np.ascontiguousarray(xb, dtype=np.float32)
        in_maps.append(m)
    return run_bass_kernel_spmd(nc, in_maps, core_ids=list(range(len(x_list))))


def kernel(**inputs):
    x = np.asarray(inputs["x"], dtype=np.float32)
    shared = prep_inputs(**inputs)
    res = _run([0, 1], True, [x[b] for b in range(x.shape[0])], shared)
    return np.stack([r["y"] for r in res.results], axis=0).astype(np.float32)
```

```python
import numpy as np
import concourse.bass as bass
import concourse.mybir as mybir
from concourse.bass_utils import run_bass_kernel_spmd

F32, BF16 = mybir.dt.float32, mybir.dt.bfloat16
AF = mybir.ActivationFunctionType
ALU = mybir.AluOpType

S = 2048
D = 1024
L = 2
NCH = 63
WIN = NCH * 128
ALPHA = (2.0 * L) ** 0.25
SB_BASE = 16512
SB_END = 229376
EMBED_WAIT = ("act", "dve", "pool")


class Sched:
    ENGS = ("pe", "act", "dve", "pool", "sp")

    def __init__(self, n_dma_sems=24):
        self.prog = {e: [] for e in self.ENGS}
        self.serial = {e: 0 for e in self.ENGS}
        self.seen = {e: {} for e in self.ENGS}
        self.lastw = {}
        self.readers = {}
        self.lastx = {}
        self.waited = {e: set() for e in self.ENGS}
        self.n_dma = n_dma_sems
        self.dma_i = 0
        self.dma_ip = 0
        self.dma_val = [0] * n_dma_sems

    def _deps(self, eng, reads, writes, excl):
        need = {}

        def add(tok, raw):
            key, val, teng = tok
            if teng == eng and not raw:
                return
            if self.seen[eng].get(key, 0) >= val:
                return
            if need.get(key, 0) < val:
                need[key] = val

        for r in reads:
            t = self.lastw.get(r)
            if t:
                add(t, True)
        for w in writes:
            t = self.lastw.get(w)
            if t:
                add(t, False)
            for t in self.readers.get(w, {}).values():
                add(t, False)
        for x in excl:
            t = self.lastx.get(x)
            if t:
                add(t, False)
        for key, val in need.items():
            self.seen[eng][key] = val
            self.prog[eng].append(("wait", key, val))
            if key[0] != "q":
                self.waited[key].add(val)

    def _commit(self, tok, reads, writes, excl):
        for r in reads:
            self.readers.setdefault(r, {})[tok[0]] = tok
        for w in writes:
            self.lastw[w] = tok
            self.readers[w] = {}
        for x in excl:
            self.lastx[x] = tok

    def op(self, eng, fn, reads=(), writes=(), excl=()):
        self._deps(eng, reads, writes, excl)
        self.serial[eng] += 1
        tok = (eng, self.serial[eng], eng)
        self.prog[eng].append(("op", fn, self.serial[eng]))
        self._commit(tok, reads, writes, excl)
        return tok

    def dma(self, eng, fn, reads=(), writes=()):
        self._deps(eng, reads, writes, ())
        if eng == "pool":
            k = 16 + self.dma_ip % (self.n_dma - 16)
            self.dma_ip += 1
        else:
            k = self.dma_i % 16
            self.dma_i += 1
        key = "q%d" % k
        prev = self.dma_val[k]
        if prev > 0 and self.seen[eng].get(key, 0) < prev:
            self.seen[eng][key] = prev
            self.prog[eng].append(("wait", key, prev))
        self.dma_val[k] += 16
        tok = (key, self.dma_val[k], None)
        self.prog[eng].append(("dma", fn, k))
        self._commit(tok, reads, writes, ())
        return tok

    def wait_all(self, eng, dmas=True):
        for e in self.ENGS:
            if e != eng and self.serial[e] > 0 and self.seen[eng].get(e, 0) < self.serial[e]:
                self.seen[eng][e] = self.serial[e]
                self.prog[eng].append(("wait", e, self.serial[e]))
                self.waited[e].add(self.serial[e])
        for k in range(self.n_dma if dmas else 0):
            key = "q%d" % k
            if self.dma_val[k] > 0 and self.seen[eng].get(key, 0) < self.dma_val[k]:
                self.seen[eng][key] = self.dma_val[k]
                self.prog[eng].append(("wait", key, self.dma_val[k]))

    def emit(self, nc, block, sems, dsems):
        rank = {}
        for e in self.ENGS:
            ws = sorted(self.waited[e])
            rank[e] = {s: i + 1 for i, s in enumerate(ws)}
        handles = {"pe": "tensor", "act": "scalar", "dve": "vector", "pool": "gpsimd", "sp": "sync"}

        def run(ename):
            def body(eng):
                pend = []

                def semval(key, val):
                    if key[0] == "q":
                        return dsems[int(key[1:])], val
                    return sems[key], rank[key][val]

                for item in self.prog[ename]:
                    if item[0] == "wait":
                        pend.append(semval(item[1], item[2]))
                        continue
                    embed = None
                    if pend and item[0] == "op" and ename in EMBED_WAIT:
                        embed = pend.pop()
                    for (sm, v) in pend:
                        eng.wait_ge(sm, v)
                    pend = []
                    ins = item[1](eng)
                    if embed is not None:
                        ins._wait_ge(embed[0], embed[1])
                    if item[0] == "op":
                        if item[2] in rank[ename]:
                            ins.then_inc(sems[ename], 1)
                    else:
                        ins.then_inc(dsems[item[2]], 16)
                for (sm, v) in pend:
                    eng.wait_ge(sm, v)
            return body

        for ename in self.ENGS:
            getattr(block, handles[ename])(run(ename))


def _win_cols():
    naq, nak, nav, naz, gqq, gqk, gqv, gqz, sgu, sgv, sgz, gat = (0, 512, 1024, 1536, 2048, 2560, 2688, 2816,
                                                                 3328, 3840, 4352, 4864)
    r = lambda a, n: list(range(a, a + n))
    cols = []
    for c in range(4):
        cols += r(naq + c * 128, 128) + r(nak + c * 128, 128) + r(nav + c * 128, 128) + r(naz + c * 128, 128)
    cols += r(gqk, 64) + r(gqk, 64) + r(gqk + 64, 64) + r(gqk + 64, 64) + r(gqv, 128)
    for c in range(4):
        cols += r(gqq + c * 128, 128) + r(gqz + c * 128, 128)
    cols += r(sgv, 512)
    for c in range(4):
        cols += r(sgu + c * 128, 128) + r(sgz + c * 128, 128)
    for dc in range(8):
        for b in range(3):
            cols += r(gat + b * 1024 + dc * 128, 128)
    assert len(cols) == WIN
    return np.asarray(cols)


def CH_NA(c, which):
    return 4 * c + which
CH_GK = (16, 17)
CH_GV = 18
def CH_GQ(c, which):
    return 19 + 2 * c + which
CH_SV = 27
def CH_SG(c, which):
    return 31 + 2 * c + which
def CH_GATE(dc, b):
    return 39 + 3 * dc + b


def _na_index_tables():
    p = np.arange(128)
    ck = p % 64
    half = p // 64
    s = np.arange(2, 16)
    cq = np.arange(64)
    dr = (8 - s)[None, :, None] + half[:, None, None] + 0 * cq[None, None, :]
    dcol = np.clip(ck[:, None, None] - cq[None, None, :] + 15, 0, 30) + 0 * s[None, :, None]
    cs = np.clip(cq - 8, 0, 48)
    colv = (ck[:, None, None] >= cs[None, None, :]) & (ck[:, None, None] < cs[None, None, :] + 16)
    colv = colv & (s[None, :, None] > -100)
    full_ok = colv & (np.abs(dr) <= 7)
    int_ok = colv & (dr >= -4) & (dr <= 3)
    ridx = np.clip(dr + 7, 0, 14)
    return ridx, dcol, full_ok, int_ok


def _rope_tables():
    t = np.arange(S)
    row = (t // 64).astype(np.float64)
    col = (t % 64).astype(np.float64)
    freqs = 10000.0 ** (-np.arange(0, 32, 2, dtype=np.float64) / 32.0)
    ang = np.concatenate([row[:, None] * freqs, col[:, None] * freqs], axis=-1)
    cos = np.cos(ang)
    sin = np.sin(ang)
    p = np.arange(128)
    d = p % 64
    C = cos[:, d // 2].T
    Sg = (sin[:, d // 2] * np.where(d % 2 == 0, -1.0, 1.0)[None, :]).T
    return np.ascontiguousarray(np.stack([C, Sg], axis=1)).astype(np.float32)


def _consts32():
    perm = np.zeros((128, 128), np.float32)
    for i in range(64):
        perm[2 * i + 1, 2 * i] = 1.0
        perm[2 * i, 2 * i + 1] = 1.0
    k = np.arange(128)
    bones = (k[:, None] // 64 == k[None, :] // 64).astype(np.float32)
    return np.concatenate([perm, bones], axis=1)


def prep_inputs(x, ln_in_g, ln_in_b, w_in, b_in, na_rpb, q_norm_g, k_norm_g, sg_ln_g, sg_ln_b, sg_w, sg_b,
                w_br_a, w_br_b, w_br_c, w_out, b_out, ln_post_g, ln_post_b):
    f = lambda a: np.ascontiguousarray(np.asarray(a, dtype=np.float32))
    cols = _win_cols()
    shared = {}
    shared["w_in_p"] = f(np.take(w_in, cols, axis=2))
    bp = np.take(b_in, cols, axis=1)
    vecs = np.zeros((L, 128, 80), np.float32)
    vecs[:, :, 0:NCH] = bp.reshape(L, NCH, 128).transpose(0, 2, 1)
    vecs[:, :, 64] = np.tile(q_norm_g, (1, 2))
    vecs[:, :, 65] = np.tile(k_norm_g, (1, 2))
    shared["vecs"] = vecs
    bv = np.concatenate([b_in[:, 1024:1536], b_in[:, 2688:2816], b_in[:, 3840:4352]], axis=1)
    shared["bvb"] = f(np.broadcast_to(bv[:, None, :], (L, 128, 1152)))
    shared["lnin"] = f(np.broadcast_to(np.stack([ln_in_g, ln_in_b])[None], (128, 2, D)))
    ridx, dcol, full_ok, int_ok = _na_index_tables()
    rp = np.asarray(na_rpb, np.float32)
    g = rp[:, :, ridx, dcol]
    neg = np.float32(-1e30)
    tfull = np.where(full_ok[None, None], g, neg)
    tint = np.where(int_ok[None, None], g, neg)
    shared["natab"] = f(np.stack([tfull, tint], axis=3).reshape(L, 8, 128, 2 * 896))
    shared["rope"] = _rope_tables()
    shared["c32"] = _consts32()
    shared["ident"] = np.eye(128, dtype=np.float32)
    shared["sgln"] = f(np.broadcast_to(np.stack([sg_ln_g, sg_ln_b], axis=1)[:, None], (L, 128, 2, 512)))
    shared["sgwT"] = f(np.transpose(sg_w, (0, 3, 1, 2)))
    sb = np.asarray(sg_b, np.float32)
    sbb = sb.reshape(L, 4, 2, 1, 128)
    sbb = np.broadcast_to(sbb, (L, 4, 2, 64, 128)).reshape(L, 4, 128, 128)
    sbb = np.broadcast_to(sbb[:, :, :, None, :], (L, 4, 128, 4, 128)).reshape(L, 4, 128, 512)
    shared["sgb"] = f(sbb.transpose(0, 2, 1, 3))
    wf = np.stack([w_br_a, w_br_b, w_br_c], axis=2)
    wf = wf.reshape(L, 512, 3, 8, 128).transpose(0, 1, 3, 2, 4).reshape(L, 512, 3072)
    shared["wfin"] = f(wf)
    shared["wout"] = f(w_out)
    shared["post"] = f(np.broadcast_to(np.stack([b_out, ln_post_g, ln_post_b], axis=1)[:, None], (L, 128, 3, D)))
    return shared


DUMMY_MM = 0
POOL_EVERY = 0


class KB:
    def __init__(self, layers, do_ln_in, dbg=False):
        self.layers = list(layers)
        self.do_ln_in = do_ln_in
        self.dbg = dbg
        self.s = Sched()
        nc = self.nc = bass.Bass("TRN2", target_bir_lowering=False)
        di = lambda n, shp: nc.dram_tensor(n, shp, F32, kind="ExternalInput").ap()
        self.d = dict(
            x=di("x", [S, D]), lnin=di("lnin", [128, 2, D]), w_in_p=di("w_in_p", [L, D, WIN]),
            vecs=di("vecs", [L, 128, 80]), bvb=di("bvb", [L, 128, 1152]), natab=di("natab", [L, 8, 128, 1792]),
            rope=di("rope", [128, 2, S]), c32=di("c32", [128, 256]), ident=di("ident", [128, 128]),
            sgln=di("sgln", [L, 128, 2, 512]), sgwT=di("sgwT", [L, 128, 8, 128]), sgb=di("sgb", [L, 128, 4, 512]),
            wfin=di("wfin", [L, 512, 3072]), wout=di("wout", [L, D, D]), post=di("post", [L, 128, 3, D]),
        )
        self.y_d = nc.dram_tensor("y", [S, D], F32, kind="ExternalOutput").ap()
        self.dbg_d = {}
        self._nm = 0
        o = SB_BASE
        self.xtok = self.A("xtok", [128, 16, D], F32, o); o += 65536
        self.xT = self.A("xT", [128, 8, S], BF16, o); o += 32768
        self.R_y = o
        self.yA = self.A("yA", [128, 4, S], BF16, o); o += 16384
        self.yB = self.A("yB", [128, 4, S], BF16, o); o += 16384
        self.yC = self.A("yC", [128, 4, S], BF16, o); o += 16384
        self.ybr = [self.yA, self.yB, self.yC]
        self.ident = self.A("ident", [128, 128], BF16, o); o += 256
        self.c32 = self.A("c32", [128, 256], F32, o); o += 1024
        self.vecs = self.A("vecs", [128, 80], F32, o); o += 320
        self.stat = self.A("stat", [128, 64], F32, o); o += 256
        self.R_loc = o
        self.wslot = [self.A("wslot0", [128, 4608], BF16, o), self.A("wslot1", [128, 4608], BF16, o + 9216)]
        o += 18432
        self.R_ph = o
        self.psall = nc.alloc_psum_tensor("psall", [128, 4096], F32)
        self.ps = [self.psall[:, b * 512:(b + 1) * 512] for b in range(8)]
        self.pst = self.psall[:, 3584:4096].bitcast(BF16)
        self.pbanks = [0, 1, 2, 3, 4, 5, 6]
        self.pb_i = 0
        self.wjobs = []
        for l in self.layers:
            self.wjobs += self.layer_wjobs(l)
        self.w_issued = 0
        self.w_used = 0
        self._xT_pending = None

    def A(self, name, shape, dtype, off):
        nbytes = int(np.prod(shape[1:])) * (4 if dtype == F32 else 2)
        assert off + nbytes <= SB_END, (name, off, nbytes)
        self._nm += 1
        return self.nc.alloc_sbuf_tensor_at("%s_%d" % (name, self._nm), list(shape), dtype, offset=off)

    def bank(self):
        b = self.pbanks[self.pb_i % len(self.pbanks)]
        self.pb_i += 1
        return b

    def mm(self, banks, out, lhsT, rhs, start, stop, reads):
        if isinstance(banks, int):
            banks = [banks]
        self.s.op("pe", lambda e: e.matmul(out, lhsT, rhs, start=start, stop=stop), reads=reads,
                  excl=[("ps", b) for b in banks])

    def act(self, out, in_, func, reads=(), writes=(), excl=(), **kw):
        self.s.op("act", lambda e: e.activation(out, in_, func, **kw), reads=reads, writes=writes, excl=excl)

    def tt(self, eng, out, in0, in1, op, reads=(), writes=(), excl=()):
        self.s.op(eng, lambda e: e.tensor_tensor(out, in0, in1, op), reads=reads, writes=writes, excl=excl)

    def ts(self, eng, out, in0, s1, s2, op0, op1=None, reads=(), writes=(), excl=()):
        if op1 is None:
            self.s.op(eng, lambda e: e.tensor_scalar(out, in0, s1, None, op0), reads=reads, writes=writes, excl=excl)
        else:
            self.s.op(eng, lambda e: e.tensor_scalar(out, in0, s1, s2, op0, op1), reads=reads, writes=writes, excl=excl)

    def stt(self, eng, out, in0, scalar, in1, op0, op1, reads=(), writes=(), excl=()):
        self.s.op(eng, lambda e: e.scalar_tensor_tensor(out, in0, scalar, in1, op0, op1), reads=reads, writes=writes,
                  excl=excl)

    def cp(self, eng, out, in_, reads=(), writes=(), excl=()):
        self.s.op(eng, lambda e: e.tensor_copy(out, in_), reads=reads, writes=writes, excl=excl)

    def dma(self, eng, out, in_, reads=(), writes=()):
        self.s.dma(eng, lambda e: e.dma_start(out=out, in_=in_), reads=reads, writes=writes)

    def layer_wjobs(self, l):
        w = self.d["w_in_p"]
        jobs = []
        for c in range(4):
            jobs.append([(0, 8, 512, w[l, :, c * 512:(c + 1) * 512])])
        g0 = CH_GK[0] * 128
        jobs.append([(0, 8, 384, w[l, :, g0:g0 + 384])])
        for c in range(4):
            c0 = CH_GQ(c, 0) * 128
            jobs.append([(0, 8, 256, w[l, :, c0:c0 + 256])])
        c0 = CH_SV * 128
        jobs.append([(0, 8, 512, w[l, :, c0:c0 + 512])])
        for c in range(4):
            c0 = CH_SG(c, 0) * 128
            jobs.append([(0, 8, 256, w[l, :, c0:c0 + 256])])
        for dc in range(8):
            g0 = CH_GATE(dc, 0) * 128
            jobs.append([(0, 8, 384, w[l, :, g0:g0 + 384]),
                         (3072, 4, 384, self.d["wfin"][l, :, dc * 384:(dc + 1) * 384])])
        for hf in range(2):
            jobs.append([(0, 8, 512, self.d["wout"][l, :, hf * 512:(hf + 1) * 512])])
        return jobs

    def _issue_w(self):
        k = self.w_issued
        if k >= len(self.wjobs):
            return
        self.w_issued += 1
        i = k % 2
        for (off, kch, ncols, src) in self.wjobs[k]:
            dst = self.wslot[i][:, off:off + kch * ncols].rearrange("p (k n) -> p k n", n=ncols)
            self.dma("pool", dst, src.rearrange("(k p) n -> p k n", p=128), writes=[("w", i)])

    def take_w(self):
        k = self.w_used
        self.w_used += 1
        while self.w_issued <= min(k + 1, len(self.wjobs) - 1):
            self._issue_w()
        i = k % 2
        views = []
        for (off, kch, ncols, src) in self.wjobs[k]:
            views.append(self.wslot[i][:, off:off + kch * ncols].rearrange("p (k n) -> p k n", n=ncols))
        return views, ("w", i)

    def xT_res(self, tc):
        return [("xT", 4 * tc + i) for i in range(4)]

    def proj_fm(self, wt, wres, col0, evac):
        for tc in range(4):
            b = self.bank()
            for kc in range(8):
                self.mm(b, self.ps[b][:, :], wt[:, kc, col0:col0 + 128], self.xT[:, kc, tc * 512:(tc + 1) * 512],
                        kc == 0, kc == 7, reads=[wres] + self.xT_res(tc))
            evac(tc, b)

    def bias(self, ch):
        return self.vecs[:, ch:ch + 1]

    def ln_stats(self, src, src_res, tt_i):
        st = self.stat
        k = (tt_i % 4) * 16
        for h in range(2):
            self.s.op("dve", (lambda h: lambda e: e.bn_stats(st[:, k + h * 6:k + h * 6 + 6],
                                                               src[:, h * 512:(h + 1) * 512]))(h),
                      reads=[src_res], writes=[("stat", tt_i % 4)])
        self.s.op("dve", lambda e: e.bn_aggr(st[:, k + 12:k + 14], st[:, k:k + 12]),
                  reads=[("stat", tt_i % 4)], writes=[("stat", tt_i % 4)])
        self.act(st[:, k + 14:k + 15], st[:, k + 13:k + 14], AF.Sqrt, bias=1e-5,
                 reads=[("stat", tt_i % 4)], writes=[("stat", tt_i % 4)])

    def ln_apply(self, src, src_res, tt_i, g_ap, b_ap, gb_res):
        st = self.stat
        k = (tt_i % 4) * 16
        self.s.op("dve", lambda e: e.reciprocal(st[:, k + 14:k + 15], st[:, k + 14:k + 15]),
                  reads=[("stat", tt_i % 4)], writes=[("stat", tt_i % 4)])
        xt = self.xtok[:, tt_i, :]
        self.stt("dve", xt, src, st[:, k + 12:k + 13], g_ap, ALU.subtract, ALU.mult,
                 reads=[src_res, ("stat", tt_i % 4), gb_res], writes=[("xtok", tt_i)])
        self.stt("dve", xt, xt, st[:, k + 14:k + 15], b_ap, ALU.mult, ALU.add,
                 reads=[("xtok", tt_i), ("stat", tt_i % 4), gb_res], writes=[("xtok", tt_i)])

    def to_xT(self, tt_i, tmp_h16):
        if self._xT_pending is not None:
            self.to_xT_finish()
        k = tt_i % 2
        h16 = tmp_h16[k]
        self.act(h16[:, :], self.xtok[:, tt_i, :], AF.Copy, reads=[("xtok", tt_i)], writes=[("h16", k)])
        for kc in range(8):
            self.s.op("pe", (lambda kc: lambda e: e.transpose(self.pst[:, kc * 128:(kc + 1) * 128],
                                                              h16[:, kc * 128:(kc + 1) * 128], self.ident[:, :]))(kc),
                      reads=[("h16", k), ("ident",)], excl=[("ps", 7)])
        self._xT_pending = tt_i

    def to_xT_finish(self):
        tt_i = self._xT_pending
        if tt_i is None:
            return
        self._xT_pending = None
        self.cp("dve", self.xT[:, :, tt_i * 128:(tt_i + 1) * 128], self.pst[:, :].rearrange("p (k n) -> p k n", n=128),
                writes=[("xT", tt_i)], excl=[("ps", 7)])

    def phase0(self):
        o = self.R_ph
        gb = self.A("lnin", [128, 2, D], F32, o); o += 8192
        xin = [self.A("xin%d" % i, [128, D], F32, o + 4096 * i) for i in range(3)]; o += 12288
        h16 = [self.A("h16a", [128, D], BF16, o), self.A("h16b", [128, D], BF16, o + 2048)]; o += 4096
        self.dma("pool", self.ident[:, :], self.d["ident"], writes=[("ident",)])
        self.dma("sp", self.c32[:, :], self.d["c32"], writes=[("c32",)])
        self._issue_w()
        if self.do_ln_in:
            self.dma("sp", gb[:, :, :], self.d["lnin"], writes=[("lnin",)])

            def load(t):
                self.dma("sp", xin[t % 3][:, :], self.d["x"][t * 128:(t + 1) * 128, :], writes=[("xin", t % 3)])
            load(0)
            load(1)
            self.ln_stats(xin[0][:, :], ("xin", 0), 0)
            for t in range(16):
                if t + 2 < 16:
                    load(t + 2)
                if t + 1 < 16:
                    self.ln_stats(xin[(t + 1) % 3][:, :], ("xin", (t + 1) % 3), t + 1)
                self.ln_apply(xin[t % 3][:, :], ("xin", t % 3), t, gb[:, 0, :], gb[:, 1, :], ("lnin",))
                self.to_xT_finish()
                self.to_xT(t, h16)
            self.to_xT_finish()
        else:
            for tt_i in range(16):
                self.dma("sp", self.xtok[:, tt_i, :], self.d["x"][tt_i * 128:(tt_i + 1) * 128, :], writes=[("xtok", tt_i)])
                self.to_xT(tt_i, h16)
            self.to_xT_finish()

    def attn_pipeline(self, units, Pb, exb, regions, scale, look, s_order=None):
        nu = len(units)
        nr = len(regions)
        nb = len(Pb)
        mulcount = [0]

        def emitS(u, extra=()):
            banks = regions[u % nr]
            offs = []
            off = 0
            for t in units[u]:
                offs.append(off)
                off += t["n"]
            order = list(range(len(units[u]))) if s_order is None or len(units[u]) != len(s_order) else list(s_order)
            for c0 in range(0, len(order), 2):
                grp = order[c0:c0 + 2]
                for i in grp:
                    t = units[u][i]
                    n = t["n"]
                    base = banks[0] * 512 + offs[i]
                    self.mm([banks[offs[i] // 512]], self.psall[:, base:base + n], t["s_lhsT"], t["s_rhs"], True,
                            t.get("b_rhs") is None, reads=list(t["s_reads"]) + list(extra))
                for i in grp:
                    t = units[u][i]
                    if t.get("b_rhs") is None:
                        continue
                    n = t["n"]
                    base = banks[0] * 512 + offs[i]
                    self.mm([banks[offs[i] // 512]], self.psall[:, base:base + n], self.ident[:, :], t["b_rhs"], False, True,
                            reads=[("ident",), t["b_res"]])

        for u in range(min(look, nu)):
            emitS(u)
        for u in range(nu):
            banks = regions[u % nr]
            k = u % nb
            tot = sum(t["n"] for t in units[u])
            base = banks[0] * 512
            src = self.psall[:, base:base + tot]
            ex = [("ps", b) for b in banks]
            if units[u][0]["etab"] is None:
                self.act(Pb[k][:, 0:tot], src, AF.Exp, scale=scale,
                         writes=[("P", k, i) for i in range(len(units[u]))], excl=ex)
            else:
                self.act(exb[k][:, 0:tot], src, AF.Exp, scale=scale, writes=[("ex", k)], excl=ex)
                off = 0
                for i, t in enumerate(units[u]):
                    n = t["n"]
                    eng = "pool" if (POOL_EVERY and mulcount[0] % POOL_EVERY == POOL_EVERY - 1) else "dve"
                    mulcount[0] += 1
                    self.tt(eng, Pb[k][:, off:off + n], exb[k][:, off:off + n], t["etab"], ALU.mult,
                            reads=[("ex", k), t["etab_res"]], writes=[("P", k, i)])
                    off += n
            if u + look < nu:
                emitS(u + look)
            off = 0
            for i, t in enumerate(units[u]):
                n = t["n"]
                ob = t["ob"]
                self.mm(ob, self.ps[ob][:, 0:n], t["v_lhsT"], Pb[k][:, off:off + n], t["first"], t["last"],
                        reads=[("P", k, i)] + t["v_reads"])
                off += n
                for _ in range(DUMMY_MM):
                    self.mm(6, self.ps[6][:, 0:n], self.ident[:, :], Pb[k][:, 0:n], True, True, reads=[("P", k, i), ("ident",)])
                if t["last"] and t["epi"] is not None:
                    t["epi"]()

    def attn_epilogue(self, ob, n, hb, rs, sz_ap, sz_res, y_ap, y_res, slot):
        so = 64 - hb
        ps = self.ps[ob]
        self.s.op("dve", lambda e: e.reciprocal(rs[hb:hb + 64, 0:n], ps[so:so + 64, 0:n]),
                  writes=[("rs", slot)], excl=[("ps", ob)])
        self.tt("pool", rs[hb:hb + 64, 0:n], rs[hb:hb + 64, 0:n], sz_ap, ALU.mult,
                reads=[("rs", slot), sz_res], writes=[("rs", slot)])
        self.tt("dve", y_ap, ps[hb:hb + 64, 0:n], rs[hb:hb + 64, 0:n], ALU.mult,
                reads=[("rs", slot)], writes=[y_res], excl=[("ps", ob)])

    def phaseA(self, l):
        o = self.R_ph
        qT = self.A("naq", [128, S], BF16, o); o += 4096
        kT = self.A("nak", [128, S], BF16, o); o += 4096
        sz = self.A("nasz", [128, S], BF16, o); o += 4096
        Va = self.A("nava", [128, 16, 2, 128], BF16, o); o += 8192
        exb = None
        Pb = [self.A("P%d" % i, [128, 1024], BF16, o + 2048 * i) for i in range(3)]; o += 6144
        rs = [[self.A("rs%d%d" % (i, h), [128, 256], F32, o + 2048 * i + 1024 * h) for h in range(2)] for i in range(2)]
        o += 4096
        bv = self.A("bvna", [128, 512], F32, o); o += 2048
        oa = self.R_y + 32768
        tabraw = self.A("tabraw", [128, 1792], F32, oa); oa += 7168
        E = [self.A("E%d" % i, [128, 1792], BF16, oa + 3584 * i) for i in range(2)]; oa += 7168
        regions = [[0, 1], [2, 3]]
        obanks = [(4, 5), (6, 7)]
        self.pbanks = [0, 1, 2, 3, 4, 5, 6, 7]
        d = self.d
        self.dma("sp", bv[:, :], d["bvb"][l, :, 0:512], writes=[("bvna",)])
        self.s.op("pool", lambda e: e.memset(Va[:, :, 0, 64:128], 1.0), writes=[("nava", j) for j in range(16)])
        self.s.op("pool", lambda e: e.memset(Va[:, :, 1, 0:64], 1.0), writes=[("nava", j) for j in range(16)])
        hcount = 0
        ecount = 0
        for c in range(4):
            (wt,), wres = self.take_w()

            def ev_q(tc, b, dst=qT, ch=CH_NA(c, 0), nm="naq"):
                self.ts("dve", dst[:, tc * 512:(tc + 1) * 512], self.ps[b][:, :], self.bias(ch), None, ALU.add,
                        reads=[("vecs",)], writes=[(nm, tc)], excl=[("ps", b)])
            self.proj_fm(wt, wres, 0, ev_q)
            self.proj_fm(wt, wres, 128, lambda tc, b: ev_q(tc, b, kT, CH_NA(c, 1), "nak"))

            def ev_z(tc, b, ch=CH_NA(c, 3)):
                self.act(sz[:, tc * 512:(tc + 1) * 512], self.ps[b][:, :], AF.Silu, bias=self.bias(ch),
                         reads=[("vecs",)], writes=[("nasz", tc)], excl=[("ps", b)])
            self.proj_fm(wt, wres, 384, ev_z)
            for j0 in range(0, 16, 4):
                b = self.bank()
                for t in range(4):
                    for kc in range(8):
                        self.mm(b, self.ps[b][:, t * 128:(t + 1) * 128], self.xT[:, kc, (j0 + t) * 128:(j0 + t + 1) * 128],
                                wt[:, kc, 256:384], kc == 0, kc == 7, reads=[wres, ("xT", j0 + t)])
                pv = self.ps[b][:, :].rearrange("p (a n) -> p a n", n=128)
                for hi in range(2):
                    bsl = bv[:, c * 128 + hi * 64:c * 128 + hi * 64 + 64].unsqueeze(1).broadcast_to([128, 4, 64])
                    self.tt("dve", Va[:, j0:j0 + 4, hi, hi * 64:hi * 64 + 64], pv[:, :, hi * 64:hi * 64 + 64], bsl, ALU.add,
                            reads=[("bvna",)], writes=[("nava", j0 + t) for t in range(4)], excl=[("ps", b)])
            units = []
            eks = []
            for hi in range(2):
                ek = ecount % 2
                ecount += 1
                eks.append(ek)
                self.dma("sp", tabraw[:, :], d["natab"][l, 2 * c + hi], writes=[("tabraw",)])
                self.act(E[ek][:, :], tabraw[:, :], AF.Identity, scale=8.0, reads=[("tabraw",)], writes=[("E", ek)])
            for q8 in range(8):
                r0 = 4 * q8
                if q8 == 0:
                    jl, kind = [0, 1, 2, 3], 0
                elif q8 == 7:
                    jl, kind = [12, 13, 14, 15], 0
                else:
                    jl, kind = list(range(2 * q8 - 2, 2 * q8 + 4)), 1
                obs = obanks[hcount % 2]
                slot = hcount % 2
                hcount += 1
                q0 = q8 * 256
                tcq = q0 // 512
                tl = {0: [], 1: []}
                for hi in range(2):
                    hb = 64 * hi
                    ek = eks[hi]
                    ob = obs[hi]

                    def epi(ob=ob, hb=hb, slot=slot, hi=hi, q0=q0, tcq=tcq, c=c):
                        self.attn_epilogue(ob, 256, hb, rs[slot][hi], sz[hb:hb + 64, q0:q0 + 256], ("nasz", tcq),
                                           self.yA[hb:hb + 64, c, q0:q0 + 256], ("yA", c, tcq), (slot, hi))
                    for idx, j in enumerate(jl):
                        s0 = r0 - 2 * j + 8
                        assert 2 <= s0 and s0 + 4 <= 16
                        tl[hi].append(dict(
                            n=256, s_lhsT=kT[hb:hb + 64, j * 128:(j + 1) * 128], s_rhs=qT[hb:hb + 64, q0:q0 + 256],
                            s_reads=[("nak", j // 4), ("naq", tcq)],
                            etab=None, etab_res=None,
                            b_rhs=E[ek][:, kind * 896 + (s0 - 2) * 64:kind * 896 + (s0 + 2) * 64], b_res=("E", ek),
                            v_lhsT=Va[:, j, hi, :], v_reads=[("nava", j)], ob=ob,
                            first=(idx == 0), last=(idx == len(jl) - 1), epi=epi if idx == len(jl) - 1 else None))
                for i in range(0, len(jl), 2):
                    units.append([tl[0][i], tl[0][i + 1], tl[1][i], tl[1][i + 1]])
            self.attn_pipeline(units, Pb, exb, regions, 0.125, look=2, s_order=[0, 2, 1, 3])
        if self.dbg:
            self.dbg_dump("yA%d" % l, self.yA, [("yA", c, t) for c in range(4) for t in range(4)])

    def dbg_dump(self, name, t, reads):
        shp = list(t.shape)
        dt = t.dtype
        dd = self.nc.dram_tensor("dbg_" + name, shp, dt, kind="ExternalOutput").ap()
        self.dbg_d[name] = dd
        self.dma("sp", dd, t[tuple(slice(None) for _ in shp)], reads=reads)

    def barrier(self):
        for e in Sched.ENGS:
            self.s.wait_all(e, dmas=False)

    def phaseB(self, l):
        o = self.R_ph
        kT2 = self.A("gk", [128, 2, S], BF16, o); o += 8192
        qT = self.A("gq", [128, S], BF16, o); o += 4096
        sz = self.A("gsz", [128, S], BF16, o); o += 4096
        ropeb = [self.A("rope", [128, 2, 512], F32, o)] * 2; o += 4096
        sq2 = [self.A("sq%d" % i, [128, 512], F32, o + 2048 * i) for i in range(2)]; o += 4096
        qb2 = [self.A("qb%d" % i, [128, 512], F32, o + 2048 * i) for i in range(2)]; o += 4096
        u2 = [self.A("u%d" % i, [128, 512], F32, o + 2048 * i) for i in range(2)]; o += 4096
        rstd2 = [self.A("rstd", [128, 512], F32, o)] * 2; o += 2048
        t12 = [self.A("t1", [128, 512], F32, o)] * 2; o += 2048
        bv = self.A("bvg", [128, 128], F32, o); o += 512
        oa = self.R_y + 32768
        Vg = self.A("gv", [128, 16, 2, 192], BF16, oa); oa += 12288
        rs = [[self.A("grs%d%d" % (i, h), [128, 256], F32, oa + 2048 * i + 1024 * h) for h in range(2)] for i in range(2)]
        oa += 4096
        Pb = [self.A("gP%d" % i, [128, 1024], BF16, o + 2048 * i) for i in range(3)]; o += 6144
        assert o <= SB_END, o
        regions = [[0, 1], [2, 3]]
        obanks = [(4, 5), (6, 7)]
        self.pbanks = [0, 1, 2, 3, 4, 5, 6, 7]
        d = self.d
        c32 = self.c32
        self.dma("sp", bv[:, :], d["bvb"][l, :, 512:640], writes=[("bvg",)])
        self.s.op("pool", lambda e: e.memset(Vg[:, :, :, 0:64], 1.0), writes=[("gv", j) for j in range(16)])
        self.s.op("pool", lambda e: e.memset(Vg[:, :, :, 128:192], 1.0), writes=[("gv", j) for j in range(16)])
        self.rope_i = 0

        def rope_stage1(tc, b, bias_ch, gain_col, dst_ap, dst_res):
            rk = self.rope_i % 2
            self.rope_i += 1
            sq, qb, u, rstd, t1, t2 = sq2[rk], qb2[rk], u2[rk], rstd2[rk], t12[rk], sq2[rk]
            R = lambda nm: ("sq", rk) if nm == "t2" else (nm, rk if nm in ("qb", "sq", "u") else 0)
            ps = self.ps[b]
            self.act(sq[:, :], ps[:, :], AF.Square, bias=self.bias(bias_ch), reads=[("vecs",)], writes=[R("sq")],
                     excl=[("ps", b)])
            self.ts("dve", u[:, :], ps[:, :], self.bias(bias_ch), self.vecs[:, gain_col:gain_col + 1], ALU.add, ALU.mult,
                    reads=[("vecs",)], writes=[R("u")], excl=[("ps", b)])

            def stage2():
                self.dma("sp", ropeb[rk][:, :, :], d["rope"][:, :, tc * 512:(tc + 1) * 512], writes=[("rope", 0)])
                b2 = self.bank()
                self.mm(b2, self.ps[b2][:, :], c32[:, 128:256], sq[:, :], True, True, reads=[R("sq"), ("c32",)])
                b3 = self.bank()
                self.mm(b3, self.ps[b3][:, :], c32[:, 0:128], u[:, :], True, True, reads=[R("u"), ("c32",)])
                self.act(rstd[:, :], self.ps[b2][:, :], AF.Ln, bias=64e-6, writes=[R("rstd")], excl=[("ps", b2)])
                self.act(rstd[:, :], rstd[:, :], AF.Exp, scale=-0.5, reads=[R("rstd")], writes=[R("rstd")])
                self.tt("pool", t1[:, :], u[:, :], ropeb[rk][:, 0, :], ALU.mult, reads=[R("u"), ("rope", 0)], writes=[R("t1")])
                self.tt("dve", t2[:, :], self.ps[b3][:, :], ropeb[rk][:, 1, :], ALU.mult, reads=[("rope", 0)],
                        writes=[R("t2")], excl=[("ps", b3)])
                self.tt("dve", t1[:, :], t1[:, :], t2[:, :], ALU.add, reads=[R("t1"), R("t2")], writes=[R("t1")])
                self.tt("dve", dst_ap, t1[:, :], rstd[:, :], ALU.mult, reads=[R("t1"), R("rstd")], writes=[dst_res])
            return stage2

        def proj_chunk(wt, wres, col0, tc):
            b = self.bank()
            for kc in range(8):
                self.mm(b, self.ps[b][:, :], wt[:, kc, col0:col0 + 128], self.xT[:, kc, tc * 512:(tc + 1) * 512],
                        kc == 0, kc == 7, reads=[wres] + self.xT_res(tc))
            return b

        (wt,), wres = self.take_w()
        pend = None
        for g in range(2):
            for tc in range(4):
                b = proj_chunk(wt, wres, g * 128, tc)
                st2 = rope_stage1(tc, b, CH_GK[g], 65, kT2[:, g, tc * 512:(tc + 1) * 512], ("gk", g, tc))
                if pend is not None:
                    pend()
                pend = st2
        kv_pend = pend
        for j0 in range(0, 16, 4):
            b = self.bank()
            for t in range(4):
                for kc in range(8):
                    self.mm(b, self.ps[b][:, t * 128:(t + 1) * 128], self.xT[:, kc, (j0 + t) * 128:(j0 + t + 1) * 128],
                            wt[:, kc, 256:384], kc == 0, kc == 7, reads=[wres, ("xT", j0 + t)])
            pv = self.ps[b][:, :].rearrange("p (a n) -> p a n", n=128)
            for g in range(2):
                bsl = bv[:, g * 64:g * 64 + 64].unsqueeze(1).broadcast_to([128, 4, 64])
                self.tt("dve", Vg[:, j0:j0 + 4, g, 64:128], pv[:, :, g * 64:g * 64 + 64], bsl, ALU.add,
                        reads=[("bvg",)], writes=[("gv", j0 + t) for t in range(4)], excl=[("ps", b)])
        kv_pend()
        hcount = 0
        for c in range(4):
            g = c // 2
            (wt,), wres = self.take_w()
            pend = None
            for tc in range(4):
                b = proj_chunk(wt, wres, 0, tc)
                st2 = rope_stage1(tc, b, CH_GQ(c, 0), 64, qT[:, tc * 512:(tc + 1) * 512], ("gq", tc))
                if pend is not None:
                    pend()
                pend = st2
            q_pend = pend

            def ev_z(tc, b, ch=CH_GQ(c, 1)):
                self.act(sz[:, tc * 512:(tc + 1) * 512], self.ps[b][:, :], AF.Silu, bias=self.bias(ch),
                         reads=[("vecs",)], writes=[("gsz", tc)], excl=[("ps", b)])
            self.proj_fm(wt, wres, 128, ev_z)
            q_pend()
            units = []
            for q8 in range(8):
                q0 = q8 * 256
                tcq = q0 // 512
                obs = obanks[hcount % 2]
                slot = hcount % 2
                hcount += 1
                tl = {0: [], 1: []}
                for hi in range(2):
                    hb = 64 * hi
                    ob = obs[hi]

                    def epi(ob=ob, hb=hb, slot=slot, hi=hi, q0=q0, tcq=tcq, c=c):
                        self.attn_epilogue(ob, 256, hb, rs[slot][hi], sz[hb:hb + 64, q0:q0 + 256], ("gsz", tcq),
                                           self.yB[hb:hb + 64, c, q0:q0 + 256], ("yB", c, tcq), (slot, hi))
                    for j in range(16):
                        vl = Vg[:, j, g, 64:192] if hi == 0 else Vg[:, j, g, 0:128]
                        tl[hi].append(dict(
                            n=256, s_lhsT=kT2[hb:hb + 64, g, j * 128:(j + 1) * 128], s_rhs=qT[hb:hb + 64, q0:q0 + 256],
                            s_reads=[("gk", g, j // 4), ("gq", tcq)], etab=None, etab_res=None,
                            v_lhsT=vl, v_reads=[("gv", j)], ob=ob, first=(j == 0), last=(j == 15),
                            epi=epi if j == 15 else None))
                for i in range(0, 16, 2):
                    units.append([tl[0][i], tl[0][i + 1], tl[1][i], tl[1][i + 1]])
            self.attn_pipeline(units, Pb, None, regions, 8.0, look=2, s_order=[0, 2, 1, 3])
        if self.dbg:
            self.dbg_dump("yB%d" % l, self.yB, [("yB", c, t) for c in range(4) for t in range(4)])

    def phaseC(self, l):
        o = self.R_ph
        vn = self.A("vn", [128, 16, 512], BF16, o); o += 16384
        sgw = self.A("sgw", [128, 8, 128], BF16, o); o += 2048
        vraw = [self.A("vraw%d" % i, [128, 512], F32, o + 2048 * i) for i in range(2)]; o += 4096
        sgln = self.A("sgln", [128, 2, 512], F32, o); o += 4096
        bvs = self.A("bvs", [128, 512], F32, o); o += 2048
        uzb = [self.A("uz%d" % i, [128, 512], F32, o + 2048 * i) for i in range(2)]; o += 4096
        szt = [self.A("szt%d" % i, [128, 512], F32, o + 2048 * i) for i in range(2)]; o += 4096
        sgb = [self.A("sgb%d" % i, [128, 512], F32, o + 2048 * i) for i in range(2)]; o += 4096
        mix = [self.A("mix%d" % i, [128, 512], F32, o + 2048 * i) for i in range(2)]; o += 4096
        assert o <= SB_END, o
        self.pbanks = [0, 1, 2, 3, 4, 5, 6]
        d = self.d
        st = self.stat
        self.dma("sp", bvs[:, :], d["bvb"][l, :, 640:1152], writes=[("bvs",)])
        self.dma("sp", sgln[:, :, :], d["sgln"][l], writes=[("sgln",)])
        self.dma("pool", sgw[:, :, :], d["sgwT"][l], writes=[("sgw",)])
        (wt,), wres = self.take_w()
        for tt_i in range(16):
            b = self.bank()
            k = tt_i % 2
            for kc in range(8):
                self.mm(b, self.ps[b][:, :], self.xT[:, kc, tt_i * 128:(tt_i + 1) * 128], wt[:, kc, 0:512],
                        kc == 0, kc == 7, reads=[wres, ("xT", tt_i)])
            vr = vraw[k]
            self.tt("dve", vr[:, :], self.ps[b][:, :], bvs[:, :], ALU.add, reads=[("bvs",)], writes=[("vraw", k)],
                    excl=[("ps", b)])
            sk = 16 * k
            self.s.op("dve", (lambda vr, sk: lambda e: e.bn_stats(st[:, sk:sk + 6], vr[:, :]))(vr, sk),
                      reads=[("vraw", k)], writes=[("stat", k)])
            self.s.op("dve", (lambda sk: lambda e: e.bn_aggr(st[:, sk + 12:sk + 14], st[:, sk:sk + 6]))(sk),
                      reads=[("stat", k)], writes=[("stat", k)])
            self.act(st[:, sk + 14:sk + 15], st[:, sk + 13:sk + 14], AF.Sqrt, bias=1e-5,
                     reads=[("stat", k)], writes=[("stat", k)])
            self.s.op("dve", (lambda sk: lambda e: e.reciprocal(st[:, sk + 14:sk + 15], st[:, sk + 14:sk + 15]))(sk),
                      reads=[("stat", k)], writes=[("stat", k)])
            self.stt("dve", vr[:, :], vr[:, :], st[:, sk + 12:sk + 13], sgln[:, 0, :], ALU.subtract, ALU.mult,
                     reads=[("vraw", k), ("stat", k), ("sgln",)], writes=[("vraw", k)])
            self.stt("dve", vn[:, tt_i, :], vr[:, :], st[:, sk + 14:sk + 15], sgln[:, 1, :], ALU.mult, ALU.add,
                     reads=[("vraw", k), ("stat", k), ("sgln",)], writes=[("vn", tt_i)])
        cnt = 0
        for c in range(4):
            (wt,), wres = self.take_w()
            self.dma("sp", sgb[c % 2][:, :], d["sgb"][l, :, c, :], writes=[("sgb", c % 2)])
            for tc in range(4):
                bu = self.bank()
                for kc in range(8):
                    self.mm(bu, self.ps[bu][:, :], wt[:, kc, 0:128], self.xT[:, kc, tc * 512:(tc + 1) * 512],
                            kc == 0, kc == 7, reads=[wres] + self.xT_res(tc))
                bz = self.bank()
                for kc in range(8):
                    self.mm(bz, self.ps[bz][:, :], wt[:, kc, 128:256], self.xT[:, kc, tc * 512:(tc + 1) * 512],
                            kc == 0, kc == 7, reads=[wres] + self.xT_res(tc))
                k = cnt % 2
                cnt += 1
                self.act(szt[k][:, :], self.ps[bz][:, :], AF.Silu, bias=self.bias(CH_SG(c, 1)), reads=[("vecs",)],
                         writes=[("szt", k)], excl=[("ps", bz)])
                uzs = uzb[k][:, :]
                self.stt("dve", uzs, self.ps[bu][:, :], self.bias(CH_SG(c, 0)), szt[k][:, :], ALU.add, ALU.mult,
                         reads=[("vecs",), ("szt", k)], writes=[("uz", k)], excl=[("ps", bu)])
                bm = [self.bank(), self.bank()]
                for gi in range(2):
                    for t in range(4):
                        tt_i = 4 * tc + t
                        self.mm(bm[gi], self.ps[bm[gi]][:, t * 128:(t + 1) * 128], vn[:, tt_i, c * 128:(c + 1) * 128],
                                sgw[:, 2 * c + gi, :], True, True, reads=[("vn", tt_i), ("sgw",)])
                mk = mix[k]
                for gi in range(2):
                    hb = 64 * gi
                    self.tt("dve", mk[hb:hb + 64, :], self.ps[bm[gi]][hb:hb + 64, :], sgb[c % 2][hb:hb + 64, :], ALU.add,
                            reads=[("sgb", c % 2)], writes=[("mix", k)], excl=[("ps", bm[gi])])
                self.tt("pool", self.yC[:, c, tc * 512:(tc + 1) * 512], mk[:, :], uzs, ALU.mult,
                        reads=[("mix", k), ("uz", k)], writes=[("yC", c, tc)])
        if self.dbg:
            self.dbg_dump("yC%d" % l, self.yC, [("yC", c, t) for c in range(4) for t in range(4)])

    def phaseD(self, l, last):
        o = self.R_ph
        mT = self.A("mT", [128, 8, S], BF16, o); o += 32768
        gsig = [self.A("gsig%d" % i, [128, 512], F32, o + 2048 * i) for i in range(2)]; o += 4096
        mt = [self.A("mt%d" % i, [128, 512], F32, o + 2048 * i) for i in range(3)]; o += 6144
        assert o <= SB_END, o
        self.pbanks = [0, 1, 2, 3, 4, 5, 6]
        d = self.d
        gcnt = 0
        for dc in range(8):
            (wg, wb), wres = self.take_w()
            for tc in range(4):
                for br in range(3):
                    bg = self.bank()
                    for kc in range(8):
                        self.mm(bg, self.ps[bg][:, :], wg[:, kc, br * 128:(br + 1) * 128],
                                self.xT[:, kc, tc * 512:(tc + 1) * 512], kc == 0, kc == 7, reads=[wres] + self.xT_res(tc))
                    k = gcnt % 2
                    gcnt += 1
                    self.act(gsig[k][:, :], self.ps[bg][:, :], AF.Sigmoid, bias=self.bias(CH_GATE(dc, br)),
                             reads=[("vecs",)], writes=[("gsig", k)], excl=[("ps", bg)])
                    bp = self.bank()
                    yb = self.ybr[br]
                    nm = ("yA", "yB", "yC")[br]
                    for kc in range(4):
                        self.mm(bp, self.ps[bp][:, :], wb[:, kc, br * 128:(br + 1) * 128], yb[:, kc, tc * 512:(tc + 1) * 512],
                                kc == 0, kc == 3, reads=[wres, (nm, kc, tc)])
                    mb = mt[0] if br == 0 else mt[1]
                    mk = ("mt", 0 if br == 0 else 1)
                    self.tt("dve", mb[:, :], self.ps[bp][:, :], gsig[k][:, :], ALU.mult, reads=[("gsig", k)],
                            writes=[mk], excl=[("ps", bp)])
                    if br == 1:
                        self.tt("dve", mt[0][:, :], mt[0][:, :], mt[1][:, :], ALU.add, reads=[("mt", 0), ("mt", 1)],
                                writes=[("mt", 0)])
                    elif br == 2:
                        self.tt("dve", mT[:, dc, tc * 512:(tc + 1) * 512], mt[0][:, :], mt[1][:, :], ALU.add,
                                reads=[("mt", 0), ("mt", 1)], writes=[("mT", tc)])
        self.barrier()
        oa = self.R_y
        post = self.A("post", [128, 3, D], F32, oa); oa += 12288
        h16 = [self.A("dh16%d" % i, [128, D], BF16, oa + 2048 * i) for i in range(2)]; oa += 4096
        self.dma("sp", post[:, :, :], d["post"][l], writes=[("post",)])
        for hf in range(2):
            (wo,), wres = self.take_w()
            for tt_i in range(16):
                tcq = tt_i // 4
                b = self.bank()
                for dc in range(8):
                    self.mm(b, self.ps[b][:, :], mT[:, dc, tt_i * 128:(tt_i + 1) * 128], wo[:, dc, 0:512],
                            dc == 0, dc == 7, reads=[("mT", tcq), wres])
                x_ap = self.xtok[:, tt_i, hf * 512:(hf + 1) * 512]
                self.stt("dve", x_ap, x_ap, float(ALPHA), self.ps[b][:, :], ALU.mult, ALU.add,
                         reads=[("xtok", tt_i)], writes=[("xtok", tt_i)], excl=[("ps", b)])
                if hf == 1:
                    self.tt("pool", self.xtok[:, tt_i, :], self.xtok[:, tt_i, :], post[:, 0, :], ALU.add,
                            reads=[("xtok", tt_i), ("post",)], writes=[("xtok", tt_i)])
        self.ln_stats(self.xtok[:, 0, :], ("xtok", 0), 0)
        for t in range(16):
            if t + 1 < 16:
                self.ln_stats(self.xtok[:, t + 1, :], ("xtok", t + 1), t + 1)
            self.ln_apply(self.xtok[:, t, :], ("xtok", t), t, post[:, 1, :], post[:, 2, :], ("post",))
            if last:
                self.dma("sp", self.y_d[t * 128:(t + 1) * 128, :], self.xtok[:, t, :], reads=[("xtok", t)])
            else:
                self.to_xT_finish()
                self.to_xT(t, h16)
        self.to_xT_finish()
        self.barrier()

    def layer(self, l, last):
        self.dma("sp", self.vecs[:, :], self.d["vecs"][l], writes=[("vecs",)])
        self.phaseA(l)
        self.barrier()
        self.phaseB(l)
        self.barrier()
        self.phaseC(l)
        self.barrier()
        self.phaseD(l, last)

    def build(self):
        from contextlib import ExitStack
        nc = self.nc
        self.phase0()
        self.barrier()
        for i, l in enumerate(self.layers):
            self.layer(l, last=(i == len(self.layers) - 1))
        self.s.wait_all("sp")
        with ExitStack() as es:
            sems = {e: es.enter_context(nc.semaphore("s_" + e)) for e in Sched.ENGS}
            dsems = [es.enter_context(nc.semaphore("q%d" % i)) for i in range(self.s.n_dma)]
            block = es.enter_context(nc.Block())
            self.s.emit(nc, block, sems, dsems)
        return nc


def _run(layers, do_ln_in, x_list, shared, dbg=False):
    kb = KB(layers, do_ln_in, dbg)
    nc = kb.build()
    in_maps = []
    for xb in x_list:
        m = dict(shared)
        m["x"] = np.ascontiguousarray(xb, dtype=np.float32)
        in_maps.append(m)
    return run_bass_kernel_spmd(nc, in_maps, core_ids=list(range(len(x_list))))


def kernel(**inputs):
    x = np.asarray(inputs["x"], dtype=np.float32)
    shared = prep_inputs(**inputs)
    res = _run([0, 1], True, [x[b] for b in range(x.shape[0])], shared)
    return np.stack([r["y"] for r in res.results], axis=0).astype(np.float32)
```

```python
import numpy as np
import concourse.bass as bass
import concourse.mybir as mybir
from concourse.bass_utils import run_bass_kernel_spmd

F32, BF16 = mybir.dt.float32, mybir.dt.bfloat16
AF = mybir.ActivationFunctionType
ALU = mybir.AluOpType

S = 2048
D = 1024
L = 2
NCH = 63
WIN = NCH * 128
ALPHA = (2.0 * L) ** 0.25
SB_BASE = 16512
SB_END = 229376
EMBED_WAIT = ("act", "dve", "pool")


class Sched:
    ENGS = ("pe", "act", "dve", "pool", "sp")

    def __init__(self, n_dma_sems=24):
        self.prog = {e: [] for e in self.ENGS}
        self.serial = {e: 0 for e in self.ENGS}
        self.seen = {e: {} for e in self.ENGS}
        self.lastw = {}
        self.readers = {}
        self.lastx = {}
        self.waited = {e: set() for e in self.ENGS}
        self.n_dma = n_dma_sems
        self.dma_i = 0
        self.dma_ip = 0
        self.dma_val = [0] * n_dma_sems

    def _deps(self, eng, reads, writes, excl):
        need = {}

        def add(tok, raw, isx=False):
            key, val, teng = tok
            if teng == eng and isx:
                return
            if self.seen[eng].get(key, 0) >= val:
                return
            if need.get(key, 0) < val:
                need[key] = val

        for r in reads:
            t = self.lastw.get(r)
            if t:
                add(t, True)
        for w in writes:
            t = self.lastw.get(w)
            if t:
                add(t, False)
            for t in self.readers.get(w, {}).values():
                add(t, False)
        for x in excl:
            t = self.lastx.get(x)
            if t:
                add(t, False, True)
        for key, val in need.items():
            self.seen[eng][key] = val
            self.prog[eng].append(("wait", key, val))
            if key[0] != "q":
                self.waited[key].add(val)

    def _commit(self, tok, reads, writes, excl):
        for r in reads:
            self.readers.setdefault(r, {})[tok[0]] = tok
        for w in writes:
            self.lastw[w] = tok
            self.readers[w] = {}
        for x in excl:
            self.lastx[x] = tok

    def op(self, eng, fn, reads=(), writes=(), excl=()):
        self._deps(eng, reads, writes, excl)
        self.serial[eng] += 1
        tok = (eng, self.serial[eng], eng)
        self.prog[eng].append(("op", fn, self.serial[eng]))
        self._commit(tok, reads, writes, excl)
        return tok

    def dma(self, eng, fn, reads=(), writes=()):
        self._deps(eng, reads, writes, ())
        if eng == "pool":
            k = 16 + self.dma_ip % (self.n_dma - 16)
            self.dma_ip += 1
        else:
            k = self.dma_i % 16
            self.dma_i += 1
        key = "q%d" % k
        prev = self.dma_val[k]
        if prev > 0 and self.seen[eng].get(key, 0) < prev:
            self.seen[eng][key] = prev
            self.prog[eng].append(("wait", key, prev))
        self.dma_val[k] += 16
        tok = (key, self.dma_val[k], None)
        self.prog[eng].append(("dma", fn, k))
        self._commit(tok, reads, writes, ())
        return tok

    def wait_all(self, eng, dmas=True):
        for e in self.ENGS:
            if e != eng and self.serial[e] > 0 and self.seen[eng].get(e, 0) < self.serial[e]:
                self.seen[eng][e] = self.serial[e]
                self.prog[eng].append(("wait", e, self.serial[e]))
                self.waited[e].add(self.serial[e])
        for k in range(self.n_dma if dmas else 0):
            key = "q%d" % k
            if self.dma_val[k] > 0 and self.seen[eng].get(key, 0) < self.dma_val[k]:
                self.seen[eng][key] = self.dma_val[k]
                self.prog[eng].append(("wait", key, self.dma_val[k]))

    def emit(self, nc, block, sems, dsems):
        rank = {}
        for e in self.ENGS:
            ws = sorted(self.waited[e])
            rank[e] = {s: i + 1 for i, s in enumerate(ws)}
        handles = {"pe": "tensor", "act": "scalar", "dve": "vector", "pool": "gpsimd", "sp": "sync"}

        def run(ename):
            def body(eng):
                pend = []

                def semval(key, val):
                    if key[0] == "q":
                        return dsems[int(key[1:])], val
                    return sems[key], rank[key][val]

                for item in self.prog[ename]:
                    if item[0] == "wait":
                        pend.append(semval(item[1], item[2]))
                        continue
                    embed = None
                    if pend and item[0] == "op" and ename in EMBED_WAIT:
                        embed = pend.pop()
                    for (sm, v) in pend:
                        eng.wait_ge(sm, v)
                    pend = []
                    ins = item[1](eng)
                    if embed is not None:
                        ins._wait_ge(embed[0], embed[1])
                    if item[0] == "op":
                        if item[2] in rank[ename]:
                            ins.then_inc(sems[ename], 1)
                    else:
                        ins.then_inc(dsems[item[2]], 16)
                for (sm, v) in pend:
                    eng.wait_ge(sm, v)
            return body

        for ename in self.ENGS:
            getattr(block, handles[ename])(run(ename))


def _win_cols():
    naq, nak, nav, naz, gqq, gqk, gqv, gqz, sgu, sgv, sgz, gat = (0, 512, 1024, 1536, 2048, 2560, 2688, 2816,
                                                                 3328, 3840, 4352, 4864)
    r = lambda a, n: list(range(a, a + n))
    cols = []
    for c in range(4):
        cols += r(naq + c * 128, 128) + r(nak + c * 128, 128) + r(nav + c * 128, 128) + r(naz + c * 128, 128)
    cols += r(gqk, 64) + r(gqk, 64) + r(gqk + 64, 64) + r(gqk + 64, 64) + r(gqv, 128)
    for c in range(4):
        cols += r(gqq + c * 128, 128) + r(gqz + c * 128, 128)
    cols += r(sgv, 512)
    for c in range(4):
        cols += r(sgu + c * 128, 128) + r(sgz + c * 128, 128)
    for dc in range(8):
        for b in range(3):
            cols += r(gat + b * 1024 + dc * 128, 128)
    assert len(cols) == WIN
    return np.asarray(cols)


def CH_NA(c, which):
    return 4 * c + which
CH_GK = (16, 17)
CH_GV = 18
def CH_GQ(c, which):
    return 19 + 2 * c + which
CH_SV = 27
def CH_SG(c, which):
    return 31 + 2 * c + which
def CH_GATE(dc, b):
    return 39 + 3 * dc + b


def _na_index_tables():
    p = np.arange(128)
    ck = p % 64
    half = p // 64
    s = np.arange(2, 16)
    cq = np.arange(64)
    dr = (8 - s)[None, :, None] + half[:, None, None] + 0 * cq[None, None, :]
    dcol = np.clip(ck[:, None, None] - cq[None, None, :] + 15, 0, 30) + 0 * s[None, :, None]
    cs = np.clip(cq - 8, 0, 48)
    colv = (ck[:, None, None] >= cs[None, None, :]) & (ck[:, None, None] < cs[None, None, :] + 16)
    colv = colv & (s[None, :, None] > -100)
    full_ok = colv & (np.abs(dr) <= 7)
    int_ok = colv & (dr >= -4) & (dr <= 3)
    ridx = np.clip(dr + 7, 0, 14)
    return ridx, dcol, full_ok, int_ok


def _rope_tables():
    t = np.arange(S)
    row = (t // 64).astype(np.float64)
    col = (t % 64).astype(np.float64)
    freqs = 10000.0 ** (-np.arange(0, 32, 2, dtype=np.float64) / 32.0)
    ang = np.concatenate([row[:, None] * freqs, col[:, None] * freqs], axis=-1)
    cos = np.cos(ang)
    sin = np.sin(ang)
    p = np.arange(128)
    d = p % 64
    C = cos[:, d // 2].T
    Sg = (sin[:, d // 2] * np.where(d % 2 == 0, -1.0, 1.0)[None, :]).T
    return np.ascontiguousarray(np.stack([C, Sg], axis=1)).astype(np.float32)


def _consts32():
    perm = np.zeros((128, 128), np.float32)
    for i in range(64):
        perm[2 * i + 1, 2 * i] = 1.0
        perm[2 * i, 2 * i + 1] = 1.0
    k = np.arange(128)
    bones = (k[:, None] // 64 == k[None, :] // 64).astype(np.float32)
    return np.concatenate([perm, bones], axis=1)


def prep_inputs(x, ln_in_g, ln_in_b, w_in, b_in, na_rpb, q_norm_g, k_norm_g, sg_ln_g, sg_ln_b, sg_w, sg_b,
                w_br_a, w_br_b, w_br_c, w_out, b_out, ln_post_g, ln_post_b):
    f = lambda a: np.ascontiguousarray(np.asarray(a, dtype=np.float32))
    cols = _win_cols()
    shared = {}
    shared["w_in_p"] = f(np.take(w_in, cols, axis=2))
    bp = np.take(b_in, cols, axis=1)
    vecs = np.zeros((L, 128, 80), np.float32)
    vecs[:, :, 0:NCH] = bp.reshape(L, NCH, 128).transpose(0, 2, 1)
    vecs[:, :, 64] = np.tile(q_norm_g, (1, 2))
    vecs[:, :, 65] = np.tile(k_norm_g, (1, 2))
    shared["vecs"] = vecs
    bv = np.concatenate([b_in[:, 1024:1536], b_in[:, 2688:2816], b_in[:, 3840:4352]], axis=1)
    shared["bvb"] = f(np.broadcast_to(bv[:, None, :], (L, 128, 1152)))
    shared["lnin"] = f(np.broadcast_to(np.stack([ln_in_g, ln_in_b])[None], (128, 2, D)))
    ridx, dcol, full_ok, int_ok = _na_index_tables()
    rp = np.asarray(na_rpb, np.float32)
    g = rp[:, :, ridx, dcol]
    neg = np.float32(-1e30)
    tfull = np.where(full_ok[None, None], g, neg)
    tint = np.where(int_ok[None, None], g, neg)
    shared["natab"] = f(np.stack([tfull, tint], axis=3).reshape(L, 8, 128, 2 * 896))
    shared["rope"] = _rope_tables()
    shared["c32"] = _consts32()
    shared["ident"] = np.eye(128, dtype=np.float32)
    shared["sgln"] = f(np.broadcast_to(np.stack([sg_ln_g, sg_ln_b], axis=1)[:, None], (L, 128, 2, 512)))
    shared["sgwT"] = f(np.transpose(sg_w, (0, 3, 1, 2)))
    sb = np.asarray(sg_b, np.float32)
    sbb = sb.reshape(L, 4, 2, 1, 128)
    sbb = np.broadcast_to(sbb, (L, 4, 2, 64, 128)).reshape(L, 4, 128, 128)
    sbb = np.broadcast_to(sbb[:, :, :, None, :], (L, 4, 128, 4, 128)).reshape(L, 4, 128, 512)
    shared["sgb"] = f(sbb.transpose(0, 2, 1, 3))
    wf = np.stack([w_br_a, w_br_b, w_br_c], axis=2)
    wf = wf.reshape(L, 512, 3, 8, 128).transpose(0, 1, 3, 2, 4).reshape(L, 512, 3072)
    shared["wfin"] = f(wf)
    shared["wout"] = f(w_out)
    shared["post"] = f(np.broadcast_to(np.stack([b_out, ln_post_g, ln_post_b], axis=1)[:, None], (L, 128, 3, D)))
    return shared


DUMMY_MM = 0
POOL_EVERY = 0


class KB:
    def __init__(self, layers, do_ln_in, dbg=False):
        self.layers = list(layers)
        self.do_ln_in = do_ln_in
        self.dbg = dbg
        self.s = Sched()
        nc = self.nc = bass.Bass("TRN2", target_bir_lowering=False)
        di = lambda n, shp: nc.dram_tensor(n, shp, F32, kind="ExternalInput").ap()
        self.d = dict(
            x=di("x", [S, D]), lnin=di("lnin", [128, 2, D]), w_in_p=di("w_in_p", [L, D, WIN]),
            vecs=di("vecs", [L, 128, 80]), bvb=di("bvb", [L, 128, 1152]), natab=di("natab", [L, 8, 128, 1792]),
            rope=di("rope", [128, 2, S]), c32=di("c32", [128, 256]), ident=di("ident", [128, 128]),
            sgln=di("sgln", [L, 128, 2, 512]), sgwT=di("sgwT", [L, 128, 8, 128]), sgb=di("sgb", [L, 128, 4, 512]),
            wfin=di("wfin", [L, 512, 3072]), wout=di("wout", [L, D, D]), post=di("post", [L, 128, 3, D]),
        )
        self.y_d = nc.dram_tensor("y", [S, D], F32, kind="ExternalOutput").ap()
        self.dbg_d = {}
        self._nm = 0
        o = SB_BASE
        self.xtok = self.A("xtok", [128, 16, D], F32, o); o += 65536
        self.xT = self.A("xT", [128, 8, S], BF16, o); o += 32768
        self.R_y = o
        self.yA = self.A("yA", [128, 4, S], BF16, o); o += 16384
        self.yB = self.A("yB", [128, 4, S], BF16, o); o += 16384
        self.yC = self.A("yC", [128, 4, S], BF16, o); o += 16384
        self.ybr = [self.yA, self.yB, self.yC]
        self.ident = self.A("ident", [128, 128], BF16, o); o += 256
        self.c32 = self.A("c32", [128, 256], F32, o); o += 1024
        self.vecs = self.A("vecs", [128, 80], F32, o); o += 320
        self.stat = self.A("stat", [128, 64], F32, o); o += 256
        self.R_loc = o
        self.wslot = [self.A("wslot0", [128, 4608], BF16, o), self.A("wslot1", [128, 4608], BF16, o + 9216)]
        o += 18432
        self.R_ph = o
        self.psall = nc.alloc_psum_tensor("psall", [128, 4096], F32)
        self.ps = [self.psall[:, b * 512:(b + 1) * 512] for b in range(8)]
        self.pst = self.psall[:, 3584:4096].bitcast(BF16)
        self.pbanks = [0, 1, 2, 3, 4, 5, 6]
        self.pb_i = 0
        self.wjobs = []
        for l in self.layers:
            self.wjobs += self.layer_wjobs(l)
        self.w_issued = 0
        self.w_used = 0
        self._xT_pending = None

    def A(self, name, shape, dtype, off):
        nbytes = int(np.prod(shape[1:])) * (4 if dtype == F32 else 2)
        assert off + nbytes <= SB_END, (name, off, nbytes)
        self._nm += 1
        return self.nc.alloc_sbuf_tensor_at("%s_%d" % (name, self._nm), list(shape), dtype, offset=off)

    def bank(self):
        b = self.pbanks[self.pb_i % len(self.pbanks)]
        self.pb_i += 1
        return b

    def mm(self, banks, out, lhsT, rhs, start, stop, reads):
        if isinstance(banks, int):
            banks = [banks]
        self.s.op("pe", lambda e: e.matmul(out, lhsT, rhs, start=start, stop=stop), reads=reads,
                  excl=[("ps", b) for b in banks])

    def act(self, out, in_, func, reads=(), writes=(), excl=(), **kw):
        self.s.op("act", lambda e: e.activation(out, in_, func, **kw), reads=reads, writes=writes, excl=excl)

    def tt(self, eng, out, in0, in1, op, reads=(), writes=(), excl=()):
        self.s.op(eng, lambda e: e.tensor_tensor(out, in0, in1, op), reads=reads, writes=writes, excl=excl)

    def ts(self, eng, out, in0, s1, s2, op0, op1=None, reads=(), writes=(), excl=()):
        if op1 is None:
            self.s.op(eng, lambda e: e.tensor_scalar(out, in0, s1, None, op0), reads=reads, writes=writes, excl=excl)
        else:
            self.s.op(eng, lambda e: e.tensor_scalar(out, in0, s1, s2, op0, op1), reads=reads, writes=writes, excl=excl)

    def stt(self, eng, out, in0, scalar, in1, op0, op1, reads=(), writes=(), excl=()):
        self.s.op(eng, lambda e: e.scalar_tensor_tensor(out, in0, scalar, in1, op0, op1), reads=reads, writes=writes,
                  excl=excl)

    def cp(self, eng, out, in_, reads=(), writes=(), excl=()):
        self.s.op(eng, lambda e: e.tensor_copy(out, in_), reads=reads, writes=writes, excl=excl)

    def dma(self, eng, out, in_, reads=(), writes=()):
        self.s.dma(eng, lambda e: e.dma_start(out=out, in_=in_), reads=reads, writes=writes)

    def layer_wjobs(self, l):
        w = self.d["w_in_p"]
        jobs = []
        for c in range(4):
            jobs.append([(0, 8, 512, w[l, :, c * 512:(c + 1) * 512])])
        g0 = CH_GK[0] * 128
        jobs.append([(0, 8, 384, w[l, :, g0:g0 + 384])])
        for c in range(4):
            c0 = CH_GQ(c, 0) * 128
            jobs.append([(0, 8, 256, w[l, :, c0:c0 + 256])])
        c0 = CH_SV * 128
        jobs.append([(0, 8, 512, w[l, :, c0:c0 + 512])])
        for c in range(4):
            c0 = CH_SG(c, 0) * 128
            jobs.append([(0, 8, 256, w[l, :, c0:c0 + 256])])
        for dc in range(8):
            g0 = CH_GATE(dc, 0) * 128
            jobs.append([(0, 8, 384, w[l, :, g0:g0 + 384]),
                         (3072, 4, 384, self.d["wfin"][l, :, dc * 384:(dc + 1) * 384])])
        for hf in range(2):
            jobs.append([(0, 8, 512, self.d["wout"][l, :, hf * 512:(hf + 1) * 512])])
        return jobs

    def _issue_w(self):
        k = self.w_issued
        if k >= len(self.wjobs):
            return
        self.w_issued += 1
        i = k % 2
        for (off, kch, ncols, src) in self.wjobs[k]:
            dst = self.wslot[i][:, off:off + kch * ncols].rearrange("p (k n) -> p k n", n=ncols)
            self.dma("pool", dst, src.rearrange("(k p) n -> p k n", p=128), writes=[("w", i)])

    def take_w(self):
        k = self.w_used
        self.w_used += 1
        while self.w_issued <= min(k + 1, len(self.wjobs) - 1):
            self._issue_w()
        i = k % 2
        views = []
        for (off, kch, ncols, src) in self.wjobs[k]:
            views.append(self.wslot[i][:, off:off + kch * ncols].rearrange("p (k n) -> p k n", n=ncols))
        return views, ("w", i)

    def xT_res(self, tc):
        return [("xT", 4 * tc + i) for i in range(4)]

    def proj_fm(self, wt, wres, col0, evac):
        for tc in range(4):
            b = self.bank()
            for kc in range(8):
                self.mm(b, self.ps[b][:, :], wt[:, kc, col0:col0 + 128], self.xT[:, kc, tc * 512:(tc + 1) * 512],
                        kc == 0, kc == 7, reads=[wres] + self.xT_res(tc))
            evac(tc, b)

    def bias(self, ch):
        return self.vecs[:, ch:ch + 1]

    def ln_stats(self, src, src_res, tt_i):
        st = self.stat
        k = (tt_i % 4) * 16
        for h in range(2):
            self.s.op("dve", (lambda h: lambda e: e.bn_stats(st[:, k + h * 6:k + h * 6 + 6],
                                                               src[:, h * 512:(h + 1) * 512]))(h),
                      reads=[src_res], writes=[("stat", tt_i % 4)])
        self.s.op("dve", lambda e: e.bn_aggr(st[:, k + 12:k + 14], st[:, k:k + 12]),
                  reads=[("stat", tt_i % 4)], writes=[("stat", tt_i % 4)])
        self.act(st[:, k + 14:k + 15], st[:, k + 13:k + 14], AF.Sqrt, bias=1e-5,
                 reads=[("stat", tt_i % 4)], writes=[("stat", tt_i % 4)])

    def ln_apply(self, src, src_res, tt_i, g_ap, b_ap, gb_res):
        st = self.stat
        k = (tt_i % 4) * 16
        self.s.op("dve", lambda e: e.reciprocal(st[:, k + 14:k + 15], st[:, k + 14:k + 15]),
                  reads=[("stat", tt_i % 4)], writes=[("stat", tt_i % 4)])
        xt = self.xtok[:, tt_i, :]
        self.stt("dve", xt, src, st[:, k + 12:k + 13], g_ap, ALU.subtract, ALU.mult,
                 reads=[src_res, ("stat", tt_i % 4), gb_res], writes=[("xtok", tt_i)])
        self.stt("dve", xt, xt, st[:, k + 14:k + 15], b_ap, ALU.mult, ALU.add,
                 reads=[("xtok", tt_i), ("stat", tt_i % 4), gb_res], writes=[("xtok", tt_i)])

    def to_xT(self, tt_i, tmp_h16):
        if self._xT_pending is not None:
            self.to_xT_finish()
        k = tt_i % 2
        h16 = tmp_h16[k]
        self.act(h16[:, :], self.xtok[:, tt_i, :], AF.Copy, reads=[("xtok", tt_i)], writes=[("h16", k)])
        for kc in range(8):
            self.s.op("pe", (lambda kc: lambda e: e.transpose(self.pst[:, kc * 128:(kc + 1) * 128],
                                                              h16[:, kc * 128:(kc + 1) * 128], self.ident[:, :]))(kc),
                      reads=[("h16", k), ("ident",)], excl=[("ps", 7)])
        self._xT_pending = tt_i

    def to_xT_finish(self):
        tt_i = self._xT_pending
        if tt_i is None:
            return
        self._xT_pending = None
        self.cp("dve", self.xT[:, :, tt_i * 128:(tt_i + 1) * 128], self.pst[:, :].rearrange("p (k n) -> p k n", n=128),
                writes=[("xT", tt_i)], excl=[("ps", 7)])

    def phase0(self):
        o = self.R_ph
        gb = self.A("lnin", [128, 2, D], F32, o); o += 8192
        xin = [self.A("xin%d" % i, [128, D], F32, o + 4096 * i) for i in range(3)]; o += 12288
        h16 = [self.A("h16a", [128, D], BF16, o), self.A("h16b", [128, D], BF16, o + 2048)]; o += 4096
        self.dma("pool", self.ident[:, :], self.d["ident"], writes=[("ident",)])
        self.dma("sp", self.c32[:, :], self.d["c32"], writes=[("c32",)])
        self._issue_w()
        if self.do_ln_in:
            self.dma("sp", gb[:, :, :], self.d["lnin"], writes=[("lnin",)])

            def load(t):
                self.dma("sp", xin[t % 3][:, :], self.d["x"][t * 128:(t + 1) * 128, :], writes=[("xin", t % 3)])
            load(0)
            load(1)
            self.ln_stats(xin[0][:, :], ("xin", 0), 0)
            for t in range(16):
                if t + 2 < 16:
                    load(t + 2)
                if t + 1 < 16:
                    self.ln_stats(xin[(t + 1) % 3][:, :], ("xin", (t + 1) % 3), t + 1)
                self.ln_apply(xin[t % 3][:, :], ("xin", t % 3), t, gb[:, 0, :], gb[:, 1, :], ("lnin",))
                self.to_xT_finish()
                self.to_xT(t, h16)
            self.to_xT_finish()
        else:
            for tt_i in range(16):
                self.dma("sp", self.xtok[:, tt_i, :], self.d["x"][tt_i * 128:(tt_i + 1) * 128, :], writes=[("xtok", tt_i)])
                self.to_xT(tt_i, h16)
            self.to_xT_finish()

    def attn_pipeline(self, units, Pb, exb, regions, scale, look, s_order=None):
        nu = len(units)
        nr = len(regions)
        nb = len(Pb)
        mulcount = [0]

        def emitS(u, extra=()):
            banks = regions[u % nr]
            offs = []
            off = 0
            for t in units[u]:
                offs.append(off)
                off += t["n"]
            order = list(range(len(units[u]))) if s_order is None or len(units[u]) != len(s_order) else list(s_order)
            for c0 in range(0, len(order), 2):
                grp = order[c0:c0 + 2]
                for i in grp:
                    t = units[u][i]
                    n = t["n"]
                    base = banks[0] * 512 + offs[i]
                    self.mm([banks[offs[i] // 512]], self.psall[:, base:base + n], t["s_lhsT"], t["s_rhs"], True,
                            t.get("b_rhs") is None, reads=list(t["s_reads"]) + list(extra))
                for i in grp:
                    t = units[u][i]
                    if t.get("b_rhs") is None:
                        continue
                    n = t["n"]
                    base = banks[0] * 512 + offs[i]
                    self.mm([banks[offs[i] // 512]], self.psall[:, base:base + n], self.ident[:, :], t["b_rhs"], False, True,
                            reads=[("ident",), t["b_res"]])

        for u in range(min(look, nu)):
            emitS(u)
        for u in range(nu):
            banks = regions[u % nr]
            k = u % nb
            tot = sum(t["n"] for t in units[u])
            base = banks[0] * 512
            src = self.psall[:, base:base + tot]
            ex = [("ps", b) for b in banks]
            if units[u][0]["etab"] is None:
                self.act(Pb[k][:, 0:tot], src, AF.Exp, scale=scale,
                         writes=[("P", k, i) for i in range(len(units[u]))], excl=ex)
            else:
                self.act(exb[k][:, 0:tot], src, AF.Exp, scale=scale, writes=[("ex", k)], excl=ex)
                off = 0
                for i, t in enumerate(units[u]):
                    n = t["n"]
                    eng = "pool" if (POOL_EVERY and mulcount[0] % POOL_EVERY == POOL_EVERY - 1) else "dve"
                    mulcount[0] += 1
                    self.tt(eng, Pb[k][:, off:off + n], exb[k][:, off:off + n], t["etab"], ALU.mult,
                            reads=[("ex", k), t["etab_res"]], writes=[("P", k, i)])
                    off += n
            if u + look < nu:
                emitS(u + look)
            off = 0
            for i, t in enumerate(units[u]):
                n = t["n"]
                ob = t["ob"]
                self.mm(ob, self.ps[ob][:, 0:n], t["v_lhsT"], Pb[k][:, off:off + n], t["first"], t["last"],
                        reads=[("P", k, i)] + t["v_reads"])
                off += n
                for _ in range(DUMMY_MM):
                    self.mm(6, self.ps[6][:, 0:n], self.ident[:, :], Pb[k][:, 0:n], True, True, reads=[("P", k, i), ("ident",)])
                if t["last"] and t["epi"] is not None:
                    t["epi"]()

    def attn_epilogue(self, ob, n, hb, rs, sz_ap, sz_res, y_ap, y_res, slot):
        so = 64 - hb
        ps = self.ps[ob]
        self.s.op("dve", lambda e: e.reciprocal(rs[hb:hb + 64, 0:n], ps[so:so + 64, 0:n]),
                  writes=[("rs", slot)], excl=[("ps", ob)])
        self.tt("pool", rs[hb:hb + 64, 0:n], rs[hb:hb + 64, 0:n], sz_ap, ALU.mult,
                reads=[("rs", slot), sz_res], writes=[("rs", slot)])
        self.tt("dve", y_ap, ps[hb:hb + 64, 0:n], rs[hb:hb + 64, 0:n], ALU.mult,
                reads=[("rs", slot)], writes=[y_res], excl=[("ps", ob)])

    def phaseA(self, l):
        o = self.R_ph
        qT = self.A("naq", [128, S], BF16, o); o += 4096
        kT = self.A("nak", [128, S], BF16, o); o += 4096
        sz = self.A("nasz", [128, S], BF16, o); o += 4096
        Va = self.A("nava", [128, 16, 2, 128], BF16, o); o += 8192
        exb = None
        Pb = [self.A("P%d" % i, [128, 1024], BF16, o + 2048 * i) for i in range(3)]; o += 6144
        rs = [[self.A("rs%d%d" % (i, h), [128, 256], F32, o + 2048 * i + 1024 * h) for h in range(2)] for i in range(2)]
        o += 4096
        bv = self.A("bvna", [128, 512], F32, o); o += 2048
        oa = self.R_y + 32768
        tabraw = self.A("tabraw", [128, 1792], F32, oa); oa += 7168
        E = [self.A("E%d" % i, [128, 1792], BF16, oa + 3584 * i) for i in range(2)]; oa += 7168
        regions = [[0, 1], [2, 3]]
        obanks = [(4, 5), (6, 7)]
        self.pbanks = [0, 1, 2, 3, 4, 5, 6, 7]
        d = self.d
        self.dma("sp", bv[:, :], d["bvb"][l, :, 0:512], writes=[("bvna",)])
        self.s.op("pool", lambda e: e.memset(Va[:, :, 0, 64:128], 1.0), writes=[("nava", j) for j in range(16)])
        self.s.op("pool", lambda e: e.memset(Va[:, :, 1, 0:64], 1.0), writes=[("nava", j) for j in range(16)])
        hcount = 0
        ecount = 0
        for c in range(4):
            (wt,), wres = self.take_w()

            def ev_q(tc, b, dst=qT, ch=CH_NA(c, 0), nm="naq"):
                self.ts("dve", dst[:, tc * 512:(tc + 1) * 512], self.ps[b][:, :], self.bias(ch), None, ALU.add,
                        reads=[("vecs",)], writes=[(nm, tc)], excl=[("ps", b)])
            self.proj_fm(wt, wres, 0, ev_q)
            self.proj_fm(wt, wres, 128, lambda tc, b: ev_q(tc, b, kT, CH_NA(c, 1), "nak"))

            def ev_z(tc, b, ch=CH_NA(c, 3)):
                self.act(sz[:, tc * 512:(tc + 1) * 512], self.ps[b][:, :], AF.Silu, bias=self.bias(ch),
                         reads=[("vecs",)], writes=[("nasz", tc)], excl=[("ps", b)])
            self.proj_fm(wt, wres, 384, ev_z)
            for j0 in range(0, 16, 4):
                b = self.bank()
                for t in range(4):
                    for kc in range(8):
                        self.mm(b, self.ps[b][:, t * 128:(t + 1) * 128], self.xT[:, kc, (j0 + t) * 128:(j0 + t + 1) * 128],
                                wt[:, kc, 256:384], kc == 0, kc == 7, reads=[wres, ("xT", j0 + t)])
                pv = self.ps[b][:, :].rearrange("p (a n) -> p a n", n=128)
                for hi in range(2):
                    bsl = bv[:, c * 128 + hi * 64:c * 128 + hi * 64 + 64].unsqueeze(1).broadcast_to([128, 4, 64])
                    self.tt("dve", Va[:, j0:j0 + 4, hi, hi * 64:hi * 64 + 64], pv[:, :, hi * 64:hi * 64 + 64], bsl, ALU.add,
                            reads=[("bvna",)], writes=[("nava", j0 + t) for t in range(4)], excl=[("ps", b)])
            units = []
            eks = []
            for hi in range(2):
                ek = ecount % 2
                ecount += 1
                eks.append(ek)
                self.dma("sp", tabraw[:, :], d["natab"][l, 2 * c + hi], writes=[("tabraw",)])
                self.act(E[ek][:, :], tabraw[:, :], AF.Identity, scale=8.0, reads=[("tabraw",)], writes=[("E", ek)])
            for q8 in range(8):
                r0 = 4 * q8
                if q8 == 0:
                    jl, kind = [0, 1, 2, 3], 0
                elif q8 == 7:
                    jl, kind = [12, 13, 14, 15], 0
                else:
                    jl, kind = list(range(2 * q8 - 2, 2 * q8 + 4)), 1
                obs = obanks[hcount % 2]
                slot = hcount % 2
                hcount += 1
                q0 = q8 * 256
                tcq = q0 // 512
                tl = {0: [], 1: []}
                for hi in range(2):
                    hb = 64 * hi
                    ek = eks[hi]
                    ob = obs[hi]

                    def epi(ob=ob, hb=hb, slot=slot, hi=hi, q0=q0, tcq=tcq, c=c):
                        self.attn_epilogue(ob, 256, hb, rs[slot][hi], sz[hb:hb + 64, q0:q0 + 256], ("nasz", tcq),
                                           self.yA[hb:hb + 64, c, q0:q0 + 256], ("yA", c, tcq), (slot, hi))
                    for idx, j in enumerate(jl):
                        s0 = r0 - 2 * j + 8
                        assert 2 <= s0 and s0 + 4 <= 16
                        tl[hi].append(dict(
                            n=256, s_lhsT=kT[hb:hb + 64, j * 128:(j + 1) * 128], s_rhs=qT[hb:hb + 64, q0:q0 + 256],
                            s_reads=[("nak", j // 4), ("naq", tcq)],
                            etab=None, etab_res=None,
                            b_rhs=E[ek][:, kind * 896 + (s0 - 2) * 64:kind * 896 + (s0 + 2) * 64], b_res=("E", ek),
                            v_lhsT=Va[:, j, hi, :], v_reads=[("nava", j)], ob=ob,
                            first=(idx == 0), last=(idx == len(jl) - 1), epi=epi if idx == len(jl) - 1 else None))
                for i in range(0, len(jl), 2):
                    units.append([tl[0][i], tl[0][i + 1], tl[1][i], tl[1][i + 1]])
            self.attn_pipeline(units, Pb, exb, regions, 0.125, look=2, s_order=[0, 2, 1, 3])
        if self.dbg:
            self.dbg_dump("yA%d" % l, self.yA, [("yA", c, t) for c in range(4) for t in range(4)])

    def dbg_dump(self, name, t, reads):
        shp = list(t.shape)
        dt = t.dtype
        dd = self.nc.dram_tensor("dbg_" + name, shp, dt, kind="ExternalOutput").ap()
        self.dbg_d[name] = dd
        self.dma("sp", dd, t[tuple(slice(None) for _ in shp)], reads=reads)

    def barrier(self):
        for e in Sched.ENGS:
            self.s.wait_all(e, dmas=False)

    def phaseB(self, l):
        o = self.R_ph
        kT2 = self.A("gk", [128, 2, S], BF16, o); o += 8192
        qT = self.A("gq", [128, S], BF16, o); o += 4096
        sz = self.A("gsz", [128, S], BF16, o); o += 4096
        ropeb = [self.A("rope", [128, 2, 512], F32, o)] * 2; o += 4096
        sq2 = [self.A("sq%d" % i, [128, 512], F32, o + 2048 * i) for i in range(2)]; o += 4096
        qb2 = [self.A("qb%d" % i, [128, 512], F32, o + 2048 * i) for i in range(2)]; o += 4096
        u2 = [self.A("u%d" % i, [128, 512], F32, o + 2048 * i) for i in range(2)]; o += 4096
        rstd2 = [self.A("rstd", [128, 512], F32, o)] * 2; o += 2048
        t12 = [self.A("t1", [128, 512], F32, o)] * 2; o += 2048
        bv = self.A("bvg", [128, 128], F32, o); o += 512
        oa = self.R_y + 32768
        Vg = self.A("gv", [128, 16, 2, 192], BF16, oa); oa += 12288
        rs = [[self.A("grs%d%d" % (i, h), [128, 256], F32, oa + 2048 * i + 1024 * h) for h in range(2)] for i in range(2)]
        oa += 4096
        Pb = [self.A("gP%d" % i, [128, 1024], BF16, o + 2048 * i) for i in range(3)]; o += 6144
        assert o <= SB_END, o
        regions = [[0, 1], [2, 3]]
        obanks = [(4, 5), (6, 7)]
        self.pbanks = [0, 1, 2, 3, 4, 5, 6, 7]
        d = self.d
        c32 = self.c32
        self.dma("sp", bv[:, :], d["bvb"][l, :, 512:640], writes=[("bvg",)])
        self.s.op("pool", lambda e: e.memset(Vg[:, :, :, 0:64], 1.0), writes=[("gv", j) for j in range(16)])
        self.s.op("pool", lambda e: e.memset(Vg[:, :, :, 128:192], 1.0), writes=[("gv", j) for j in range(16)])
        self.rope_i = 0

        def rope_stage1(tc, b, bias_ch, gain_col, dst_ap, dst_res):
            rk = self.rope_i % 2
            self.rope_i += 1
            sq, qb, u, rstd, t1, t2 = sq2[rk], qb2[rk], u2[rk], rstd2[rk], t12[rk], sq2[rk]
            R = lambda nm: ("sq", rk) if nm == "t2" else (nm, rk if nm in ("qb", "sq", "u") else 0)
            ps = self.ps[b]
            self.act(sq[:, :], ps[:, :], AF.Square, bias=self.bias(bias_ch), reads=[("vecs",)], writes=[R("sq")],
                     excl=[("ps", b)])
            self.ts("dve", u[:, :], ps[:, :], self.bias(bias_ch), self.vecs[:, gain_col:gain_col + 1], ALU.add, ALU.mult,
                    reads=[("vecs",)], writes=[R("u")], excl=[("ps", b)])

            def stage2():
                self.dma("sp", ropeb[rk][:, :, :], d["rope"][:, :, tc * 512:(tc + 1) * 512], writes=[("rope", 0)])
                b2 = self.bank()
                self.mm(b2, self.ps[b2][:, :], c32[:, 128:256], sq[:, :], True, True, reads=[R("sq"), ("c32",)])
                b3 = self.bank()
                self.mm(b3, self.ps[b3][:, :], c32[:, 0:128], u[:, :], True, True, reads=[R("u"), ("c32",)])
                self.act(rstd[:, :], self.ps[b2][:, :], AF.Ln, bias=64e-6, writes=[R("rstd")], excl=[("ps", b2)])
                self.act(rstd[:, :], rstd[:, :], AF.Exp, scale=-0.5, reads=[R("rstd")], writes=[R("rstd")])
                self.tt("pool", t1[:, :], u[:, :], ropeb[rk][:, 0, :], ALU.mult, reads=[R("u"), ("rope", 0)], writes=[R("t1")])
                self.tt("dve", t2[:, :], self.ps[b3][:, :], ropeb[rk][:, 1, :], ALU.mult, reads=[("rope", 0)],
                        writes=[R("t2")], excl=[("ps", b3)])
                self.tt("dve", t1[:, :], t1[:, :], t2[:, :], ALU.add, reads=[R("t1"), R("t2")], writes=[R("t1")])
                self.tt("dve", dst_ap, t1[:, :], rstd[:, :], ALU.mult, reads=[R("t1"), R("rstd")], writes=[dst_res])
            return stage2

        def proj_chunk(wt, wres, col0, tc):
            b = self.bank()
            for kc in range(8):
                self.mm(b, self.ps[b][:, :], wt[:, kc, col0:col0 + 128], self.xT[:, kc, tc * 512:(tc + 1) * 512],
                        kc == 0, kc == 7, reads=[wres] + self.xT_res(tc))
            return b

        (wt,), wres = self.take_w()
        pend = None
        for g in range(2):
            for tc in range(4):
                b = proj_chunk(wt, wres, g * 128, tc)
                st2 = rope_stage1(tc, b, CH_GK[g], 65, kT2[:, g, tc * 512:(tc + 1) * 512], ("gk", g, tc))
                if pend is not None:
                    pend()
                pend = st2
        kv_pend = pend
        for j0 in range(0, 16, 4):
            b = self.bank()
            for t in range(4):
                for kc in range(8):
                    self.mm(b, self.ps[b][:, t * 128:(t + 1) * 128], self.xT[:, kc, (j0 + t) * 128:(j0 + t + 1) * 128],
                            wt[:, kc, 256:384], kc == 0, kc == 7, reads=[wres, ("xT", j0 + t)])
            pv = self.ps[b][:, :].rearrange("p (a n) -> p a n", n=128)
            for g in range(2):
                bsl = bv[:, g * 64:g * 64 + 64].unsqueeze(1).broadcast_to([128, 4, 64])
                self.tt("dve", Vg[:, j0:j0 + 4, g, 64:128], pv[:, :, g * 64:g * 64 + 64], bsl, ALU.add,
                        reads=[("bvg",)], writes=[("gv", j0 + t) for t in range(4)], excl=[("ps", b)])
        kv_pend()
        hcount = 0
        for c in range(4):
            g = c // 2
            (wt,), wres = self.take_w()
            pend = None
            for tc in range(4):
                b = proj_chunk(wt, wres, 0, tc)
                st2 = rope_stage1(tc, b, CH_GQ(c, 0), 64, qT[:, tc * 512:(tc + 1) * 512], ("gq", tc))
                if pend is not None:
                    pend()
                pend = st2
            q_pend = pend

            def ev_z(tc, b, ch=CH_GQ(c, 1)):
                self.act(sz[:, tc * 512:(tc + 1) * 512], self.ps[b][:, :], AF.Silu, bias=self.bias(ch),
                         reads=[("vecs",)], writes=[("gsz", tc)], excl=[("ps", b)])
            self.proj_fm(wt, wres, 128, ev_z)
            q_pend()
            units = []
            for q8 in range(8):
                q0 = q8 * 256
                tcq = q0 // 512
                obs = obanks[hcount % 2]
                slot = hcount % 2
                hcount += 1
                tl = {0: [], 1: []}
                for hi in range(2):
                    hb = 64 * hi
                    ob = obs[hi]

                    def epi(ob=ob, hb=hb, slot=slot, hi=hi, q0=q0, tcq=tcq, c=c):
                        self.attn_epilogue(ob, 256, hb, rs[slot][hi], sz[hb:hb + 64, q0:q0 + 256], ("gsz", tcq),
                                           self.yB[hb:hb + 64, c, q0:q0 + 256], ("yB", c, tcq), (slot, hi))
                    for j in range(16):
                        vl = Vg[:, j, g, 64:192] if hi == 0 else Vg[:, j, g, 0:128]
                        tl[hi].append(dict(
                            n=256, s_lhsT=kT2[hb:hb + 64, g, j * 128:(j + 1) * 128], s_rhs=qT[hb:hb + 64, q0:q0 + 256],
                            s_reads=[("gk", g, j // 4), ("gq", tcq)], etab=None, etab_res=None,
                            v_lhsT=vl, v_reads=[("gv", j)], ob=ob, first=(j == 0), last=(j == 15),
                            epi=epi if j == 15 else None))
                for i in range(0, 16, 2):
                    units.append([tl[0][i], tl[0][i + 1], tl[1][i], tl[1][i + 1]])
            self.attn_pipeline(units, Pb, None, regions, 8.0, look=2, s_order=[0, 2, 1, 3])
        if self.dbg:
            self.dbg_dump("yB%d" % l, self.yB, [("yB", c, t) for c in range(4) for t in range(4)])

    def phaseC(self, l):
        o = self.R_ph
        vn = self.A("vn", [128, 16, 512], BF16, o); o += 16384
        sgw = self.A("sgw", [128, 8, 128], BF16, o); o += 2048
        vraw = [self.A("vraw%d" % i, [128, 512], F32, o + 2048 * i) for i in range(2)]; o += 4096
        sgln = self.A("sgln", [128, 2, 512], F32, o); o += 4096
        bvs = self.A("bvs", [128, 512], F32, o); o += 2048
        uzb = [self.A("uz%d" % i, [128, 512], F32, o + 2048 * i) for i in range(2)]; o += 4096
        szt = [self.A("szt%d" % i, [128, 512], F32, o + 2048 * i) for i in range(2)]; o += 4096
        sgb = [self.A("sgb%d" % i, [128, 512], F32, o + 2048 * i) for i in range(2)]; o += 4096
        mix = [self.A("mix%d" % i, [128, 512], F32, o + 2048 * i) for i in range(2)]; o += 4096
        assert o <= SB_END, o
        self.pbanks = [0, 1, 2, 3, 4, 5, 6]
        d = self.d
        st = self.stat
        self.dma("sp", bvs[:, :], d["bvb"][l, :, 640:1152], writes=[("bvs",)])
        self.dma("sp", sgln[:, :, :], d["sgln"][l], writes=[("sgln",)])
        self.dma("pool", sgw[:, :, :], d["sgwT"][l], writes=[("sgw",)])
        (wt,), wres = self.take_w()
        for tt_i in range(16):
            b = self.bank()
            k = tt_i % 2
            for kc in range(8):
                self.mm(b, self.ps[b][:, :], self.xT[:, kc, tt_i * 128:(tt_i + 1) * 128], wt[:, kc, 0:512],
                        kc == 0, kc == 7, reads=[wres, ("xT", tt_i)])
            vr = vraw[k]
            self.tt("dve", vr[:, :], self.ps[b][:, :], bvs[:, :], ALU.add, reads=[("bvs",)], writes=[("vraw", k)],
                    excl=[("ps", b)])
            sk = 16 * k
            self.s.op("dve", (lambda vr, sk: lambda e: e.bn_stats(st[:, sk:sk + 6], vr[:, :]))(vr, sk),
                      reads=[("vraw", k)], writes=[("stat", k)])
            self.s.op("dve", (lambda sk: lambda e: e.bn_aggr(st[:, sk + 12:sk + 14], st[:, sk:sk + 6]))(sk),
                      reads=[("stat", k)], writes=[("stat", k)])
            self.act(st[:, sk + 14:sk + 15], st[:, sk + 13:sk + 14], AF.Sqrt, bias=1e-5,
                     reads=[("stat", k)], writes=[("stat", k)])
            self.s.op("dve", (lambda sk: lambda e: e.reciprocal(st[:, sk + 14:sk + 15], st[:, sk + 14:sk + 15]))(sk),
                      reads=[("stat", k)], writes=[("stat", k)])
            self.stt("dve", vr[:, :], vr[:, :], st[:, sk + 12:sk + 13], sgln[:, 0, :], ALU.subtract, ALU.mult,
                     reads=[("vraw", k), ("stat", k), ("sgln",)], writes=[("vraw", k)])
            self.stt("dve", vn[:, tt_i, :], vr[:, :], st[:, sk + 14:sk + 15], sgln[:, 1, :], ALU.mult, ALU.add,
                     reads=[("vraw", k), ("stat", k), ("sgln",)], writes=[("vn", tt_i)])
        cnt = 0
        for c in range(4):
            (wt,), wres = self.take_w()
            self.dma("sp", sgb[c % 2][:, :], d["sgb"][l, :, c, :], writes=[("sgb", c % 2)])
            for tc in range(4):
                bu = self.bank()
                for kc in range(8):
                    self.mm(bu, self.ps[bu][:, :], wt[:, kc, 0:128], self.xT[:, kc, tc * 512:(tc + 1) * 512],
                            kc == 0, kc == 7, reads=[wres] + self.xT_res(tc))
                bz = self.bank()
                for kc in range(8):
                    self.mm(bz, self.ps[bz][:, :], wt[:, kc, 128:256], self.xT[:, kc, tc * 512:(tc + 1) * 512],
                            kc == 0, kc == 7, reads=[wres] + self.xT_res(tc))
                k = cnt % 2
                cnt += 1
                self.act(szt[k][:, :], self.ps[bz][:, :], AF.Silu, bias=self.bias(CH_SG(c, 1)), reads=[("vecs",)],
                         writes=[("szt", k)], excl=[("ps", bz)])
                uzs = uzb[k][:, :]
                self.stt("dve", uzs, self.ps[bu][:, :], self.bias(CH_SG(c, 0)), szt[k][:, :], ALU.add, ALU.mult,
                         reads=[("vecs",), ("szt", k)], writes=[("uz", k)], excl=[("ps", bu)])
                bm = [self.bank(), self.bank()]
                for gi in range(2):
                    for t in range(4):
                        tt_i = 4 * tc + t
                        self.mm(bm[gi], self.ps[bm[gi]][:, t * 128:(t + 1) * 128], vn[:, tt_i, c * 128:(c + 1) * 128],
                                sgw[:, 2 * c + gi, :], True, True, reads=[("vn", tt_i), ("sgw",)])
                mk = mix[k]
                for gi in range(2):
                    hb = 64 * gi
                    self.tt("dve", mk[hb:hb + 64, :], self.ps[bm[gi]][hb:hb + 64, :], sgb[c % 2][hb:hb + 64, :], ALU.add,
                            reads=[("sgb", c % 2)], writes=[("mix", k)], excl=[("ps", bm[gi])])
                self.tt("pool", self.yC[:, c, tc * 512:(tc + 1) * 512], mk[:, :], uzs, ALU.mult,
                        reads=[("mix", k), ("uz", k)], writes=[("yC", c, tc)])
        if self.dbg:
            self.dbg_dump("yC%d" % l, self.yC, [("yC", c, t) for c in range(4) for t in range(4)])

    def phaseD(self, l, last):
        o = self.R_ph
        mT = self.A("mT", [128, 8, S], BF16, o); o += 32768
        gsig = [self.A("gsig%d" % i, [128, 512], F32, o + 2048 * i) for i in range(2)]; o += 4096
        mt = [self.A("mt%d" % i, [128, 512], F32, o + 2048 * i) for i in range(3)]; o += 6144
        assert o <= SB_END, o
        self.pbanks = [0, 1, 2, 3, 4, 5, 6]
        d = self.d
        gcnt = 0
        for dc in range(8):
            (wg, wb), wres = self.take_w()
            for tc in range(4):
                for br in range(3):
                    bg = self.bank()
                    for kc in range(8):
                        self.mm(bg, self.ps[bg][:, :], wg[:, kc, br * 128:(br + 1) * 128],
                                self.xT[:, kc, tc * 512:(tc + 1) * 512], kc == 0, kc == 7, reads=[wres] + self.xT_res(tc))
                    k = gcnt % 2
                    gcnt += 1
                    self.act(gsig[k][:, :], self.ps[bg][:, :], AF.Sigmoid, bias=self.bias(CH_GATE(dc, br)),
                             reads=[("vecs",)], writes=[("gsig", k)], excl=[("ps", bg)])
                    bp = self.bank()
                    yb = self.ybr[br]
                    nm = ("yA", "yB", "yC")[br]
                    for kc in range(4):
                        self.mm(bp, self.ps[bp][:, :], wb[:, kc, br * 128:(br + 1) * 128], yb[:, kc, tc * 512:(tc + 1) * 512],
                                kc == 0, kc == 3, reads=[wres, (nm, kc, tc)])
                    mb = mt[0] if br == 0 else mt[1]
                    mk = ("mt", 0 if br == 0 else 1)
                    self.tt("dve", mb[:, :], self.ps[bp][:, :], gsig[k][:, :], ALU.mult, reads=[("gsig", k)],
                            writes=[mk], excl=[("ps", bp)])
                    if br == 1:
                        self.tt("dve", mt[0][:, :], mt[0][:, :], mt[1][:, :], ALU.add, reads=[("mt", 0), ("mt", 1)],
                                writes=[("mt", 0)])
                    elif br == 2:
                        self.tt("dve", mT[:, dc, tc * 512:(tc + 1) * 512], mt[0][:, :], mt[1][:, :], ALU.add,
                                reads=[("mt", 0), ("mt", 1)], writes=[("mT", tc)])
        self.barrier()
        oa = self.R_y
        post = self.A("post", [128, 3, D], F32, oa); oa += 12288
        h16 = [self.A("dh16%d" % i, [128, D], BF16, oa + 2048 * i) for i in range(2)]; oa += 4096
        self.dma("sp", post[:, :, :], d["post"][l], writes=[("post",)])
        for hf in range(2):
            (wo,), wres = self.take_w()
            for tt_i in range(16):
                tcq = tt_i // 4
                b = self.bank()
                for dc in range(8):
                    self.mm(b, self.ps[b][:, :], mT[:, dc, tt_i * 128:(tt_i + 1) * 128], wo[:, dc, 0:512],
                            dc == 0, dc == 7, reads=[("mT", tcq), wres])
                x_ap = self.xtok[:, tt_i, hf * 512:(hf + 1) * 512]
                self.stt("dve", x_ap, x_ap, float(ALPHA), self.ps[b][:, :], ALU.mult, ALU.add,
                         reads=[("xtok", tt_i)], writes=[("xtok", tt_i)], excl=[("ps", b)])
                if hf == 1:
                    self.tt("pool", self.xtok[:, tt_i, :], self.xtok[:, tt_i, :], post[:, 0, :], ALU.add,
                            reads=[("xtok", tt_i), ("post",)], writes=[("xtok", tt_i)])
        self.ln_stats(self.xtok[:, 0, :], ("xtok", 0), 0)
        for t in range(16):
            if t + 1 < 16:
                self.ln_stats(self.xtok[:, t + 1, :], ("xtok", t + 1), t + 1)
            self.ln_apply(self.xtok[:, t, :], ("xtok", t), t, post[:, 1, :], post[:, 2, :], ("post",))
            if last:
                self.dma("sp", self.y_d[t * 128:(t + 1) * 128, :], self.xtok[:, t, :], reads=[("xtok", t)])
            else:
                self.to_xT_finish()
                self.to_xT(t, h16)
        self.to_xT_finish()
        self.barrier()

    def layer(self, l, last):
        self.dma("sp", self.vecs[:, :], self.d["vecs"][l], writes=[("vecs",)])
        self.phaseA(l)
        self.barrier()
        self.phaseB(l)
        self.barrier()
        self.phaseC(l)
        self.barrier()
        self.phaseD(l, last)

    def build(self):
        from contextlib import ExitStack
        nc = self.nc
        self.phase0()
        self.barrier()
        for i, l in enumerate(self.layers):
            self.layer(l, last=(i == len(self.layers) - 1))
        self.s.wait_all("sp")
        with ExitStack() as es:
            sems = {e: es.enter_context(nc.semaphore("s_" + e)) for e in Sched.ENGS}
            dsems = [es.enter_context(nc.semaphore("q%d" % i)) for i in range(self.s.n_dma)]
            block = es.enter_context(nc.Block())
            self.s.emit(nc, block, sems, dsems)
        return nc


def _run(layers, do_ln_in, x_list, shared, dbg=False):
    kb = KB(layers, do_ln_in, dbg)
    nc = kb.build()
    in_maps = []
    for xb in x_list:
        m = dict(shared)
        m["x"] = np.ascontiguousarray(xb, dtype=np.float32)
        in_maps.append(m)
    return run_bass_kernel_spmd(nc, in_maps, core_ids=list(range(len(x_list))))


def kernel(**inputs):
    x = np.asarray(inputs["x"], dtype=np.float32)
    shared = prep_inputs(**inputs)
    res = _run([0, 1], True, [x[b] for b in range(x.shape[0])], shared)
    return np.stack([r["y"] for r in res.results], axis=0).astype(np.float32)
```

```python
import numpy as np
import concourse.bass as bass
import concourse.mybir as mybir
from concourse.bass_utils import run_bass_kernel_spmd

F32, BF16 = mybir.dt.float32, mybir.dt.bfloat16
AF = mybir.ActivationFunctionType
ALU = mybir.AluOpType

S = 2048
D = 1024
L = 2
NCH = 63
WIN = NCH * 128
ALPHA = (2.0 * L) ** 0.25
SB_BASE = 16512
SB_END = 229376
EMBED_WAIT = ("act", "dve", "pool")


class Sched:
    ENGS = ("pe", "act", "dve", "pool", "sp")

    def __init__(self, n_dma_sems=24):
        self.prog = {e: [] for e in self.ENGS}
        self.serial = {e: 0 for e in self.ENGS}
        self.seen = {e: {} for e in self.ENGS}
        self.lastw = {}
        self.readers = {}
        self.lastx = {}
        self.waited = {e: set() for e in self.ENGS}
        self.n_dma = n_dma_sems
        self.dma_i = 0
        self.dma_ip = 0
        self.dma_val = [0] * n_dma_sems

    def _deps(self, eng, reads, writes, excl):
        need = {}

        def add(tok, raw, isx=False):
            key, val, teng = tok
            if teng == eng and isx:
                return
            if self.seen[eng].get(key, 0) >= val:
                return
            if need.get(key, 0) < val:
                need[key] = val

        for r in reads:
            t = self.lastw.get(r)
            if t:
                add(t, True)
        for w in writes:
            t = self.lastw.get(w)
            if t:
                add(t, False)
            for t in self.readers.get(w, {}).values():
                add(t, False)
        for x in excl:
            t = self.lastx.get(x)
            if t:
                add(t, False, True)
        for key, val in need.items():
            self.seen[eng][key] = val
            self.prog[eng].append(("wait", key, val))
            if key[0] != "q":
                self.waited[key].add(val)

    def _commit(self, tok, reads, writes, excl):
        for r in reads:
            self.readers.setdefault(r, {})[tok[0]] = tok
        for w in writes:
            self.lastw[w] = tok
            self.readers[w] = {}
        for x in excl:
            self.lastx[x] = tok

    def op(self, eng, fn, reads=(), writes=(), excl=()):
        self._deps(eng, reads, writes, excl)
        self.serial[eng] += 1
        tok = (eng, self.serial[eng], eng)
        self.prog[eng].append(("op", fn, self.serial[eng]))
        self._commit(tok, reads, writes, excl)
        return tok

    def dma(self, eng, fn, reads=(), writes=()):
        self._deps(eng, reads, writes, ())
        if eng == "pool":
            k = 16 + self.dma_ip % (self.n_dma - 16)
            self.dma_ip += 1
        else:
            k = self.dma_i % 16
            self.dma_i += 1
        key = "q%d" % k
        prev = self.dma_val[k]
        if prev > 0 and self.seen[eng].get(key, 0) < prev:
            self.seen[eng][key] = prev
            self.prog[eng].append(("wait", key, prev))
        self.dma_val[k] += 16
        tok = (key, self.dma_val[k], None)
        self.prog[eng].append(("dma", fn, k))
        self._commit(tok, reads, writes, ())
        return tok

    def wait_all(self, eng, dmas=True):
        for e in self.ENGS:
            if e != eng and self.serial[e] > 0 and self.seen[eng].get(e, 0) < self.serial[e]:
                self.seen[eng][e] = self.serial[e]
                self.prog[eng].append(("wait", e, self.serial[e]))
                self.waited[e].add(self.serial[e])
        for k in range(self.n_dma if dmas else 0):
            key = "q%d" % k
            if self.dma_val[k] > 0 and self.seen[eng].get(key, 0) < self.dma_val[k]:
                self.seen[eng][key] = self.dma_val[k]
                self.prog[eng].append(("wait", key, self.dma_val[k]))

    def emit(self, nc, block, sems, dsems):
        rank = {}
        for e in self.ENGS:
            ws = sorted(self.waited[e])
            rank[e] = {s: i + 1 for i, s in enumerate(ws)}
        handles = {"pe": "tensor", "act": "scalar", "dve": "vector", "pool": "gpsimd", "sp": "sync"}

        def run(ename):
            def body(eng):
                pend = []

                def semval(key, val):
                    if key[0] == "q":
                        return dsems[int(key[1:])], val
                    return sems[key], rank[key][val]

                for item in self.prog[ename]:
                    if item[0] == "wait":
                        pend.append(semval(item[1], item[2]))
                        continue
                    embed = None
                    if pend and item[0] == "op" and ename in EMBED_WAIT:
                        embed = pend.pop()
                    for (sm, v) in pend:
                        eng.wait_ge(sm, v)
                    pend = []
                    ins = item[1](eng)
                    if embed is not None:
                        ins._wait_ge(embed[0], embed[1])
                    if item[0] == "op":
                        if item[2] in rank[ename]:
                            ins.then_inc(sems[ename], 1)
                    else:
                        ins.then_inc(dsems[item[2]], 16)
                for (sm, v) in pend:
                    eng.wait_ge(sm, v)
            return body

        for ename in self.ENGS:
            getattr(block, handles[ename])(run(ename))


def _win_cols():
    naq, nak, nav, naz, gqq, gqk, gqv, gqz, sgu, sgv, sgz, gat = (0, 512, 1024, 1536, 2048, 2560, 2688, 2816,
                                                                 3328, 3840, 4352, 4864)
    r = lambda a, n: list(range(a, a + n))
    cols = []
    for c in range(4):
        cols += r(naq + c * 128, 128) + r(nak + c * 128, 128) + r(nav + c * 128, 128) + r(naz + c * 128, 128)
    cols += r(gqk, 64) + r(gqk, 64) + r(gqk + 64, 64) + r(gqk + 64, 64) + r(gqv, 128)
    for c in range(4):
        cols += r(gqq + c * 128, 128) + r(gqz + c * 128, 128)
    cols += r(sgv, 512)
    for c in range(4):
        cols += r(sgu + c * 128, 128) + r(sgz + c * 128, 128)
    for dc in range(8):
        for b in range(3):
            cols += r(gat + b * 1024 + dc * 128, 128)
    assert len(cols) == WIN
    return np.asarray(cols)


def CH_NA(c, which):
    return 4 * c + which
CH_GK = (16, 17)
CH_GV = 18
def CH_GQ(c, which):
    return 19 + 2 * c + which
CH_SV = 27
def CH_SG(c, which):
    return 31 + 2 * c + which
def CH_GATE(dc, b):
    return 39 + 3 * dc + b


def _na_index_tables():
    p = np.arange(128)
    ck = p % 64
    half = p // 64
    s = np.arange(2, 16)
    cq = np.arange(64)
    dr = (8 - s)[None, :, None] + half[:, None, None] + 0 * cq[None, None, :]
    dcol = np.clip(ck[:, None, None] - cq[None, None, :] + 15, 0, 30) + 0 * s[None, :, None]
    cs = np.clip(cq - 8, 0, 48)
    colv = (ck[:, None, None] >= cs[None, None, :]) & (ck[:, None, None] < cs[None, None, :] + 16)
    colv = colv & (s[None, :, None] > -100)
    full_ok = colv & (np.abs(dr) <= 7)
    int_ok = colv & (dr >= -4) & (dr <= 3)
    ridx = np.clip(dr + 7, 0, 14)
    return ridx, dcol, full_ok, int_ok


def _rope_tables():
    t = np.arange(S)
    row = (t // 64).astype(np.float64)
    col = (t % 64).astype(np.float64)
    freqs = 10000.0 ** (-np.arange(0, 32, 2, dtype=np.float64) / 32.0)
    ang = np.concatenate([row[:, None] * freqs, col[:, None] * freqs], axis=-1)
    cos = np.cos(ang)
    sin = np.sin(ang)
    p = np.arange(128)
    d = p % 64
    C = cos[:, d // 2].T
    Sg = (sin[:, d // 2] * np.where(d % 2 == 0, -1.0, 1.0)[None, :]).T
    return np.ascontiguousarray(np.stack([C, Sg], axis=1)).astype(np.float32)


def _consts32():
    perm = np.zeros((128, 128), np.float32)
    for i in range(64):
        perm[2 * i + 1, 2 * i] = 1.0
        perm[2 * i, 2 * i + 1] = 1.0
    k = np.arange(128)
    bones = (k[:, None] // 64 == k[None, :] // 64).astype(np.float32)
    return np.concatenate([perm, bones], axis=1)


def prep_inputs(x, ln_in_g, ln_in_b, w_in, b_in, na_rpb, q_norm_g, k_norm_g, sg_ln_g, sg_ln_b, sg_w, sg_b,
                w_br_a, w_br_b, w_br_c, w_out, b_out, ln_post_g, ln_post_b):
    f = lambda a: np.ascontiguousarray(np.asarray(a, dtype=np.float32))
    cols = _win_cols()
    shared = {}
    shared["w_in_p"] = f(np.take(w_in, cols, axis=2))
    bp = np.take(b_in, cols, axis=1)
    vecs = np.zeros((L, 128, 80), np.float32)
    vecs[:, :, 0:NCH] = bp.reshape(L, NCH, 128).transpose(0, 2, 1)
    vecs[:, :, 64] = np.tile(q_norm_g, (1, 2))
    vecs[:, :, 65] = np.tile(k_norm_g, (1, 2))
    shared["vecs"] = vecs
    bv = np.concatenate([b_in[:, 1024:1536], b_in[:, 2688:2816], b_in[:, 3840:4352]], axis=1)
    shared["bvb"] = f(np.broadcast_to(bv[:, None, :], (L, 128, 1152)))
    shared["lnin"] = f(np.broadcast_to(np.stack([ln_in_g, ln_in_b])[None], (128, 2, D)))
    ridx, dcol, full_ok, int_ok = _na_index_tables()
    rp = np.asarray(na_rpb, np.float32)
    g = rp[:, :, ridx, dcol]
    neg = np.float32(-1e30)
    tfull = np.where(full_ok[None, None], g, neg)
    tint = np.where(int_ok[None, None], g, neg)
    shared["natab"] = f(np.stack([tfull, tint], axis=3).reshape(L, 8, 128, 2 * 896))
    shared["rope"] = _rope_tables()
    shared["c32"] = _consts32()
    shared["ident"] = np.eye(128, dtype=np.float32)
    shared["sgln"] = f(np.broadcast_to(np.stack([sg_ln_g, sg_ln_b], axis=1)[:, None], (L, 128, 2, 512)))
    shared["sgwT"] = f(np.transpose(sg_w, (0, 3, 1, 2)))
    sb = np.asarray(sg_b, np.float32)
    sbb = sb.reshape(L, 4, 2, 1, 128)
    sbb = np.broadcast_to(sbb, (L, 4, 2, 64, 128)).reshape(L, 4, 128, 128)
    sbb = np.broadcast_to(sbb[:, :, :, None, :], (L, 4, 128, 4, 128)).reshape(L, 4, 128, 512)
    shared["sgb"] = f(sbb.transpose(0, 2, 1, 3))
    wf = np.stack([w_br_a, w_br_b, w_br_c], axis=2)
    wf = wf.reshape(L, 512, 3, 8, 128).transpose(0, 1, 3, 2, 4).reshape(L, 512, 3072)
    shared["wfin"] = f(wf)
    shared["wout"] = f(w_out)
    shared["post"] = f(np.broadcast_to(np.stack([b_out, ln_post_g, ln_post_b], axis=1)[:, None], (L, 128, 3, D)))
    return shared


DUMMY_MM = 0
POOL_EVERY = 0


class KB:
    def __init__(self, layers, do_ln_in, dbg=False):
        self.layers = list(layers)
        self.do_ln_in = do_ln_in
        self.dbg = dbg
        self.s = Sched()
        nc = self.nc = bass.Bass("TRN2", target_bir_lowering=False)
        di = lambda n, shp: nc.dram_tensor(n, shp, F32, kind="ExternalInput").ap()
        self.d = dict(
            x=di("x", [S, D]), lnin=di("lnin", [128, 2, D]), w_in_p=di("w_in_p", [L, D, WIN]),
            vecs=di("vecs", [L, 128, 80]), bvb=di("bvb", [L, 128, 1152]), natab=di("natab", [L, 8, 128, 1792]),
            rope=di("rope", [128, 2, S]), c32=di("c32", [128, 256]), ident=di("ident", [128, 128]),
            sgln=di("sgln", [L, 128, 2, 512]), sgwT=di("sgwT", [L, 128, 8, 128]), sgb=di("sgb", [L, 128, 4, 512]),
            wfin=di("wfin", [L, 512, 3072]), wout=di("wout", [L, D, D]), post=di("post", [L, 128, 3, D]),
        )
        self.y_d = nc.dram_tensor("y", [S, D], F32, kind="ExternalOutput").ap()
        self.dbg_d = {}
        self._nm = 0
        o = SB_BASE
        self.xtok = self.A("xtok", [128, 16, D], F32, o); o += 65536
        self.xT = self.A("xT", [128, 8, S], BF16, o); o += 32768
        self.R_y = o
        self.yA = self.A("yA", [128, 4, S], BF16, o); o += 16384
        self.yB = self.A("yB", [128, 4, S], BF16, o); o += 16384
        self.yC = self.A("yC", [128, 4, S], BF16, o); o += 16384
        self.ybr = [self.yA, self.yB, self.yC]
        self.ident = self.A("ident", [128, 128], BF16, o); o += 256
        self.c32 = self.A("c32", [128, 256], F32, o); o += 1024
        self.vecs = self.A("vecs", [128, 80], F32, o); o += 320
        self.stat = self.A("stat", [128, 64], F32, o); o += 256
        self.R_loc = o
        self.wslot = [self.A("wslot0", [128, 4608], BF16, o), self.A("wslot1", [128, 4608], BF16, o + 9216)]
        o += 18432
        self.R_ph = o
        self.psall = nc.alloc_psum_tensor("psall", [128, 4096], F32)
        self.ps = [self.psall[:, b * 512:(b + 1) * 512] for b in range(8)]
        self.pst = self.psall[:, 3584:4096].bitcast(BF16)
        self.pbanks = [0, 1, 2, 3, 4, 5, 6]
        self.pb_i = 0
        self.wjobs = []
        for l in self.layers:
            self.wjobs += self.layer_wjobs(l)
        self.w_issued = 0
        self.w_used = 0
        self._xT_pending = None

    def A(self, name, shape, dtype, off):
        nbytes = int(np.prod(shape[1:])) * (4 if dtype == F32 else 2)
        assert off + nbytes <= SB_END, (name, off, nbytes)
        self._nm += 1
        return self.nc.alloc_sbuf_tensor_at("%s_%d" % (name, self._nm), list(shape), dtype, offset=off)

    def bank(self):
        b = self.pbanks[self.pb_i % len(self.pbanks)]
        self.pb_i += 1
        return b

    def mm(self, banks, out, lhsT, rhs, start, stop, reads):
        if isinstance(banks, int):
            banks = [banks]
        self.s.op("pe", lambda e: e.matmul(out, lhsT, rhs, start=start, stop=stop), reads=reads,
                  excl=[("ps", b) for b in banks])

    def act(self, out, in_, func, reads=(), writes=(), excl=(), **kw):
        self.s.op("act", lambda e: e.activation(out, in_, func, **kw), reads=reads, writes=writes, excl=excl)

    def tt(self, eng, out, in0, in1, op, reads=(), writes=(), excl=()):
        self.s.op(eng, lambda e: e.tensor_tensor(out, in0, in1, op), reads=reads, writes=writes, excl=excl)

    def ts(self, eng, out, in0, s1, s2, op0, op1=None, reads=(), writes=(), excl=()):
        if op1 is None:
            self.s.op(eng, lambda e: e.tensor_scalar(out, in0, s1, None, op0), reads=reads, writes=writes, excl=excl)
        else:
            self.s.op(eng, lambda e: e.tensor_scalar(out, in0, s1, s2, op0, op1), reads=reads, writes=writes, excl=excl)

    def stt(self, eng, out, in0, scalar, in1, op0, op1, reads=(), writes=(), excl=()):
        self.s.op(eng, lambda e: e.scalar_tensor_tensor(out, in0, scalar, in1, op0, op1), reads=reads, writes=writes,
                  excl=excl)

    def cp(self, eng, out, in_, reads=(), writes=(), excl=()):
        self.s.op(eng, lambda e: e.tensor_copy(out, in_), reads=reads, writes=writes, excl=excl)

    def dma(self, eng, out, in_, reads=(), writes=()):
        self.s.dma(eng, lambda e: e.dma_start(out=out, in_=in_), reads=reads, writes=writes)

    def layer_wjobs(self, l):
        w = self.d["w_in_p"]
        jobs = []
        for c in range(4):
            jobs.append([(0, 8, 512, w[l, :, c * 512:(c + 1) * 512])])
        g0 = CH_GK[0] * 128
        jobs.append([(0, 8, 384, w[l, :, g0:g0 + 384])])
        for c in range(4):
            c0 = CH_GQ(c, 0) * 128
            jobs.append([(0, 8, 256, w[l, :, c0:c0 + 256])])
        c0 = CH_SV * 128
        jobs.append([(0, 8, 512, w[l, :, c0:c0 + 512])])
        for c in range(4):
            c0 = CH_SG(c, 0) * 128
            jobs.append([(0, 8, 256, w[l, :, c0:c0 + 256])])
        for dc in range(8):
            g0 = CH_GATE(dc, 0) * 128
            jobs.append([(0, 8, 384, w[l, :, g0:g0 + 384]),
                         (3072, 4, 384, self.d["wfin"][l, :, dc * 384:(dc + 1) * 384])])
        for hf in range(2):
            jobs.append([(0, 8, 512, self.d["wout"][l, :, hf * 512:(hf + 1) * 512])])
        return jobs

    def _issue_w(self):
        k = self.w_issued
        if k >= len(self.wjobs):
            return
        self.w_issued += 1
        i = k % 2
        for (off, kch, ncols, src) in self.wjobs[k]:
            dst = self.wslot[i][:, off:off + kch * ncols].rearrange("p (k n) -> p k n", n=ncols)
            self.dma("pool", dst, src.rearrange("(k p) n -> p k n", p=128), writes=[("w", i)])

    def take_w(self):
        k = self.w_used
        self.w_used += 1
        while self.w_issued <= min(k + 1, len(self.wjobs) - 1):
            self._issue_w()
        i = k % 2
        views = []
        for (off, kch, ncols, src) in self.wjobs[k]:
            views.append(self.wslot[i][:, off:off + kch * ncols].rearrange("p (k n) -> p k n", n=ncols))
        return views, ("w", i)

    def xT_res(self, tc):
        return [("xT", 4 * tc + i) for i in range(4)]

    def proj_fm(self, wt, wres, col0, evac):
        for tc in range(4):
            b = self.bank()
            for kc in range(8):
                self.mm(b, self.ps[b][:, :], wt[:, kc, col0:col0 + 128], self.xT[:, kc, tc * 512:(tc + 1) * 512],
                        kc == 0, kc == 7, reads=[wres] + self.xT_res(tc))
            evac(tc, b)

    def bias(self, ch):
        return self.vecs[:, ch:ch + 1]

    def ln_stats(self, src, src_res, tt_i):
        st = self.stat
        k = (tt_i % 4) * 16
        for h in range(2):
            self.s.op("dve", (lambda h: lambda e: e.bn_stats(st[:, k + h * 6:k + h * 6 + 6],
                                                               src[:, h * 512:(h + 1) * 512]))(h),
                      reads=[src_res], writes=[("stat", tt_i % 4)])
        self.s.op("dve", lambda e: e.bn_aggr(st[:, k + 12:k + 14], st[:, k:k + 12]),
                  reads=[("stat", tt_i % 4)], writes=[("stat", tt_i % 4)])
        self.act(st[:, k + 14:k + 15], st[:, k + 13:k + 14], AF.Sqrt, bias=1e-5,
                 reads=[("stat", tt_i % 4)], writes=[("stat", tt_i % 4)])

    def ln_apply(self, src, src_res, tt_i, g_ap, b_ap, gb_res):
        st = self.stat
        k = (tt_i % 4) * 16
        self.s.op("dve", lambda e: e.reciprocal(st[:, k + 14:k + 15], st[:, k + 14:k + 15]),
                  reads=[("stat", tt_i % 4)], writes=[("stat", tt_i % 4)])
        xt = self.xtok[:, tt_i, :]
        self.stt("dve", xt, src, st[:, k + 12:k + 13], g_ap, ALU.subtract, ALU.mult,
                 reads=[src_res, ("stat", tt_i % 4), gb_res], writes=[("xtok", tt_i)])
        self.act(xt, xt, AF.Copy, scale=st[:, k + 14:k + 15], reads=[("xtok", tt_i), ("stat", tt_i % 4)],
                 writes=[("xtok", tt_i)])
        self.tt("pool", xt, xt, b_ap, ALU.add, reads=[("xtok", tt_i), gb_res], writes=[("xtok", tt_i)])

    def to_xT(self, tt_i, tmp_h16):
        if self._xT_pending is not None:
            self.to_xT_finish()
        k = tt_i % 2
        h16 = tmp_h16[k]
        self.act(h16[:, :], self.xtok[:, tt_i, :], AF.Copy, reads=[("xtok", tt_i)], writes=[("h16", k)])
        for kc in range(8):
            self.s.op("pe", (lambda kc: lambda e: e.transpose(self.pst[:, kc * 128:(kc + 1) * 128],
                                                              h16[:, kc * 128:(kc + 1) * 128], self.ident[:, :]))(kc),
                      reads=[("h16", k), ("ident",)], excl=[("ps", 7)])
        self._xT_pending = tt_i

    def to_xT_finish(self):
        tt_i = self._xT_pending
        if tt_i is None:
            return
        self._xT_pending = None
        self.cp("dve", self.xT[:, :, tt_i * 128:(tt_i + 1) * 128], self.pst[:, :].rearrange("p (k n) -> p k n", n=128),
                writes=[("xT", tt_i)], excl=[("ps", 7)])

    def phase0(self):
        o = self.R_ph
        gb = self.A("lnin", [128, 2, D], F32, o); o += 8192
        xin = [self.A("xin%d" % i, [128, D], F32, o + 4096 * i) for i in range(3)]; o += 12288
        h16 = [self.A("h16a", [128, D], BF16, o), self.A("h16b", [128, D], BF16, o + 2048)]; o += 4096
        self.dma("pool", self.ident[:, :], self.d["ident"], writes=[("ident",)])
        self.dma("sp", self.c32[:, :], self.d["c32"], writes=[("c32",)])
        self._issue_w()
        if self.do_ln_in:
            self.dma("sp", gb[:, :, :], self.d["lnin"], writes=[("lnin",)])

            def load(t):
                self.dma("sp", xin[t % 3][:, :], self.d["x"][t * 128:(t + 1) * 128, :], writes=[("xin", t % 3)])
            load(0)
            load(1)
            self.ln_stats(xin[0][:, :], ("xin", 0), 0)
            for t in range(16):
                if t + 2 < 16:
                    load(t + 2)
                if t + 1 < 16:
                    self.ln_stats(xin[(t + 1) % 3][:, :], ("xin", (t + 1) % 3), t + 1)
                self.ln_apply(xin[t % 3][:, :], ("xin", t % 3), t, gb[:, 0, :], gb[:, 1, :], ("lnin",))
                self.to_xT_finish()
                self.to_xT(t, h16)
            self.to_xT_finish()
        else:
            for tt_i in range(16):
                self.dma("sp", self.xtok[:, tt_i, :], self.d["x"][tt_i * 128:(tt_i + 1) * 128, :], writes=[("xtok", tt_i)])
                self.to_xT(tt_i, h16)
            self.to_xT_finish()

    def attn_pipeline(self, units, Pb, exb, regions, scale, look, s_order=None):
        nu = len(units)
        nr = len(regions)
        nb = len(Pb)
        mulcount = [0]

        def emitS(u, extra=()):
            banks = regions[u % nr]
            offs = []
            off = 0
            for t in units[u]:
                offs.append(off)
                off += t["n"]
            order = list(range(len(units[u]))) if s_order is None or len(units[u]) != len(s_order) else list(s_order)
            for c0 in range(0, len(order), 2):
                grp = order[c0:c0 + 2]
                for i in grp:
                    t = units[u][i]
                    n = t["n"]
                    base = banks[0] * 512 + offs[i]
                    self.mm([banks[offs[i] // 512]], self.psall[:, base:base + n], t["s_lhsT"], t["s_rhs"], True,
                            t.get("b_rhs") is None, reads=list(t["s_reads"]) + list(extra))
                for i in grp:
                    t = units[u][i]
                    if t.get("b_rhs") is None:
                        continue
                    n = t["n"]
                    base = banks[0] * 512 + offs[i]
                    self.mm([banks[offs[i] // 512]], self.psall[:, base:base + n], self.ident[:, :], t["b_rhs"], False, True,
                            reads=[("ident",), t["b_res"]])

        for u in range(min(look, nu)):
            emitS(u)
        for u in range(nu):
            banks = regions[u % nr]
            k = u % nb
            tot = sum(t["n"] for t in units[u])
            base = banks[0] * 512
            src = self.psall[:, base:base + tot]
            ex = [("ps", b) for b in banks]
            if units[u][0]["etab"] is None:
                self.act(Pb[k][:, 0:tot], src, AF.Exp, scale=scale,
                         writes=[("P", k, i) for i in range(len(units[u]))], excl=ex)
            else:
                self.act(exb[k][:, 0:tot], src, AF.Exp, scale=scale, writes=[("ex", k)], excl=ex)
                off = 0
                for i, t in enumerate(units[u]):
                    n = t["n"]
                    eng = "pool" if (POOL_EVERY and mulcount[0] % POOL_EVERY == POOL_EVERY - 1) else "dve"
                    mulcount[0] += 1
                    self.tt(eng, Pb[k][:, off:off + n], exb[k][:, off:off + n], t["etab"], ALU.mult,
                            reads=[("ex", k), t["etab_res"]], writes=[("P", k, i)])
                    off += n
            if u + look < nu:
                emitS(u + look)
            off = 0
            for i, t in enumerate(units[u]):
                n = t["n"]
                ob = t["ob"]
                self.mm(ob, self.ps[ob][:, 0:n], t["v_lhsT"], Pb[k][:, off:off + n], t["first"], t["last"],
                        reads=[("P", k, i)] + t["v_reads"])
                off += n
                for _ in range(DUMMY_MM):
                    self.mm(6, self.ps[6][:, 0:n], self.ident[:, :], Pb[k][:, 0:n], True, True, reads=[("P", k, i), ("ident",)])
                if t["last"] and t["epi"] is not None:
                    t["epi"]()

    def attn_epilogue(self, ob, n, hb, rs, sz_ap, sz_res, y_ap, y_res, slot):
        so = 64 - hb
        ps = self.ps[ob]
        self.s.op("dve", lambda e: e.reciprocal(rs[hb:hb + 64, 0:n], ps[so:so + 64, 0:n]),
                  writes=[("rs", slot)], excl=[("ps", ob)])
        self.tt("pool", rs[hb:hb + 64, 0:n], rs[hb:hb + 64, 0:n], sz_ap, ALU.mult,
                reads=[("rs", slot), sz_res], writes=[("rs", slot)])
        self.tt("dve", y_ap, ps[hb:hb + 64, 0:n], rs[hb:hb + 64, 0:n], ALU.mult,
                reads=[("rs", slot)], writes=[y_res], excl=[("ps", ob)])

    def phaseA(self, l):
        o = self.R_ph
        qT = self.A("naq", [128, S], BF16, o); o += 4096
        kT = self.A("nak", [128, S], BF16, o); o += 4096
        sz = self.A("nasz", [128, S], BF16, o); o += 4096
        Va = self.A("nava", [128, 16, 2, 128], BF16, o); o += 8192
        exb = None
        Pb = [self.A("P%d" % i, [128, 1024], BF16, o + 2048 * i) for i in range(3)]; o += 6144
        rs = [[self.A("rs%d%d" % (i, h), [128, 256], F32, o + 2048 * i + 1024 * h) for h in range(2)] for i in range(2)]
        o += 4096
        bv = self.A("bvna", [128, 512], F32, o); o += 2048
        oa = self.R_y + 32768
        tabraw = self.A("tabraw", [128, 1792], F32, oa); oa += 7168
        E = [self.A("E%d" % i, [128, 1792], BF16, oa + 3584 * i) for i in range(2)]; oa += 7168
        regions = [[0, 1], [2, 3]]
        obanks = [(4, 5), (6, 7)]
        self.pbanks = [0, 1, 2, 3, 4, 5, 6, 7]
        d = self.d
        self.dma("sp", bv[:, :], d["bvb"][l, :, 0:512], writes=[("bvna",)])
        self.s.op("pool", lambda e: e.memset(Va[:, :, 0, 64:128], 1.0), writes=[("nava", j) for j in range(16)])
        self.s.op("pool", lambda e: e.memset(Va[:, :, 1, 0:64], 1.0), writes=[("nava", j) for j in range(16)])
        hcount = 0
        ecount = 0
        for c in range(4):
            (wt,), wres = self.take_w()

            def ev_q(tc, b, dst=qT, ch=CH_NA(c, 0), nm="naq"):
                self.ts("dve", dst[:, tc * 512:(tc + 1) * 512], self.ps[b][:, :], self.bias(ch), None, ALU.add,
                        reads=[("vecs",)], writes=[(nm, tc)], excl=[("ps", b)])
            self.proj_fm(wt, wres, 0, ev_q)
            self.proj_fm(wt, wres, 128, lambda tc, b: ev_q(tc, b, kT, CH_NA(c, 1), "nak"))

            def ev_z(tc, b, ch=CH_NA(c, 3)):
                self.act(sz[:, tc * 512:(tc + 1) * 512], self.ps[b][:, :], AF.Silu, bias=self.bias(ch),
                         reads=[("vecs",)], writes=[("nasz", tc)], excl=[("ps", b)])
            self.proj_fm(wt, wres, 384, ev_z)
            for j0 in range(0, 16, 4):
                b = self.bank()
                for t in range(4):
                    for kc in range(8):
                        self.mm(b, self.ps[b][:, t * 128:(t + 1) * 128], self.xT[:, kc, (j0 + t) * 128:(j0 + t + 1) * 128],
                                wt[:, kc, 256:384], kc == 0, kc == 7, reads=[wres, ("xT", j0 + t)])
                pv = self.ps[b][:, :].rearrange("p (a n) -> p a n", n=128)
                for hi in range(2):
                    bsl = bv[:, c * 128 + hi * 64:c * 128 + hi * 64 + 64].unsqueeze(1).broadcast_to([128, 4, 64])
                    self.tt("dve", Va[:, j0:j0 + 4, hi, hi * 64:hi * 64 + 64], pv[:, :, hi * 64:hi * 64 + 64], bsl, ALU.add,
                            reads=[("bvna",)], writes=[("nava", j0 + t) for t in range(4)], excl=[("ps", b)])
            units = []
            eks = []
            for hi in range(2):
                ek = ecount % 2
                ecount += 1
                eks.append(ek)
                self.dma("sp", tabraw[:, :], d["natab"][l, 2 * c + hi], writes=[("tabraw",)])
                self.act(E[ek][:, :], tabraw[:, :], AF.Identity, scale=8.0, reads=[("tabraw",)], writes=[("E", ek)])
            for q8 in range(8):
                r0 = 4 * q8
                if q8 == 0:
                    jl, kind = [0, 1, 2, 3], 0
                elif q8 == 7:
                    jl, kind = [12, 13, 14, 15], 0
                else:
                    jl, kind = list(range(2 * q8 - 2, 2 * q8 + 4)), 1
                obs = obanks[hcount % 2]
                slot = hcount % 2
                hcount += 1
                q0 = q8 * 256
                tcq = q0 // 512
                tl = {0: [], 1: []}
                for hi in range(2):
                    hb = 64 * hi
                    ek = eks[hi]
                    ob = obs[hi]

                    def epi(ob=ob, hb=hb, slot=slot, hi=hi, q0=q0, tcq=tcq, c=c):
                        self.attn_epilogue(ob, 256, hb, rs[slot][hi], sz[hb:hb + 64, q0:q0 + 256], ("nasz", tcq),
                                           self.yA[hb:hb + 64, c, q0:q0 + 256], ("yA", c, tcq), (slot, hi))
                    for idx, j in enumerate(jl):
                        s0 = r0 - 2 * j + 8
                        assert 2 <= s0 and s0 + 4 <= 16
                        tl[hi].append(dict(
                            n=256, s_lhsT=kT[hb:hb + 64, j * 128:(j + 1) * 128], s_rhs=qT[hb:hb + 64, q0:q0 + 256],
                            s_reads=[("nak", j // 4), ("naq", tcq)],
                            etab=None, etab_res=None,
                            b_rhs=E[ek][:, kind * 896 + (s0 - 2) * 64:kind * 896 + (s0 + 2) * 64], b_res=("E", ek),
                            v_lhsT=Va[:, j, hi, :], v_reads=[("nava", j)], ob=ob,
                            first=(idx == 0), last=(idx == len(jl) - 1), epi=epi if idx == len(jl) - 1 else None))
                for i in range(0, len(jl), 2):
                    units.append([tl[0][i], tl[0][i + 1], tl[1][i], tl[1][i + 1]])
            self.attn_pipeline(units, Pb, exb, regions, 0.125, look=2, s_order=[0, 2, 1, 3])
        if self.dbg:
            self.dbg_dump("yA%d" % l, self.yA, [("yA", c, t) for c in range(4) for t in range(4)])

    def dbg_dump(self, name, t, reads):
        shp = list(t.shape)
        dt = t.dtype
        dd = self.nc.dram_tensor("dbg_" + name, shp, dt, kind="ExternalOutput").ap()
        self.dbg_d[name] = dd
        self.dma("sp", dd, t[tuple(slice(None) for _ in shp)], reads=reads)

    def barrier(self):
        for e in Sched.ENGS:
            self.s.wait_all(e, dmas=False)

    def phaseB(self, l):
        o = self.R_ph
        kT2 = self.A("gk", [128, 2, S], BF16, o); o += 8192
        qT = self.A("gq", [128, S], BF16, o); o += 4096
        sz = self.A("gsz", [128, S], BF16, o); o += 4096
        ropeb = [self.A("rope", [128, 2, 512], F32, o)] * 2; o += 4096
        sq2 = [self.A("sq%d" % i, [128, 512], F32, o + 2048 * i) for i in range(2)]; o += 4096
        qb2 = [self.A("qb%d" % i, [128, 512], F32, o + 2048 * i) for i in range(2)]; o += 4096
        u2 = [self.A("u%d" % i, [128, 512], F32, o + 2048 * i) for i in range(2)]; o += 4096
        rstd2 = [self.A("rstd", [128, 512], F32, o)] * 2; o += 2048
        t12 = [self.A("t1", [128, 512], F32, o)] * 2; o += 2048
        bv = self.A("bvg", [128, 128], F32, o); o += 512
        oa = self.R_y + 32768
        Vg = self.A("gv", [128, 16, 2, 192], BF16, oa); oa += 12288
        rs = [[self.A("grs%d%d" % (i, h), [128, 256], F32, oa + 2048 * i + 1024 * h) for h in range(2)] for i in range(2)]
        oa += 4096
        Pb = [self.A("gP%d" % i, [128, 1024], BF16, o + 2048 * i) for i in range(3)]; o += 6144
        assert o <= SB_END, o
        regions = [[0, 1], [2, 3]]
        obanks = [(4, 5), (6, 7)]
        self.pbanks = [0, 1, 2, 3, 4, 5, 6, 7]
        d = self.d
        c32 = self.c32
        self.dma("sp", bv[:, :], d["bvb"][l, :, 512:640], writes=[("bvg",)])
        self.s.op("pool", lambda e: e.memset(Vg[:, :, :, 0:64], 1.0), writes=[("gv", j) for j in range(16)])
        self.s.op("pool", lambda e: e.memset(Vg[:, :, :, 128:192], 1.0), writes=[("gv", j) for j in range(16)])
        self.rope_i = 0

        def rope_stage1(tc, b, bias_ch, gain_col, dst_ap, dst_res):
            rk = self.rope_i % 2
            self.rope_i += 1
            sq, qb, u, rstd, t1, t2 = sq2[rk], qb2[rk], u2[rk], rstd2[rk], t12[rk], sq2[rk]
            R = lambda nm: ("sq", rk) if nm == "t2" else (nm, rk if nm in ("qb", "sq", "u") else 0)
            ps = self.ps[b]
            self.act(sq[:, :], ps[:, :], AF.Square, bias=self.bias(bias_ch), reads=[("vecs",)], writes=[R("sq")],
                     excl=[("ps", b)])
            self.ts("dve", u[:, :], ps[:, :], self.bias(bias_ch), self.vecs[:, gain_col:gain_col + 1], ALU.add, ALU.mult,
                    reads=[("vecs",)], writes=[R("u")], excl=[("ps", b)])

            def stage2():
                self.dma("sp", ropeb[rk][:, :, :], d["rope"][:, :, tc * 512:(tc + 1) * 512], writes=[("rope", 0)])
                b2 = self.bank()
                self.mm(b2, self.ps[b2][:, :], c32[:, 128:256], sq[:, :], True, True, reads=[R("sq"), ("c32",)])
                b3 = self.bank()
                self.mm(b3, self.ps[b3][:, :], c32[:, 0:128], u[:, :], True, True, reads=[R("u"), ("c32",)])
                self.act(rstd[:, :], self.ps[b2][:, :], AF.Ln, bias=64e-6, writes=[R("rstd")], excl=[("ps", b2)])
                self.act(rstd[:, :], rstd[:, :], AF.Exp, scale=-0.5, reads=[R("rstd")], writes=[R("rstd")])
                self.tt("pool", t1[:, :], u[:, :], ropeb[rk][:, 0, :], ALU.mult, reads=[R("u"), ("rope", 0)], writes=[R("t1")])
                self.tt("dve", t2[:, :], self.ps[b3][:, :], ropeb[rk][:, 1, :], ALU.mult, reads=[("rope", 0)],
                        writes=[R("t2")], excl=[("ps", b3)])
                self.tt("dve", t1[:, :], t1[:, :], t2[:, :], ALU.add, reads=[R("t1"), R("t2")], writes=[R("t1")])
                self.tt("dve", dst_ap, t1[:, :], rstd[:, :], ALU.mult, reads=[R("t1"), R("rstd")], writes=[dst_res])
            return stage2

        def proj_chunk(wt, wres, col0, tc):
            b = self.bank()
            for kc in range(8):
                self.mm(b, self.ps[b][:, :], wt[:, kc, col0:col0 + 128], self.xT[:, kc, tc * 512:(tc + 1) * 512],
                        kc == 0, kc == 7, reads=[wres] + self.xT_res(tc))
            return b

        (wt,), wres = self.take_w()
        pend = None
        for g in range(2):
            for tc in range(4):
                b = proj_chunk(wt, wres, g * 128, tc)
                st2 = rope_stage1(tc, b, CH_GK[g], 65, kT2[:, g, tc * 512:(tc + 1) * 512], ("gk", g, tc))
                if pend is not None:
                    pend()
                pend = st2
        kv_pend = pend
        for j0 in range(0, 16, 4):
            b = self.bank()
            for t in range(4):
                for kc in range(8):
                    self.mm(b, self.ps[b][:, t * 128:(t + 1) * 128], self.xT[:, kc, (j0 + t) * 128:(j0 + t + 1) * 128],
                            wt[:, kc, 256:384], kc == 0, kc == 7, reads=[wres, ("xT", j0 + t)])
            pv = self.ps[b][:, :].rearrange("p (a n) -> p a n", n=128)
            for g in range(2):
                bsl = bv[:, g * 64:g * 64 + 64].unsqueeze(1).broadcast_to([128, 4, 64])
                self.tt("dve", Vg[:, j0:j0 + 4, g, 64:128], pv[:, :, g * 64:g * 64 + 64], bsl, ALU.add,
                        reads=[("bvg",)], writes=[("gv", j0 + t) for t in range(4)], excl=[("ps", b)])
        kv_pend()
        hcount = 0
        for c in range(4):
            g = c // 2
            (wt,), wres = self.take_w()
            pend = None
            for tc in range(4):
                b = proj_chunk(wt, wres, 0, tc)
                st2 = rope_stage1(tc, b, CH_GQ(c, 0), 64, qT[:, tc * 512:(tc + 1) * 512], ("gq", tc))
                if pend is not None:
                    pend()
                pend = st2
            q_pend = pend

            def ev_z(tc, b, ch=CH_GQ(c, 1)):
                self.act(sz[:, tc * 512:(tc + 1) * 512], self.ps[b][:, :], AF.Silu, bias=self.bias(ch),
                         reads=[("vecs",)], writes=[("gsz", tc)], excl=[("ps", b)])
            self.proj_fm(wt, wres, 128, ev_z)
            q_pend()
            units = []
            for q8 in range(8):
                q0 = q8 * 256
                tcq = q0 // 512
                obs = obanks[hcount % 2]
                slot = hcount % 2
                hcount += 1
                tl = {0: [], 1: []}
                for hi in range(2):
                    hb = 64 * hi
                    ob = obs[hi]

                    def epi(ob=ob, hb=hb, slot=slot, hi=hi, q0=q0, tcq=tcq, c=c):
                        self.attn_epilogue(ob, 256, hb, rs[slot][hi], sz[hb:hb + 64, q0:q0 + 256], ("gsz", tcq),
                                           self.yB[hb:hb + 64, c, q0:q0 + 256], ("yB", c, tcq), (slot, hi))
                    for j in range(16):
                        vl = Vg[:, j, g, 64:192] if hi == 0 else Vg[:, j, g, 0:128]
                        tl[hi].append(dict(
                            n=256, s_lhsT=kT2[hb:hb + 64, g, j * 128:(j + 1) * 128], s_rhs=qT[hb:hb + 64, q0:q0 + 256],
                            s_reads=[("gk", g, j // 4), ("gq", tcq)], etab=None, etab_res=None,
                            v_lhsT=vl, v_reads=[("gv", j)], ob=ob, first=(j == 0), last=(j == 15),
                            epi=epi if j == 15 else None))
                for i in range(0, 16, 2):
                    units.append([tl[0][i], tl[0][i + 1], tl[1][i], tl[1][i + 1]])
            self.attn_pipeline(units, Pb, None, regions, 8.0, look=2, s_order=[0, 2, 1, 3])
        if self.dbg:
            self.dbg_dump("yB%d" % l, self.yB, [("yB", c, t) for c in range(4) for t in range(4)])

    def phaseC(self, l):
        o = self.R_ph
        vn = self.A("vn", [128, 16, 512], BF16, o); o += 16384
        sgw = self.A("sgw", [128, 8, 128], BF16, o); o += 2048
        vraw = [self.A("vraw%d" % i, [128, 512], F32, o + 2048 * i) for i in range(2)]; o += 4096
        sgln = self.A("sgln", [128, 2, 512], F32, o); o += 4096
        bvs = self.A("bvs", [128, 512], F32, o); o += 2048
        uzb = [self.A("uz%d" % i, [128, 512], F32, o + 2048 * i) for i in range(2)]; o += 4096
        szt = [self.A("szt%d" % i, [128, 512], F32, o + 2048 * i) for i in range(2)]; o += 4096
        sgb = [self.A("sgb%d" % i, [128, 512], F32, o + 2048 * i) for i in range(2)]; o += 4096
        mix = [self.A("mix%d" % i, [128, 512], F32, o + 2048 * i) for i in range(2)]; o += 4096
        assert o <= SB_END, o
        self.pbanks = [0, 1, 2, 3, 4, 5, 6]
        d = self.d
        st = self.stat
        self.dma("sp", bvs[:, :], d["bvb"][l, :, 640:1152], writes=[("bvs",)])
        self.dma("sp", sgln[:, :, :], d["sgln"][l], writes=[("sgln",)])
        self.dma("pool", sgw[:, :, :], d["sgwT"][l], writes=[("sgw",)])
        (wt,), wres = self.take_w()
        for tt_i in range(16):
            b = self.bank()
            k = tt_i % 2
            for kc in range(8):
                self.mm(b, self.ps[b][:, :], self.xT[:, kc, tt_i * 128:(tt_i + 1) * 128], wt[:, kc, 0:512],
                        kc == 0, kc == 7, reads=[wres, ("xT", tt_i)])
            vr = vraw[k]
            self.tt("dve", vr[:, :], self.ps[b][:, :], bvs[:, :], ALU.add, reads=[("bvs",)], writes=[("vraw", k)],
                    excl=[("ps", b)])
            sk = 16 * k
            self.s.op("dve", (lambda vr, sk: lambda e: e.bn_stats(st[:, sk:sk + 6], vr[:, :]))(vr, sk),
                      reads=[("vraw", k)], writes=[("stat", k)])
            self.s.op("dve", (lambda sk: lambda e: e.bn_aggr(st[:, sk + 12:sk + 14], st[:, sk:sk + 6]))(sk),
                      reads=[("stat", k)], writes=[("stat", k)])
            self.act(st[:, sk + 14:sk + 15], st[:, sk + 13:sk + 14], AF.Sqrt, bias=1e-5,
                     reads=[("stat", k)], writes=[("stat", k)])
            self.s.op("dve", (lambda sk: lambda e: e.reciprocal(st[:, sk + 14:sk + 15], st[:, sk + 14:sk + 15]))(sk),
                      reads=[("stat", k)], writes=[("stat", k)])
            self.stt("dve", vr[:, :], vr[:, :], st[:, sk + 12:sk + 13], sgln[:, 0, :], ALU.subtract, ALU.mult,
                     reads=[("vraw", k), ("stat", k), ("sgln",)], writes=[("vraw", k)])
            self.act(vr[:, :], vr[:, :], AF.Copy, scale=st[:, sk + 14:sk + 15], reads=[("vraw", k), ("stat", k)],
                     writes=[("vraw", k)])
            self.tt("pool", vn[:, tt_i, :], vr[:, :], sgln[:, 1, :], ALU.add, reads=[("vraw", k), ("sgln",)],
                    writes=[("vn", tt_i)])
        cnt = 0
        for c in range(4):
            (wt,), wres = self.take_w()
            self.dma("sp", sgb[c % 2][:, :], d["sgb"][l, :, c, :], writes=[("sgb", c % 2)])
            for tc in range(4):
                bu = self.bank()
                for kc in range(8):
                    self.mm(bu, self.ps[bu][:, :], wt[:, kc, 0:128], self.xT[:, kc, tc * 512:(tc + 1) * 512],
                            kc == 0, kc == 7, reads=[wres] + self.xT_res(tc))
                bz = self.bank()
                for kc in range(8):
                    self.mm(bz, self.ps[bz][:, :], wt[:, kc, 128:256], self.xT[:, kc, tc * 512:(tc + 1) * 512],
                            kc == 0, kc == 7, reads=[wres] + self.xT_res(tc))
                k = cnt % 2
                cnt += 1
                self.act(szt[k][:, :], self.ps[bz][:, :], AF.Silu, bias=self.bias(CH_SG(c, 1)), reads=[("vecs",)],
                         writes=[("szt", k)], excl=[("ps", bz)])
                uzs = uzb[k][:, :]
                self.stt("dve", uzs, self.ps[bu][:, :], self.bias(CH_SG(c, 0)), szt[k][:, :], ALU.add, ALU.mult,
                         reads=[("vecs",), ("szt", k)], writes=[("uz", k)], excl=[("ps", bu)])
                bm = [self.bank(), self.bank()]
                for gi in range(2):
                    for t in range(4):
                        tt_i = 4 * tc + t
                        self.mm(bm[gi], self.ps[bm[gi]][:, t * 128:(t + 1) * 128], vn[:, tt_i, c * 128:(c + 1) * 128],
                                sgw[:, 2 * c + gi, :], True, True, reads=[("vn", tt_i), ("sgw",)])
                mk = mix[k]
                for gi in range(2):
                    hb = 64 * gi
                    self.tt("dve", mk[hb:hb + 64, :], self.ps[bm[gi]][hb:hb + 64, :], sgb[c % 2][hb:hb + 64, :], ALU.add,
                            reads=[("sgb", c % 2)], writes=[("mix", k)], excl=[("ps", bm[gi])])
                self.tt("pool", self.yC[:, c, tc * 512:(tc + 1) * 512], mk[:, :], uzs, ALU.mult,
                        reads=[("mix", k), ("uz", k)], writes=[("yC", c, tc)])
        if self.dbg:
            self.dbg_dump("yC%d" % l, self.yC, [("yC", c, t) for c in range(4) for t in range(4)])

    def phaseD(self, l, last):
        o = self.R_ph
        mT = self.A("mT", [128, 8, S], BF16, o); o += 32768
        gsig = [self.A("gsig%d" % i, [128, 512], F32, o + 2048 * i) for i in range(2)]; o += 4096
        mt = [self.A("mt%d" % i, [128, 512], F32, o + 2048 * i) for i in range(3)]; o += 6144
        assert o <= SB_END, o
        self.pbanks = [0, 1, 2, 3, 4, 5, 6]
        d = self.d
        gcnt = 0
        for dc in range(8):
            (wg, wb), wres = self.take_w()
            for tc in range(4):
                for br in range(3):
                    bg = self.bank()
                    for kc in range(8):
                        self.mm(bg, self.ps[bg][:, :], wg[:, kc, br * 128:(br + 1) * 128],
                                self.xT[:, kc, tc * 512:(tc + 1) * 512], kc == 0, kc == 7, reads=[wres] + self.xT_res(tc))
                    k = gcnt % 2
                    gcnt += 1
                    self.act(gsig[k][:, :], self.ps[bg][:, :], AF.Sigmoid, bias=self.bias(CH_GATE(dc, br)),
                             reads=[("vecs",)], writes=[("gsig", k)], excl=[("ps", bg)])
                    bp = self.bank()
                    yb = self.ybr[br]
                    nm = ("yA", "yB", "yC")[br]
                    for kc in range(4):
                        self.mm(bp, self.ps[bp][:, :], wb[:, kc, br * 128:(br + 1) * 128], yb[:, kc, tc * 512:(tc + 1) * 512],
                                kc == 0, kc == 3, reads=[wres, (nm, kc, tc)])
                    mb = mt[0] if br == 0 else mt[1]
                    mk = ("mt", 0 if br == 0 else 1)
                    self.tt("dve", mb[:, :], self.ps[bp][:, :], gsig[k][:, :], ALU.mult, reads=[("gsig", k)],
                            writes=[mk], excl=[("ps", bp)])
                    if br == 1:
                        self.tt("dve", mt[0][:, :], mt[0][:, :], mt[1][:, :], ALU.add, reads=[("mt", 0), ("mt", 1)],
                                writes=[("mt", 0)])
                    elif br == 2:
                        self.tt("dve", mT[:, dc, tc * 512:(tc + 1) * 512], mt[0][:, :], mt[1][:, :], ALU.add,
                                reads=[("mt", 0), ("mt", 1)], writes=[("mT", tc)])
        self.barrier()
        oa = self.R_y
        post = self.A("post", [128, 3, D], F32, oa); oa += 12288
        h16 = [self.A("dh16%d" % i, [128, D], BF16, oa + 2048 * i) for i in range(2)]; oa += 4096
        self.dma("sp", post[:, :, :], d["post"][l], writes=[("post",)])
        for hf in range(2):
            (wo,), wres = self.take_w()
            for tt_i in range(16):
                tcq = tt_i // 4
                b = self.bank()
                for dc in range(8):
                    self.mm(b, self.ps[b][:, :], mT[:, dc, tt_i * 128:(tt_i + 1) * 128], wo[:, dc, 0:512],
                            dc == 0, dc == 7, reads=[("mT", tcq), wres])
                x_ap = self.xtok[:, tt_i, hf * 512:(hf + 1) * 512]
                self.stt("dve", x_ap, x_ap, float(ALPHA), self.ps[b][:, :], ALU.mult, ALU.add,
                         reads=[("xtok", tt_i)], writes=[("xtok", tt_i)], excl=[("ps", b)])
                if hf == 1:
                    self.tt("pool", self.xtok[:, tt_i, :], self.xtok[:, tt_i, :], post[:, 0, :], ALU.add,
                            reads=[("xtok", tt_i), ("post",)], writes=[("xtok", tt_i)])
        self.ln_stats(self.xtok[:, 0, :], ("xtok", 0), 0)
        for t in range(16):
            if t + 1 < 16:
                self.ln_stats(self.xtok[:, t + 1, :], ("xtok", t + 1), t + 1)
            self.ln_apply(self.xtok[:, t, :], ("xtok", t), t, post[:, 1, :], post[:, 2, :], ("post",))
            if last:
                self.dma("sp", self.y_d[t * 128:(t + 1) * 128, :], self.xtok[:, t, :], reads=[("xtok", t)])
            else:
                self.to_xT_finish()
                self.to_xT(t, h16)
        self.to_xT_finish()
        self.barrier()

    def layer(self, l, last):
        self.dma("sp", self.vecs[:, :], self.d["vecs"][l], writes=[("vecs",)])
        self.phaseA(l)
        self.barrier()
        self.phaseB(l)
        self.barrier()
        self.phaseC(l)
        self.barrier()
        self.phaseD(l, last)

    def build(self):
        from contextlib import ExitStack
        nc = self.nc
        self.phase0()
        self.barrier()
        for i, l in enumerate(self.layers):
            self.layer(l, last=(i == len(self.layers) - 1))
        self.s.wait_all("sp")
        with ExitStack() as es:
            sems = {e: es.enter_context(nc.semaphore("s_" + e)) for e in Sched.ENGS}
            dsems = [es.enter_context(nc.semaphore("q%d" % i)) for i in range(self.s.n_dma)]
            block = es.enter_context(nc.Block())
            self.s.emit(nc, block, sems, dsems)
        return nc


def _run(layers, do_ln_in, x_list, shared, dbg=False):
    kb = KB(layers, do_ln_in, dbg)
    nc = kb.build()
    in_maps = []
    for xb in x_list:
        m = dict(shared)
        m["x"] = np.ascontiguousarray(xb, dtype=np.float32)
        in_maps.append(m)
    return run_bass_kernel_spmd(nc, in_maps, core_ids=list(range(len(x_list))))


def kernel(**inputs):
    x = np.asarray(inputs["x"], dtype=np.float32)
    shared = prep_inputs(**inputs)
    res = _run([0, 1], True, [x[b] for b in range(x.shape[0])], shared)
    return np.stack([r["y"] for r in res.results], axis=0).astype(np.float32)
```

```python
import numpy as np
import concourse.bass as bass
import concourse.mybir as mybir
from concourse.bass_utils import run_bass_kernel_spmd

F32, BF16 = mybir.dt.float32, mybir.dt.bfloat16
AF = mybir.ActivationFunctionType
ALU = mybir.AluOpType

S = 2048
D = 1024
L = 2
NCH = 63
WIN = NCH * 128
ALPHA = (2.0 * L) ** 0.25
SB_BASE = 16512
SB_END = 229376
EMBED_WAIT = ("act", "dve", "pool")


class Sched:
    ENGS = ("pe", "act", "dve", "pool", "sp")

    def __init__(self, n_dma_sems=24):
        self.prog = {e: [] for e in self.ENGS}
        self.serial = {e: 0 for e in self.ENGS}
        self.seen = {e: {} for e in self.ENGS}
        self.lastw = {}
        self.readers = {}
        self.lastx = {}
        self.waited = {e: set() for e in self.ENGS}
        self.n_dma = n_dma_sems
        self.dma_i = 0
        self.dma_ip = 0
        self.dma_val = [0] * n_dma_sems

    def _deps(self, eng, reads, writes, excl):
        need = {}

        def add(tok, raw, isx=False):
            key, val, teng = tok
            if teng == eng and isx:
                return
            if self.seen[eng].get(key, 0) >= val:
                return
            if need.get(key, 0) < val:
                need[key] = val

        for r in reads:
            t = self.lastw.get(r)
            if t:
                add(t, True)
        for w in writes:
            t = self.lastw.get(w)
            if t:
                add(t, False)
            for t in self.readers.get(w, {}).values():
                add(t, False)
        for x in excl:
            t = self.lastx.get(x)
            if t:
                add(t, False, True)
        for key, val in need.items():
            self.seen[eng][key] = val
            self.prog[eng].append(("wait", key, val))
            if key[0] != "q":
                self.waited[key].add(val)

    def _commit(self, tok, reads, writes, excl):
        for r in reads:
            self.readers.setdefault(r, {})[tok[0]] = tok
        for w in writes:
            self.lastw[w] = tok
            self.readers[w] = {}
        for x in excl:
            self.lastx[x] = tok

    def op(self, eng, fn, reads=(), writes=(), excl=()):
        self._deps(eng, reads, writes, excl)
        self.serial[eng] += 1
        tok = (eng, self.serial[eng], eng)
        self.prog[eng].append(("op", fn, self.serial[eng]))
        self._commit(tok, reads, writes, excl)
        return tok

    def dma(self, eng, fn, reads=(), writes=()):
        self._deps(eng, reads, writes, ())
        if eng == "pool":
            k = 16 + self.dma_ip % (self.n_dma - 16)
            self.dma_ip += 1
        else:
            k = self.dma_i % 16
            self.dma_i += 1
        key = "q%d" % k
        prev = self.dma_val[k]
        if prev > 0 and self.seen[eng].get(key, 0) < prev:
            self.seen[eng][key] = prev
            self.prog[eng].append(("wait", key, prev))
        self.dma_val[k] += 16
        tok = (key, self.dma_val[k], None)
        self.prog[eng].append(("dma", fn, k))
        self._commit(tok, reads, writes, ())
        return tok

    def wait_all(self, eng, dmas=True):
        for e in self.ENGS:
            if e != eng and self.serial[e] > 0 and self.seen[eng].get(e, 0) < self.serial[e]:
                self.seen[eng][e] = self.serial[e]
                self.prog[eng].append(("wait", e, self.serial[e]))
                self.waited[e].add(self.serial[e])
        for k in range(self.n_dma if dmas else 0):
            key = "q%d" % k
            if self.dma_val[k] > 0 and self.seen[eng].get(key, 0) < self.dma_val[k]:
                self.seen[eng][key] = self.dma_val[k]
                self.prog[eng].append(("wait", key, self.dma_val[k]))

    def emit(self, nc, block, sems, dsems):
        rank = {}
        for e in self.ENGS:
            ws = sorted(self.waited[e])
            rank[e] = {s: i + 1 for i, s in enumerate(ws)}
        handles = {"pe": "tensor", "act": "scalar", "dve": "vector", "pool": "gpsimd", "sp": "sync"}

        def run(ename):
            def body(eng):
                pend = []

                def semval(key, val):
                    if key[0] == "q":
                        return dsems[int(key[1:])], val
                    return sems[key], rank[key][val]

                for item in self.prog[ename]:
                    if item[0] == "wait":
                        pend.append(semval(item[1], item[2]))
                        continue
                    embed = None
                    if pend and item[0] == "op" and ename in EMBED_WAIT:
                        embed = pend.pop()
                    for (sm, v) in pend:
                        eng.wait_ge(sm, v)
                    pend = []
                    ins = item[1](eng)
                    if embed is not None:
                        ins._wait_ge(embed[0], embed[1])
                    if item[0] == "op":
                        if item[2] in rank[ename]:
                            ins.then_inc(sems[ename], 1)
                    else:
                        ins.then_inc(dsems[item[2]], 16)
                for (sm, v) in pend:
                    eng.wait_ge(sm, v)
            return body

        for ename in self.ENGS:
            getattr(block, handles[ename])(run(ename))


def _win_cols():
    naq, nak, nav, naz, gqq, gqk, gqv, gqz, sgu, sgv, sgz, gat = (0, 512, 1024, 1536, 2048, 2560, 2688, 2816,
                                                                 3328, 3840, 4352, 4864)
    r = lambda a, n: list(range(a, a + n))
    cols = []
    for c in range(4):
        cols += r(naq + c * 128, 128) + r(nak + c * 128, 128) + r(nav + c * 128, 128) + r(naz + c * 128, 128)
    cols += r(gqk, 64) + r(gqk, 64) + r(gqk + 64, 64) + r(gqk + 64, 64) + r(gqv, 128)
    for c in range(4):
        cols += r(gqq + c * 128, 128) + r(gqz + c * 128, 128)
    cols += r(sgv, 512)
    for c in range(4):
        cols += r(sgu + c * 128, 128) + r(sgz + c * 128, 128)
    for dc in range(8):
        for b in range(3):
            cols += r(gat + b * 1024 + dc * 128, 128)
    assert len(cols) == WIN
    return np.asarray(cols)


def CH_NA(c, which):
    return 4 * c + which
CH_GK = (16, 17)
CH_GV = 18
def CH_GQ(c, which):
    return 19 + 2 * c + which
CH_SV = 27
def CH_SG(c, which):
    return 31 + 2 * c + which
def CH_GATE(dc, b):
    return 39 + 3 * dc + b


def _na_index_tables():
    p = np.arange(128)
    ck = p % 64
    half = p // 64
    s = np.arange(2, 16)
    cq = np.arange(64)
    dr = (8 - s)[None, :, None] + half[:, None, None] + 0 * cq[None, None, :]
    dcol = np.clip(ck[:, None, None] - cq[None, None, :] + 15, 0, 30) + 0 * s[None, :, None]
    cs = np.clip(cq - 8, 0, 48)
    colv = (ck[:, None, None] >= cs[None, None, :]) & (ck[:, None, None] < cs[None, None, :] + 16)
    colv = colv & (s[None, :, None] > -100)
    full_ok = colv & (np.abs(dr) <= 7)
    int_ok = colv & (dr >= -4) & (dr <= 3)
    ridx = np.clip(dr + 7, 0, 14)
    return ridx, dcol, full_ok, int_ok


def _rope_tables():
    t = np.arange(S)
    row = (t // 64).astype(np.float64)
    col = (t % 64).astype(np.float64)
    freqs = 10000.0 ** (-np.arange(0, 32, 2, dtype=np.float64) / 32.0)
    ang = np.concatenate([row[:, None] * freqs, col[:, None] * freqs], axis=-1)
    cos = np.cos(ang)
    sin = np.sin(ang)
    p = np.arange(128)
    d = p % 64
    C = cos[:, d // 2].T
    Sg = (sin[:, d // 2] * np.where(d % 2 == 0, -1.0, 1.0)[None, :]).T
    return np.ascontiguousarray(np.stack([C, Sg], axis=1)).astype(np.float32)


def _consts32():
    perm = np.zeros((128, 128), np.float32)
    for i in range(64):
        perm[2 * i + 1, 2 * i] = 1.0
        perm[2 * i, 2 * i + 1] = 1.0
    k = np.arange(128)
    bones = (k[:, None] // 64 == k[None, :] // 64).astype(np.float32)
    return np.concatenate([perm, bones], axis=1)


def prep_inputs(x, ln_in_g, ln_in_b, w_in, b_in, na_rpb, q_norm_g, k_norm_g, sg_ln_g, sg_ln_b, sg_w, sg_b,
                w_br_a, w_br_b, w_br_c, w_out, b_out, ln_post_g, ln_post_b):
    f = lambda a: np.ascontiguousarray(np.asarray(a, dtype=np.float32))
    cols = _win_cols()
    shared = {}
    shared["w_in_p"] = f(np.take(w_in, cols, axis=2))
    bp = np.take(b_in, cols, axis=1)
    vecs = np.zeros((L, 128, 80), np.float32)
    vecs[:, :, 0:NCH] = bp.reshape(L, NCH, 128).transpose(0, 2, 1)
    vecs[:, :, 64] = np.tile(q_norm_g, (1, 2))
    vecs[:, :, 65] = np.tile(k_norm_g, (1, 2))
    shared["vecs"] = vecs
    bv = np.concatenate([b_in[:, 1024:1536], b_in[:, 2688:2816], b_in[:, 3840:4352]], axis=1)
    shared["bvb"] = f(np.broadcast_to(bv[:, None, :], (L, 128, 1152)))
    shared["lnin"] = f(np.broadcast_to(np.stack([ln_in_g, ln_in_b])[None], (128, 2, D)))
    ridx, dcol, full_ok, int_ok = _na_index_tables()
    rp = np.asarray(na_rpb, np.float32)
    g = rp[:, :, ridx, dcol]
    neg = np.float32(-1e30)
    tfull = np.where(full_ok[None, None], g, neg)
    tint = np.where(int_ok[None, None], g, neg)
    shared["natab"] = f(np.stack([tfull, tint], axis=3).reshape(L, 8, 128, 2 * 896))
    shared["rope"] = _rope_tables()
    shared["c32"] = _consts32()
    shared["ident"] = np.eye(128, dtype=np.float32)
    shared["sgln"] = f(np.broadcast_to(np.stack([sg_ln_g, sg_ln_b], axis=1)[:, None], (L, 128, 2, 512)))
    shared["sgwT"] = f(np.transpose(sg_w, (0, 3, 1, 2)))
    sb = np.asarray(sg_b, np.float32)
    sbb = sb.reshape(L, 4, 2, 1, 128)
    sbb = np.broadcast_to(sbb, (L, 4, 2, 64, 128)).reshape(L, 4, 128, 128)
    sbb = np.broadcast_to(sbb[:, :, :, None, :], (L, 4, 128, 4, 128)).reshape(L, 4, 128, 512)
    shared["sgb"] = f(sbb.transpose(0, 2, 1, 3))
    wf = np.stack([w_br_a, w_br_b, w_br_c], axis=2)
    wf = wf.reshape(L, 512, 3, 8, 128).transpose(0, 1, 3, 2, 4).reshape(L, 512, 3072)
    shared["wfin"] = f(wf)
    shared["wout"] = f(w_out)
    shared["post"] = f(np.broadcast_to(np.stack([b_out, ln_post_g, ln_post_b], axis=1)[:, None], (L, 128, 3, D)))
    return shared


DUMMY_MM = 0
POOL_EVERY = 0


class KB:
    def __init__(self, layers, do_ln_in, dbg=False):
        self.layers = list(layers)
        self.do_ln_in = do_ln_in
        self.dbg = dbg
        self.s = Sched()
        nc = self.nc = bass.Bass("TRN2", target_bir_lowering=False)
        di = lambda n, shp: nc.dram_tensor(n, shp, F32, kind="ExternalInput").ap()
        self.d = dict(
            x=di("x", [S, D]), lnin=di("lnin", [128, 2, D]), w_in_p=di("w_in_p", [L, D, WIN]),
            vecs=di("vecs", [L, 128, 80]), bvb=di("bvb", [L, 128, 1152]), natab=di("natab", [L, 8, 128, 1792]),
            rope=di("rope", [128, 2, S]), c32=di("c32", [128, 256]), ident=di("ident", [128, 128]),
            sgln=di("sgln", [L, 128, 2, 512]), sgwT=di("sgwT", [L, 128, 8, 128]), sgb=di("sgb", [L, 128, 4, 512]),
            wfin=di("wfin", [L, 512, 3072]), wout=di("wout", [L, D, D]), post=di("post", [L, 128, 3, D]),
        )
        self.y_d = nc.dram_tensor("y", [S, D], F32, kind="ExternalOutput").ap()
        self.dbg_d = {}
        self._nm = 0
        o = SB_BASE
        self.xtok = self.A("xtok", [128, 16, D], F32, o); o += 65536
        self.xT = self.A("xT", [128, 8, S], BF16, o); o += 32768
        self.R_y = o
        self.yA = self.A("yA", [128, 4, S], BF16, o); o += 16384
        self.yB = self.A("yB", [128, 4, S], BF16, o); o += 16384
        self.yC = self.A("yC", [128, 4, S], BF16, o); o += 16384
        self.ybr = [self.yA, self.yB, self.yC]
        self.ident = self.A("ident", [128, 128], BF16, o); o += 256
        self.c32 = self.A("c32", [128, 256], F32, o); o += 1024
        self.vecs = self.A("vecs", [128, 80], F32, o); o += 320
        self.stat = self.A("stat", [128, 64], F32, o); o += 256
        self.R_loc = o
        self.wslot = [self.A("wslot0", [128, 4608], BF16, o), self.A("wslot1", [128, 4608], BF16, o + 9216)]
        o += 18432
        self.R_ph = o
        self.psall = nc.alloc_psum_tensor("psall", [128, 4096], F32)
        self.ps = [self.psall[:, b * 512:(b + 1) * 512] for b in range(8)]
        self.pst = self.psall[:, 3584:4096].bitcast(BF16)
        self.pbanks = [0, 1, 2, 3, 4, 5, 6]
        self.pb_i = 0
        self.wjobs = []
        for l in self.layers:
            self.wjobs += self.layer_wjobs(l)
        self.w_issued = 0
        self.w_used = 0
        self._xT_pending = None

    def A(self, name, shape, dtype, off):
        nbytes = int(np.prod(shape[1:])) * (4 if dtype == F32 else 2)
        assert off + nbytes <= SB_END, (name, off, nbytes)
        self._nm += 1
        return self.nc.alloc_sbuf_tensor_at("%s_%d" % (name, self._nm), list(shape), dtype, offset=off)

    def bank(self):
        b = self.pbanks[self.pb_i % len(self.pbanks)]
        self.pb_i += 1
        return b

    def mm(self, banks, out, lhsT, rhs, start, stop, reads):
        if isinstance(banks, int):
            banks = [banks]
        self.s.op("pe", lambda e: e.matmul(out, lhsT, rhs, start=start, stop=stop), reads=reads,
                  excl=[("ps", b) for b in banks])

    def act(self, out, in_, func, reads=(), writes=(), excl=(), **kw):
        self.s.op("act", lambda e: e.activation(out, in_, func, **kw), reads=reads, writes=writes, excl=excl)

    def tt(self, eng, out, in0, in1, op, reads=(), writes=(), excl=()):
        self.s.op(eng, lambda e: e.tensor_tensor(out, in0, in1, op), reads=reads, writes=writes, excl=excl)

    def ts(self, eng, out, in0, s1, s2, op0, op1=None, reads=(), writes=(), excl=()):
        if op1 is None:
            self.s.op(eng, lambda e: e.tensor_scalar(out, in0, s1, None, op0), reads=reads, writes=writes, excl=excl)
        else:
            self.s.op(eng, lambda e: e.tensor_scalar(out, in0, s1, s2, op0, op1), reads=reads, writes=writes, excl=excl)

    def stt(self, eng, out, in0, scalar, in1, op0, op1, reads=(), writes=(), excl=()):
        self.s.op(eng, lambda e: e.scalar_tensor_tensor(out, in0, scalar, in1, op0, op1), reads=reads, writes=writes,
                  excl=excl)

    def cp(self, eng, out, in_, reads=(), writes=(), excl=()):
        self.s.op(eng, lambda e: e.tensor_copy(out, in_), reads=reads, writes=writes, excl=excl)

    def dma(self, eng, out, in_, reads=(), writes=()):
        self.s.dma(eng, lambda e: e.dma_start(out=out, in_=in_), reads=reads, writes=writes)

    def layer_wjobs(self, l):
        w = self.d["w_in_p"]
        jobs = []
        for c in range(4):
            jobs.append([(0, 8, 512, w[l, :, c * 512:(c + 1) * 512])])
        g0 = CH_GK[0] * 128
        jobs.append([(0, 8, 384, w[l, :, g0:g0 + 384])])
        for c in range(4):
            c0 = CH_GQ(c, 0) * 128
            jobs.append([(0, 8, 256, w[l, :, c0:c0 + 256])])
        c0 = CH_SV * 128
        jobs.append([(0, 8, 512, w[l, :, c0:c0 + 512])])
        for c in range(4):
            c0 = CH_SG(c, 0) * 128
            jobs.append([(0, 8, 256, w[l, :, c0:c0 + 256])])
        for dc in range(8):
            g0 = CH_GATE(dc, 0) * 128
            jobs.append([(0, 8, 384, w[l, :, g0:g0 + 384]),
                         (3072, 4, 384, self.d["wfin"][l, :, dc * 384:(dc + 1) * 384])])
        for hf in range(2):
            jobs.append([(0, 8, 512, self.d["wout"][l, :, hf * 512:(hf + 1) * 512])])
        return jobs

    def _issue_w(self):
        k = self.w_issued
        if k >= len(self.wjobs):
            return
        self.w_issued += 1
        i = k % 2
        for (off, kch, ncols, src) in self.wjobs[k]:
            dst = self.wslot[i][:, off:off + kch * ncols].rearrange("p (k n) -> p k n", n=ncols)
            self.dma("pool", dst, src.rearrange("(k p) n -> p k n", p=128), writes=[("w", i)])

    def take_w(self):
        k = self.w_used
        self.w_used += 1
        while self.w_issued <= min(k + 1, len(self.wjobs) - 1):
            self._issue_w()
        i = k % 2
        views = []
        for (off, kch, ncols, src) in self.wjobs[k]:
            views.append(self.wslot[i][:, off:off + kch * ncols].rearrange("p (k n) -> p k n", n=ncols))
        return views, ("w", i)

    def xT_res(self, tc):
        return [("xT", 4 * tc + i) for i in range(4)]

    def proj_fm(self, wt, wres, col0, evac):
        for tc in range(4):
            b = self.bank()
            for kc in range(8):
                self.mm(b, self.ps[b][:, :], wt[:, kc, col0:col0 + 128], self.xT[:, kc, tc * 512:(tc + 1) * 512],
                        kc == 0, kc == 7, reads=[wres] + self.xT_res(tc))
            evac(tc, b)

    def bias(self, ch):
        return self.vecs[:, ch:ch + 1]

    def ln_stats(self, src, src_res, tt_i):
        st = self.stat
        k = (tt_i % 4) * 16
        for h in range(2):
            self.s.op("dve", (lambda h: lambda e: e.bn_stats(st[:, k + h * 6:k + h * 6 + 6],
                                                               src[:, h * 512:(h + 1) * 512]))(h),
                      reads=[src_res], writes=[("stat", tt_i % 4)])
        self.s.op("dve", lambda e: e.bn_aggr(st[:, k + 12:k + 14], st[:, k:k + 12]),
                  reads=[("stat", tt_i % 4)], writes=[("stat", tt_i % 4)])
        self.act(st[:, k + 14:k + 15], st[:, k + 13:k + 14], AF.Sqrt, bias=1e-5,
                 reads=[("stat", tt_i % 4)], writes=[("stat", tt_i % 4)])

    def ln_apply(self, src, src_res, tt_i, g_ap, b_ap, gb_res):
        st = self.stat
        k = (tt_i % 4) * 16
        self.s.op("dve", lambda e: e.reciprocal(st[:, k + 14:k + 15], st[:, k + 14:k + 15]),
                  reads=[("stat", tt_i % 4)], writes=[("stat", tt_i % 4)])
        xt = self.xtok[:, tt_i, :]
        self.stt("dve", xt, src, st[:, k + 12:k + 13], g_ap, ALU.subtract, ALU.mult,
                 reads=[src_res, ("stat", tt_i % 4), gb_res], writes=[("xtok", tt_i)])
        self.act(xt, xt, AF.Copy, scale=st[:, k + 14:k + 15], reads=[("xtok", tt_i), ("stat", tt_i % 4)],
                 writes=[("xtok", tt_i)])
        self.tt("pool", xt, xt, b_ap, ALU.add, reads=[("xtok", tt_i), gb_res], writes=[("xtok", tt_i)])

    def to_xT(self, tt_i, tmp_h16):
        if self._xT_pending is not None:
            self.to_xT_finish()
        k = tt_i % 2
        h16 = tmp_h16[k]
        self.act(h16[:, :], self.xtok[:, tt_i, :], AF.Copy, reads=[("xtok", tt_i)], writes=[("h16", k)])
        for kc in range(8):
            self.s.op("pe", (lambda kc: lambda e: e.transpose(self.pst[:, kc * 128:(kc + 1) * 128],
                                                              h16[:, kc * 128:(kc + 1) * 128], self.ident[:, :]))(kc),
                      reads=[("h16", k), ("ident",)], excl=[("ps", 7)])
        self._xT_pending = tt_i

    def to_xT_finish(self):
        tt_i = self._xT_pending
        if tt_i is None:
            return
        self._xT_pending = None
        self.act(self.xT[:, :, tt_i * 128:(tt_i + 1) * 128], self.pst[:, :].rearrange("p (k n) -> p k n", n=128), AF.Copy,
                 writes=[("xT", tt_i)], excl=[("ps", 7)])

    def phase0(self):
        o = self.R_ph
        gb = self.A("lnin", [128, 2, D], F32, o); o += 8192
        xin = [self.A("xin%d" % i, [128, D], F32, o + 4096 * i) for i in range(3)]; o += 12288
        h16 = [self.A("h16a", [128, D], BF16, o), self.A("h16b", [128, D], BF16, o + 2048)]; o += 4096
        self.dma("pool", self.ident[:, :], self.d["ident"], writes=[("ident",)])
        self.dma("sp", self.c32[:, :], self.d["c32"], writes=[("c32",)])
        self._issue_w()
        if self.do_ln_in:
            self.dma("sp", gb[:, :, :], self.d["lnin"], writes=[("lnin",)])

            def load(t):
                self.dma("sp", xin[t % 3][:, :], self.d["x"][t * 128:(t + 1) * 128, :], writes=[("xin", t % 3)])
            load(0)
            load(1)
            self.ln_stats(xin[0][:, :], ("xin", 0), 0)
            for t in range(16):
                if t + 2 < 16:
                    load(t + 2)
                if t + 1 < 16:
                    self.ln_stats(xin[(t + 1) % 3][:, :], ("xin", (t + 1) % 3), t + 1)
                self.ln_apply(xin[t % 3][:, :], ("xin", t % 3), t, gb[:, 0, :], gb[:, 1, :], ("lnin",))
                self.to_xT_finish()
                self.to_xT(t, h16)
            self.to_xT_finish()
        else:
            for tt_i in range(16):
                self.dma("sp", self.xtok[:, tt_i, :], self.d["x"][tt_i * 128:(tt_i + 1) * 128, :], writes=[("xtok", tt_i)])
                self.to_xT(tt_i, h16)
            self.to_xT_finish()

    def attn_pipeline(self, units, Pb, exb, regions, scale, look, s_order=None):
        nu = len(units)
        nr = len(regions)
        nb = len(Pb)
        mulcount = [0]

        def emitS(u, extra=()):
            banks = regions[u % nr]
            offs = []
            off = 0
            for t in units[u]:
                offs.append(off)
                off += t["n"]
            order = list(range(len(units[u]))) if s_order is None or len(units[u]) != len(s_order) else list(s_order)
            for c0 in range(0, len(order), 2):
                grp = order[c0:c0 + 2]
                for i in grp:
                    t = units[u][i]
                    n = t["n"]
                    base = banks[0] * 512 + offs[i]
                    self.mm([banks[offs[i] // 512]], self.psall[:, base:base + n], t["s_lhsT"], t["s_rhs"], True,
                            t.get("b_rhs") is None, reads=list(t["s_reads"]) + list(extra))
                for i in grp:
                    t = units[u][i]
                    if t.get("b_rhs") is None:
                        continue
                    n = t["n"]
                    base = banks[0] * 512 + offs[i]
                    self.mm([banks[offs[i] // 512]], self.psall[:, base:base + n], self.ident[:, :], t["b_rhs"], False, True,
                            reads=[("ident",), t["b_res"]])

        for u in range(min(look, nu)):
            emitS(u)
        for u in range(nu):
            banks = regions[u % nr]
            k = u % nb
            tot = sum(t["n"] for t in units[u])
            base = banks[0] * 512
            src = self.psall[:, base:base + tot]
            ex = [("ps", b) for b in banks]
            if units[u][0]["etab"] is None:
                self.act(Pb[k][:, 0:tot], src, AF.Exp, scale=scale,
                         writes=[("P", k, i) for i in range(len(units[u]))], excl=ex)
            else:
                self.act(exb[k][:, 0:tot], src, AF.Exp, scale=scale, writes=[("ex", k)], excl=ex)
                off = 0
                for i, t in enumerate(units[u]):
                    n = t["n"]
                    eng = "pool" if (POOL_EVERY and mulcount[0] % POOL_EVERY == POOL_EVERY - 1) else "dve"
                    mulcount[0] += 1
                    self.tt(eng, Pb[k][:, off:off + n], exb[k][:, off:off + n], t["etab"], ALU.mult,
                            reads=[("ex", k), t["etab_res"]], writes=[("P", k, i)])
                    off += n
            if u + look < nu:
                emitS(u + look)
            off = 0
            for i, t in enumerate(units[u]):
                n = t["n"]
                ob = t["ob"]
                self.mm(ob, self.ps[ob][:, 0:n], t["v_lhsT"], Pb[k][:, off:off + n], t["first"], t["last"],
                        reads=[("P", k, i)] + t["v_reads"])
                off += n
                for _ in range(DUMMY_MM):
                    self.mm(6, self.ps[6][:, 0:n], self.ident[:, :], Pb[k][:, 0:n], True, True, reads=[("P", k, i), ("ident",)])
                if t["last"] and t["epi"] is not None:
                    t["epi"]()

    def attn_epilogue(self, ob, n, hb, rs, sz_ap, sz_res, y_ap, y_res, slot):
        so = 64 - hb
        ps = self.ps[ob]
        self.s.op("dve", lambda e: e.reciprocal(rs[hb:hb + 64, 0:n], ps[so:so + 64, 0:n]),
                  writes=[("rs", slot)], excl=[("ps", ob)])
        self.tt("pool", rs[hb:hb + 64, 0:n], rs[hb:hb + 64, 0:n], sz_ap, ALU.mult,
                reads=[("rs", slot), sz_res], writes=[("rs", slot)])
        self.tt("dve", y_ap, ps[hb:hb + 64, 0:n], rs[hb:hb + 64, 0:n], ALU.mult,
                reads=[("rs", slot)], writes=[y_res], excl=[("ps", ob)])

    def phaseA(self, l):
        o = self.R_ph
        qT = self.A("naq", [128, S], BF16, o); o += 4096
        kT = self.A("nak", [128, S], BF16, o); o += 4096
        sz = self.A("nasz", [128, S], BF16, o); o += 4096
        Va = self.A("nava", [128, 16, 2, 128], BF16, o); o += 8192
        exb = None
        Pb = [self.A("P%d" % i, [128, 1024], BF16, o + 2048 * i) for i in range(3)]; o += 6144
        rs = [[self.A("rs%d%d" % (i, h), [128, 256], F32, o + 2048 * i + 1024 * h) for h in range(2)] for i in range(2)]
        o += 4096
        bv = self.A("bvna", [128, 512], F32, o); o += 2048
        oa = self.R_y + 32768
        tabraw = self.A("tabraw", [128, 1792], F32, oa); oa += 7168
        E = [self.A("E%d" % i, [128, 1792], BF16, oa + 3584 * i) for i in range(2)]; oa += 7168
        regions = [[0, 1], [2, 3]]
        obanks = [(4, 5), (6, 7)]
        self.pbanks = [0, 1, 2, 3, 4, 5, 6, 7]
        d = self.d
        self.dma("sp", bv[:, :], d["bvb"][l, :, 0:512], writes=[("bvna",)])
        self.s.op("pool", lambda e: e.memset(Va[:, :, 0, 64:128], 1.0), writes=[("nava", j) for j in range(16)])
        self.s.op("pool", lambda e: e.memset(Va[:, :, 1, 0:64], 1.0), writes=[("nava", j) for j in range(16)])
        hcount = 0
        ecount = 0
        for c in range(4):
            (wt,), wres = self.take_w()

            def ev_q(tc, b, dst=qT, ch=CH_NA(c, 0), nm="naq"):
                self.ts("dve", dst[:, tc * 512:(tc + 1) * 512], self.ps[b][:, :], self.bias(ch), None, ALU.add,
                        reads=[("vecs",)], writes=[(nm, tc)], excl=[("ps", b)])
            self.proj_fm(wt, wres, 0, ev_q)
            self.proj_fm(wt, wres, 128, lambda tc, b: ev_q(tc, b, kT, CH_NA(c, 1), "nak"))

            def ev_z(tc, b, ch=CH_NA(c, 3)):
                self.act(sz[:, tc * 512:(tc + 1) * 512], self.ps[b][:, :], AF.Silu, bias=self.bias(ch),
                         reads=[("vecs",)], writes=[("nasz", tc)], excl=[("ps", b)])
            self.proj_fm(wt, wres, 384, ev_z)
            for j0 in range(0, 16, 4):
                b = self.bank()
                for t in range(4):
                    for kc in range(8):
                        self.mm(b, self.ps[b][:, t * 128:(t + 1) * 128], self.xT[:, kc, (j0 + t) * 128:(j0 + t + 1) * 128],
                                wt[:, kc, 256:384], kc == 0, kc == 7, reads=[wres, ("xT", j0 + t)])
                pv = self.ps[b][:, :].rearrange("p (a n) -> p a n", n=128)
                for hi in range(2):
                    bsl = bv[:, c * 128 + hi * 64:c * 128 + hi * 64 + 64].unsqueeze(1).broadcast_to([128, 4, 64])
                    self.tt("dve", Va[:, j0:j0 + 4, hi, hi * 64:hi * 64 + 64], pv[:, :, hi * 64:hi * 64 + 64], bsl, ALU.add,
                            reads=[("bvna",)], writes=[("nava", j0 + t) for t in range(4)], excl=[("ps", b)])
            units = []
            eks = []
            for hi in range(2):
                ek = ecount % 2
                ecount += 1
                eks.append(ek)
                self.dma("sp", tabraw[:, :], d["natab"][l, 2 * c + hi], writes=[("tabraw",)])
                self.act(E[ek][:, :], tabraw[:, :], AF.Identity, scale=8.0, reads=[("tabraw",)], writes=[("E", ek)])
            for q8 in range(8):
                r0 = 4 * q8
                if q8 == 0:
                    jl, kind = [0, 1, 2, 3], 0
                elif q8 == 7:
                    jl, kind = [12, 13, 14, 15], 0
                else:
                    jl, kind = list(range(2 * q8 - 2, 2 * q8 + 4)), 1
                obs = obanks[hcount % 2]
                slot = hcount % 2
                hcount += 1
                q0 = q8 * 256
                tcq = q0 // 512
                tl = {0: [], 1: []}
                for hi in range(2):
                    hb = 64 * hi
                    ek = eks[hi]
                    ob = obs[hi]

                    def epi(ob=ob, hb=hb, slot=slot, hi=hi, q0=q0, tcq=tcq, c=c):
                        self.attn_epilogue(ob, 256, hb, rs[slot][hi], sz[hb:hb + 64, q0:q0 + 256], ("nasz", tcq),
                                           self.yA[hb:hb + 64, c, q0:q0 + 256], ("yA", c, tcq), (slot, hi))
                    for idx, j in enumerate(jl):
                        s0 = r0 - 2 * j + 8
                        assert 2 <= s0 and s0 + 4 <= 16
                        tl[hi].append(dict(
                            n=256, s_lhsT=kT[hb:hb + 64, j * 128:(j + 1) * 128], s_rhs=qT[hb:hb + 64, q0:q0 + 256],
                            s_reads=[("nak", j // 4), ("naq", tcq)],
                            etab=None, etab_res=None,
                            b_rhs=E[ek][:, kind * 896 + (s0 - 2) * 64:kind * 896 + (s0 + 2) * 64], b_res=("E", ek),
                            v_lhsT=Va[:, j, hi, :], v_reads=[("nava", j)], ob=ob,
                            first=(idx == 0), last=(idx == len(jl) - 1), epi=epi if idx == len(jl) - 1 else None))
                for i in range(0, len(jl), 2):
                    units.append([tl[0][i], tl[0][i + 1], tl[1][i], tl[1][i + 1]])
            self.attn_pipeline(units, Pb, exb, regions, 0.125, look=2, s_order=[0, 2, 1, 3])
        if self.dbg:
            self.dbg_dump("yA%d" % l, self.yA, [("yA", c, t) for c in range(4) for t in range(4)])

    def dbg_dump(self, name, t, reads):
        shp = list(t.shape)
        dt = t.dtype
        dd = self.nc.dram_tensor("dbg_" + name, shp, dt, kind="ExternalOutput").ap()
        self.dbg_d[name] = dd
        self.dma("sp", dd, t[tuple(slice(None) for _ in shp)], reads=reads)

    def barrier(self):
        for e in Sched.ENGS:
            self.s.wait_all(e, dmas=False)

    def phaseB(self, l):
        o = self.R_ph
        kT2 = self.A("gk", [128, 2, S], BF16, o); o += 8192
        qT = self.A("gq", [128, S], BF16, o); o += 4096
        sz = self.A("gsz", [128, S], BF16, o); o += 4096
        ropeb = [self.A("rope", [128, 2, 512], F32, o)] * 2; o += 4096
        sq2 = [self.A("sq%d" % i, [128, 512], F32, o + 2048 * i) for i in range(2)]; o += 4096
        qb2 = [self.A("qb%d" % i, [128, 512], F32, o + 2048 * i) for i in range(2)]; o += 4096
        u2 = [self.A("u%d" % i, [128, 512], F32, o + 2048 * i) for i in range(2)]; o += 4096
        rstd2 = [self.A("rstd", [128, 512], F32, o)] * 2; o += 2048
        t12 = [self.A("t1", [128, 512], F32, o)] * 2; o += 2048
        bv = self.A("bvg", [128, 128], F32, o); o += 512
        oa = self.R_y + 32768
        Vg = self.A("gv", [128, 16, 2, 192], BF16, oa); oa += 12288
        rs = [[self.A("grs%d%d" % (i, h), [128, 256], F32, oa + 2048 * i + 1024 * h) for h in range(2)] for i in range(2)]
        oa += 4096
        Pb = [self.A("gP%d" % i, [128, 1024], BF16, o + 2048 * i) for i in range(3)]; o += 6144
        assert o <= SB_END, o
        regions = [[0, 1], [2, 3]]
        obanks = [(4, 5), (6, 7)]
        self.pbanks = [0, 1, 2, 3, 4, 5, 6, 7]
        d = self.d
        c32 = self.c32
        self.dma("sp", bv[:, :], d["bvb"][l, :, 512:640], writes=[("bvg",)])
        self.s.op("pool", lambda e: e.memset(Vg[:, :, :, 0:64], 1.0), writes=[("gv", j) for j in range(16)])
        self.s.op("pool", lambda e: e.memset(Vg[:, :, :, 128:192], 1.0), writes=[("gv", j) for j in range(16)])
        self.rope_i = 0

        def rope_stage1(tc, b, bias_ch, gain_col, dst_ap, dst_res):
            rk = self.rope_i % 2
            self.rope_i += 1
            sq, qb, u, rstd, t1, t2 = sq2[rk], qb2[rk], u2[rk], rstd2[rk], t12[rk], sq2[rk]
            R = lambda nm: ("sq", rk) if nm == "t2" else (nm, rk if nm in ("qb", "sq", "u") else 0)
            ps = self.ps[b]
            self.act(sq[:, :], ps[:, :], AF.Square, bias=self.bias(bias_ch), reads=[("vecs",)], writes=[R("sq")],
                     excl=[("ps", b)])
            self.ts("dve", u[:, :], ps[:, :], self.bias(bias_ch), self.vecs[:, gain_col:gain_col + 1], ALU.add, ALU.mult,
                    reads=[("vecs",)], writes=[R("u")], excl=[("ps", b)])

            def stage2():
                self.dma("sp", ropeb[rk][:, :, :], d["rope"][:, :, tc * 512:(tc + 1) * 512], writes=[("rope", 0)])
                b2 = self.bank()
                self.mm(b2, self.ps[b2][:, :], c32[:, 128:256], sq[:, :], True, True, reads=[R("sq"), ("c32",)])
                b3 = self.bank()
                self.mm(b3, self.ps[b3][:, :], c32[:, 0:128], u[:, :], True, True, reads=[R("u"), ("c32",)])
                self.act(rstd[:, :], self.ps[b2][:, :], AF.Ln, bias=64e-6, writes=[R("rstd")], excl=[("ps", b2)])
                self.act(rstd[:, :], rstd[:, :], AF.Exp, scale=-0.5, reads=[R("rstd")], writes=[R("rstd")])
                self.tt("pool", t1[:, :], u[:, :], ropeb[rk][:, 0, :], ALU.mult, reads=[R("u"), ("rope", 0)], writes=[R("t1")])
                self.tt("dve", t2[:, :], self.ps[b3][:, :], ropeb[rk][:, 1, :], ALU.mult, reads=[("rope", 0)],
                        writes=[R("t2")], excl=[("ps", b3)])
                self.tt("dve", t1[:, :], t1[:, :], t2[:, :], ALU.add, reads=[R("t1"), R("t2")], writes=[R("t1")])
                self.tt("dve", dst_ap, t1[:, :], rstd[:, :], ALU.mult, reads=[R("t1"), R("rstd")], writes=[dst_res])
            return stage2

        def proj_chunk(wt, wres, col0, tc):
            b = self.bank()
            for kc in range(8):
                self.mm(b, self.ps[b][:, :], wt[:, kc, col0:col0 + 128], self.xT[:, kc, tc * 512:(tc + 1) * 512],
                        kc == 0, kc == 7, reads=[wres] + self.xT_res(tc))
            return b

        (wt,), wres = self.take_w()
        pend = None
        for g in range(2):
            for tc in range(4):
                b = proj_chunk(wt, wres, g * 128, tc)
                st2 = rope_stage1(tc, b, CH_GK[g], 65, kT2[:, g, tc * 512:(tc + 1) * 512], ("gk", g, tc))
                if pend is not None:
                    pend()
                pend = st2
        kv_pend = pend
        for j0 in range(0, 16, 4):
            b = self.bank()
            for t in range(4):
                for kc in range(8):
                    self.mm(b, self.ps[b][:, t * 128:(t + 1) * 128], self.xT[:, kc, (j0 + t) * 128:(j0 + t + 1) * 128],
                            wt[:, kc, 256:384], kc == 0, kc == 7, reads=[wres, ("xT", j0 + t)])
            pv = self.ps[b][:, :].rearrange("p (a n) -> p a n", n=128)
            for g in range(2):
                bsl = bv[:, g * 64:g * 64 + 64].unsqueeze(1).broadcast_to([128, 4, 64])
                self.tt("dve", Vg[:, j0:j0 + 4, g, 64:128], pv[:, :, g * 64:g * 64 + 64], bsl, ALU.add,
                        reads=[("bvg",)], writes=[("gv", j0 + t) for t in range(4)], excl=[("ps", b)])
        kv_pend()
        hcount = 0
        for c in range(4):
            g = c // 2
            (wt,), wres = self.take_w()
            pend = None
            for tc in range(4):
                b = proj_chunk(wt, wres, 0, tc)
                st2 = rope_stage1(tc, b, CH_GQ(c, 0), 64, qT[:, tc * 512:(tc + 1) * 512], ("gq", tc))
                if pend is not None:
                    pend()
                pend = st2
            q_pend = pend

            def ev_z(tc, b, ch=CH_GQ(c, 1)):
                self.act(sz[:, tc * 512:(tc + 1) * 512], self.ps[b][:, :], AF.Silu, bias=self.bias(ch),
                         reads=[("vecs",)], writes=[("gsz", tc)], excl=[("ps", b)])
            self.proj_fm(wt, wres, 128, ev_z)
            q_pend()
            units = []
            for q8 in range(8):
                q0 = q8 * 256
                tcq = q0 // 512
                obs = obanks[hcount % 2]
                slot = hcount % 2
                hcount += 1
                tl = {0: [], 1: []}
                for hi in range(2):
                    hb = 64 * hi
                    ob = obs[hi]

                    def epi(ob=ob, hb=hb, slot=slot, hi=hi, q0=q0, tcq=tcq, c=c):
                        self.attn_epilogue(ob, 256, hb, rs[slot][hi], sz[hb:hb + 64, q0:q0 + 256], ("gsz", tcq),
                                           self.yB[hb:hb + 64, c, q0:q0 + 256], ("yB", c, tcq), (slot, hi))
                    for j in range(16):
                        vl = Vg[:, j, g, 64:192] if hi == 0 else Vg[:, j, g, 0:128]
                        tl[hi].append(dict(
                            n=256, s_lhsT=kT2[hb:hb + 64, g, j * 128:(j + 1) * 128], s_rhs=qT[hb:hb + 64, q0:q0 + 256],
                            s_reads=[("gk", g, j // 4), ("gq", tcq)], etab=None, etab_res=None,
                            v_lhsT=vl, v_reads=[("gv", j)], ob=ob, first=(j == 0), last=(j == 15),
                            epi=epi if j == 15 else None))
                for i in range(0, 16, 2):
                    units.append([tl[0][i], tl[0][i + 1], tl[1][i], tl[1][i + 1]])
            self.attn_pipeline(units, Pb, None, regions, 8.0, look=2, s_order=[0, 2, 1, 3])
        if self.dbg:
            self.dbg_dump("yB%d" % l, self.yB, [("yB", c, t) for c in range(4) for t in range(4)])

    def phaseC(self, l):
        o = self.R_ph
        vn = self.A("vn", [128, 16, 512], BF16, o); o += 16384
        sgw = self.A("sgw", [128, 8, 128], BF16, o); o += 2048
        vraw = [self.A("vraw%d" % i, [128, 512], F32, o + 2048 * i) for i in range(2)]; o += 4096
        sgln = self.A("sgln", [128, 2, 512], F32, o); o += 4096
        bvs = self.A("bvs", [128, 512], F32, o); o += 2048
        uzb = [self.A("uz%d" % i, [128, 512], F32, o + 2048 * i) for i in range(2)]; o += 4096
        szt = [self.A("szt%d" % i, [128, 512], F32, o + 2048 * i) for i in range(2)]; o += 4096
        sgb = [self.A("sgb%d" % i, [128, 512], F32, o + 2048 * i) for i in range(2)]; o += 4096
        mix = [self.A("mix%d" % i, [128, 512], F32, o + 2048 * i) for i in range(2)]; o += 4096
        assert o <= SB_END, o
        self.pbanks = [0, 1, 2, 3, 4, 5, 6]
        d = self.d
        st = self.stat
        self.dma("sp", bvs[:, :], d["bvb"][l, :, 640:1152], writes=[("bvs",)])
        self.dma("sp", sgln[:, :, :], d["sgln"][l], writes=[("sgln",)])
        self.dma("pool", sgw[:, :, :], d["sgwT"][l], writes=[("sgw",)])
        (wt,), wres = self.take_w()
        for tt_i in range(16):
            b = self.bank()
            k = tt_i % 2
            for kc in range(8):
                self.mm(b, self.ps[b][:, :], self.xT[:, kc, tt_i * 128:(tt_i + 1) * 128], wt[:, kc, 0:512],
                        kc == 0, kc == 7, reads=[wres, ("xT", tt_i)])
            vr = vraw[k]
            self.tt("dve", vr[:, :], self.ps[b][:, :], bvs[:, :], ALU.add, reads=[("bvs",)], writes=[("vraw", k)],
                    excl=[("ps", b)])
            sk = 16 * k
            self.s.op("dve", (lambda vr, sk: lambda e: e.bn_stats(st[:, sk:sk + 6], vr[:, :]))(vr, sk),
                      reads=[("vraw", k)], writes=[("stat", k)])
            self.s.op("dve", (lambda sk: lambda e: e.bn_aggr(st[:, sk + 12:sk + 14], st[:, sk:sk + 6]))(sk),
                      reads=[("stat", k)], writes=[("stat", k)])
            self.act(st[:, sk + 14:sk + 15], st[:, sk + 13:sk + 14], AF.Sqrt, bias=1e-5,
                     reads=[("stat", k)], writes=[("stat", k)])
            self.s.op("dve", (lambda sk: lambda e: e.reciprocal(st[:, sk + 14:sk + 15], st[:, sk + 14:sk + 15]))(sk),
                      reads=[("stat", k)], writes=[("stat", k)])
            self.stt("dve", vr[:, :], vr[:, :], st[:, sk + 12:sk + 13], sgln[:, 0, :], ALU.subtract, ALU.mult,
                     reads=[("vraw", k), ("stat", k), ("sgln",)], writes=[("vraw", k)])
            self.act(vr[:, :], vr[:, :], AF.Copy, scale=st[:, sk + 14:sk + 15], reads=[("vraw", k), ("stat", k)],
                     writes=[("vraw", k)])
            self.tt("pool", vn[:, tt_i, :], vr[:, :], sgln[:, 1, :], ALU.add, reads=[("vraw", k), ("sgln",)],
                    writes=[("vn", tt_i)])
        cnt = 0
        for c in range(4):
            (wt,), wres = self.take_w()
            self.dma("sp", sgb[c % 2][:, :], d["sgb"][l, :, c, :], writes=[("sgb", c % 2)])
            for tc in range(4):
                bu = self.bank()
                for kc in range(8):
                    self.mm(bu, self.ps[bu][:, :], wt[:, kc, 0:128], self.xT[:, kc, tc * 512:(tc + 1) * 512],
                            kc == 0, kc == 7, reads=[wres] + self.xT_res(tc))
                bz = self.bank()
                for kc in range(8):
                    self.mm(bz, self.ps[bz][:, :], wt[:, kc, 128:256], self.xT[:, kc, tc * 512:(tc + 1) * 512],
                            kc == 0, kc == 7, reads=[wres] + self.xT_res(tc))
                k = cnt % 2
                cnt += 1
                self.act(szt[k][:, :], self.ps[bz][:, :], AF.Silu, bias=self.bias(CH_SG(c, 1)), reads=[("vecs",)],
                         writes=[("szt", k)], excl=[("ps", bz)])
                uzs = uzb[k][:, :]
                self.stt("dve", uzs, self.ps[bu][:, :], self.bias(CH_SG(c, 0)), szt[k][:, :], ALU.add, ALU.mult,
                         reads=[("vecs",), ("szt", k)], writes=[("uz", k)], excl=[("ps", bu)])
                bm = [self.bank(), self.bank()]
                for gi in range(2):
                    for t in range(4):
                        tt_i = 4 * tc + t
                        self.mm(bm[gi], self.ps[bm[gi]][:, t * 128:(t + 1) * 128], vn[:, tt_i, c * 128:(c + 1) * 128],
                                sgw[:, 2 * c + gi, :], True, True, reads=[("vn", tt_i), ("sgw",)])
                mk = mix[k]
                for gi in range(2):
                    hb = 64 * gi
                    self.tt("dve", mk[hb:hb + 64, :], self.ps[bm[gi]][hb:hb + 64, :], sgb[c % 2][hb:hb + 64, :], ALU.add,
                            reads=[("sgb", c % 2)], writes=[("mix", k)], excl=[("ps", bm[gi])])
                self.tt("pool", self.yC[:, c, tc * 512:(tc + 1) * 512], mk[:, :], uzs, ALU.mult,
                        reads=[("mix", k), ("uz", k)], writes=[("yC", c, tc)])
        if self.dbg:
            self.dbg_dump("yC%d" % l, self.yC, [("yC", c, t) for c in range(4) for t in range(4)])

    def phaseD(self, l, last):
        o = self.R_ph
        mT = self.A("mT", [128, 8, S], BF16, o); o += 32768
        gsig = [self.A("gsig%d" % i, [128, 512], F32, o + 2048 * i) for i in range(2)]; o += 4096
        mt = [self.A("mt%d" % i, [128, 512], F32, o + 2048 * i) for i in range(3)]; o += 6144
        assert o <= SB_END, o
        self.pbanks = [0, 1, 2, 3, 4, 5, 6]
        d = self.d
        gcnt = 0
        for dc in range(8):
            (wg, wb), wres = self.take_w()
            for tc in range(4):
                for br in range(3):
                    bg = self.bank()
                    for kc in range(8):
                        self.mm(bg, self.ps[bg][:, :], wg[:, kc, br * 128:(br + 1) * 128],
                                self.xT[:, kc, tc * 512:(tc + 1) * 512], kc == 0, kc == 7, reads=[wres] + self.xT_res(tc))
                    k = gcnt % 2
                    gcnt += 1
                    self.act(gsig[k][:, :], self.ps[bg][:, :], AF.Sigmoid, bias=self.bias(CH_GATE(dc, br)),
                             reads=[("vecs",)], writes=[("gsig", k)], excl=[("ps", bg)])
                    bp = self.bank()
                    yb = self.ybr[br]
                    nm = ("yA", "yB", "yC")[br]
                    for kc in range(4):
                        self.mm(bp, self.ps[bp][:, :], wb[:, kc, br * 128:(br + 1) * 128], yb[:, kc, tc * 512:(tc + 1) * 512],
                                kc == 0, kc == 3, reads=[wres, (nm, kc, tc)])
                    mb = mt[0] if br == 0 else mt[1]
                    mk = ("mt", 0 if br == 0 else 1)
                    self.tt("dve", mb[:, :], self.ps[bp][:, :], gsig[k][:, :], ALU.mult, reads=[("gsig", k)],
                            writes=[mk], excl=[("ps", bp)])
                    if br == 1:
                        self.tt("dve", mt[0][:, :], mt[0][:, :], mt[1][:, :], ALU.add, reads=[("mt", 0), ("mt", 1)],
                                writes=[("mt", 0)])
                    elif br == 2:
                        self.tt("dve", mT[:, dc, tc * 512:(tc + 1) * 512], mt[0][:, :], mt[1][:, :], ALU.add,
                                reads=[("mt", 0), ("mt", 1)], writes=[("mT", tc)])
        self.barrier()
        oa = self.R_y
        post = self.A("post", [128, 3, D], F32, oa); oa += 12288
        h16 = [self.A("dh16%d" % i, [128, D], BF16, oa + 2048 * i) for i in range(2)]; oa += 4096
        self.dma("sp", post[:, :, :], d["post"][l], writes=[("post",)])
        for hf in range(2):
            (wo,), wres = self.take_w()
            for tt_i in range(16):
                tcq = tt_i // 4
                b = self.bank()
                for dc in range(8):
                    self.mm(b, self.ps[b][:, :], mT[:, dc, tt_i * 128:(tt_i + 1) * 128], wo[:, dc, 0:512],
                            dc == 0, dc == 7, reads=[("mT", tcq), wres])
                x_ap = self.xtok[:, tt_i, hf * 512:(hf + 1) * 512]
                self.stt("dve", x_ap, x_ap, float(ALPHA), self.ps[b][:, :], ALU.mult, ALU.add,
                         reads=[("xtok", tt_i)], writes=[("xtok", tt_i)], excl=[("ps", b)])
                if hf == 1:
                    self.tt("pool", self.xtok[:, tt_i, :], self.xtok[:, tt_i, :], post[:, 0, :], ALU.add,
                            reads=[("xtok", tt_i), ("post",)], writes=[("xtok", tt_i)])
        self.ln_stats(self.xtok[:, 0, :], ("xtok", 0), 0)
        for t in range(16):
            if t + 1 < 16:
                self.ln_stats(self.xtok[:, t + 1, :], ("xtok", t + 1), t + 1)
            self.ln_apply(self.xtok[:, t, :], ("xtok", t), t, post[:, 1, :], post[:, 2, :], ("post",))
            if last:
                self.dma("sp", self.y_d[t * 128:(t + 1) * 128, :], self.xtok[:, t, :], reads=[("xtok", t)])
            else:
                self.to_xT_finish()
                self.to_xT(t, h16)
        self.to_xT_finish()
        self.barrier()

    def layer(self, l, last):
        self.dma("sp", self.vecs[:, :], self.d["vecs"][l], writes=[("vecs",)])
        self.phaseA(l)
        self.barrier()
        self.phaseB(l)
        self.barrier()
        self.phaseC(l)
        self.barrier()
        self.phaseD(l, last)

    def build(self):
        from contextlib import ExitStack
        nc = self.nc
        self.phase0()
        self.barrier()
        for i, l in enumerate(self.layers):
            self.layer(l, last=(i == len(self.layers) - 1))
        self.s.wait_all("sp")
        with ExitStack() as es:
            sems = {e: es.enter_context(nc.semaphore("s_" + e)) for e in Sched.ENGS}
            dsems = [es.enter_context(nc.semaphore("q%d" % i)) for i in range(self.s.n_dma)]
            block = es.enter_context(nc.Block())
            self.s.emit(nc, block, sems, dsems)
        return nc


def _run(layers, do_ln_in, x_list, shared, dbg=False):
    kb = KB(layers, do_ln_in, dbg)
    nc = kb.build()
    in_maps = []
    for xb in x_list:
        m = dict(shared)
        m["x"] = np.ascontiguousarray(xb, dtype=np.float32)
        in_maps.append(m)
    return run_bass_kernel_spmd(nc, in_maps, core_ids=list(range(len(x_list))))


def kernel(**inputs):
    x = np.asarray(inputs["x"], dtype=np.float32)
    shared = prep_inputs(**inputs)
    res = _run([0, 1], True, [x[b] for b in range(x.shape[0])], shared)
    return np.stack([r["y"] for r in res.results], axis=0).astype(np.float32)
```
